# Optimizing a Trainium2 kernel written in Bass

```python
import jax, jax.numpy as jnp
from jax import lax
import numpy as np

D_MODEL = 1024
BATCH = 32
SEQ = 2048
DEPTH = 4
DEC_BATCH = 32
DEC_SEQ = 64
PAST_LEN = 4096

CHUNK = 64
N_META = 16
N_MIXERS = 4
EXPAND = 2
E_WIDTH = EXPAND * D_MODEL
EPS = 1e-6
A_HEADS = 16
A_DK = E_WIDTH // A_HEADS
A_DV = E_WIDTH // A_HEADS
A_BLOCK = 16
B_WIDTH = E_WIDTH
B_BLOCKS = 8
B_BS = B_WIDTH // B_BLOCKS
B_CONV = 4
B_C = 8.0
C_HEADS = 16
C_NOPE = 128
C_ROPE = 64
C_V = E_WIDTH // C_HEADS
C_Q_LORA = 512
C_KV_LORA = 256
C_SCALE = (C_NOPE + C_ROPE) ** -0.5
ROPE_BASE = 10000.0
Q_BLOCK = 128
PAD_CHUNK = 2 ** 30
D_WIDTH = E_WIDTH
D_CONV = 31
NL_A = (DEPTH + 3) // 4
NL_B = (DEPTH + 2) // 4
NL_C = (DEPTH + 1) // 4
NL_D = DEPTH // 4

kernel_name = "hybrid_streaming_encoder_step"

F32 = jnp.float32


def rmsnorm(x, g):
    xf = x.astype(F32)
    y = xf * lax.rsqrt(jnp.mean(xf * xf, axis=-1, keepdims=True) + EPS)
    return (y * g.astype(F32)).astype(x.dtype)


def layernorm(x, g, b):
    xf = x.astype(F32)
    mu = jnp.mean(xf, axis=-1, keepdims=True)
    xc = xf - mu
    var = jnp.mean(xc * xc, axis=-1, keepdims=True)
    return (xc * lax.rsqrt(var + EPS) * g.astype(F32) + b.astype(F32)).astype(x.dtype)


def causal_dwconv(x_ext, w, b):
    c = x_ext.shape[-1]
    out = lax.conv_general_dilated(x_ext, w[:, None, :].astype(x_ext.dtype), window_strides=(1,),
                                   padding="VALID", dimension_numbers=("NWC", "WIO", "NWC"),
                                   feature_group_count=c)
    return out + b.astype(x_ext.dtype)


def rope_tables(pos):
    inv = 1.0 / (ROPE_BASE ** (jnp.arange(0, C_ROPE, 2, dtype=F32) / C_ROPE))
    ang = pos.astype(F32)[:, None] * inv[None, :]
    return jnp.cos(ang), jnp.sin(ang)


def apply_rope(x, cos, sin):
    xf = x.astype(F32)
    x1, x2 = jnp.split(xf, 2, axis=-1)
    return jnp.concatenate([x1 * cos - x2 * sin, x1 * sin + x2 * cos], axis=-1).astype(x.dtype)


def hgrn2_recurrence(q, k, lf, v, s0):
    bsz, t = q.shape[:2]
    nb = -(-t // A_BLOCK)
    pad = nb * A_BLOCK - t

    def blocks(z):
        z = jnp.pad(z, ((0, 0), (0, pad), (0, 0), (0, 0)))
        z = z.reshape(bsz, nb, A_BLOCK, A_HEADS, z.shape[-1])
        return jnp.transpose(z, (1, 0, 3, 2, 4))

    tri = jnp.tril(jnp.ones((A_BLOCK, A_BLOCK), dtype=bool))

    def step(s, inp):
        qb, kb, lfb, vb = inp
        cum = jnp.cumsum(lfb, axis=2)
        qt = qb * jnp.exp(cum)
        kt = kb * jnp.exp(-cum)
        att = jnp.where(tri, jnp.einsum("bhld,bhmd->bhlm", qt, kt), 0.0)
        o = jnp.einsum("bhld,bhdv->bhlv", qt, s) + jnp.einsum("bhlm,bhmv->bhlv", att, vb)
        last = cum[:, :, -1:, :]
        s_new = jnp.exp(last[:, :, 0, :])[..., None] * s + jnp.einsum("bhmd,bhmv->bhdv", kb * jnp.exp(last - cum), vb)
        return s_new, o

    s_fin, o = lax.scan(step, s0, (blocks(q), blocks(k), blocks(lf), blocks(v)))
    o = jnp.transpose(o, (1, 0, 3, 2, 4)).reshape(bsz, nb * A_BLOCK, A_HEADS, A_DV)[:, :t]
    return o, s_fin


def hgrn2_mixer(u, w_in, lb, norm_g, w_out, s0):
    bsz, t, _ = u.shape
    q, f, i, g = jnp.split(u @ w_in, 4, axis=-1)
    lb = lb.astype(F32)
    fg = lb + (1.0 - lb) * jax.nn.sigmoid(f.astype(F32))
    heads = lambda z: z.reshape(bsz, t, A_HEADS, -1)
    o, s_fin = hgrn2_recurrence(heads(jax.nn.silu(q.astype(F32))), heads(1.0 - fg), heads(jnp.log(fg)),
                                heads(i.astype(F32)), s0.astype(F32))
    o = rmsnorm(o, norm_g).reshape(bsz, t, E_WIDTH).astype(u.dtype)
    y = (o * jax.nn.silu(g)) @ w_out
    return y, s_fin.astype(u.dtype)


def rglru_mixer(u, w_in, conv_w, conv_b, wa, ba, wx, bx, lam, w_out, h0, buf, reset_first):
    bsz, t, _ = u.shape
    xb, gb = jnp.split(u @ w_in, 2, axis=-1)
    ext = jnp.concatenate([buf.astype(xb.dtype), xb], axis=1)
    new_buf = ext[:, -(B_CONV - 1):]
    xc = causal_dwconv(ext, conv_w, conv_b)
    xblk = xc.reshape(bsz, t, B_BLOCKS, B_BS)
    r = jax.nn.sigmoid((jnp.einsum("btnc,ncd->btnd", xblk, wa).reshape(bsz, t, B_WIDTH) + ba).astype(F32))
    ig = jax.nn.sigmoid((jnp.einsum("btnc,ncd->btnd", xblk, wx).reshape(bsz, t, B_WIDTH) + bx).astype(F32))
    log_a = -B_C * r * jax.nn.softplus(-lam.astype(F32))
    a = jnp.exp(log_a)
    mult = jnp.sqrt(-jnp.expm1(2.0 * log_a))
    if reset_first:
        mult = mult.at[:, 0].set(1.0)
    bterm = mult * ig * xc.astype(F32)
    bterm = bterm.at[:, 0].add(a[:, 0] * h0.astype(F32))

    def combine(left, right):
        al, bl = left
        ar, br = right
        return al * ar, ar * bl + br

    _, h = lax.associative_scan(combine, (a, bterm), axis=1)
    y = (h.astype(u.dtype) * jax.nn.silu(gb)) @ w_out
    return y, h[:, -1].astype(u.dtype), new_buf


def mla_mixer(u, w_in, q_norm, kv_norm, w_uq, w_uk, w_uv, w_out, cache_c, cache_pe, pos, q_chunk, k_chunk_cache):
    bsz, t, _ = u.shape
    i1 = C_Q_LORA
    i2 = i1 + C_KV_LORA
    i3 = i2 + C_ROPE
    q_lat, kv_lat, k_pe, g = jnp.split(u @ w_in, [i1, i2, i3], axis=-1)
    q = (rmsnorm(q_lat, q_norm) @ w_uq).reshape(bsz, t, C_HEADS, C_NOPE + C_ROPE)
    cos, sin = rope_tables(pos)
    q_nope = q[..., :C_NOPE]
    q_pe = apply_rope(q[..., C_NOPE:], cos[:, None, :], sin[:, None, :])
    c_kv = rmsnorm(kv_lat, kv_norm)
    k_pe = apply_rope(k_pe, cos, sin)
    keys_c = jnp.concatenate([cache_c.astype(c_kv.dtype), c_kv], axis=1)
    keys_pe = jnp.concatenate([cache_pe.astype(k_pe.dtype), k_pe], axis=1)
    k_chunk = jnp.concatenate([k_chunk_cache, q_chunk])
    w_uk_h = w_uk.reshape(C_KV_LORA, C_HEADS, C_NOPE)
    w_uv_h = w_uv.reshape(C_KV_LORA, C_HEADS, C_V)
    nq = -(-t // Q_BLOCK)
    pad = nq * Q_BLOCK - t

    def blocks(z):
        z = jnp.pad(z, ((0, 0), (0, pad), (0, 0), (0, 0)))
        return jnp.moveaxis(z.reshape(bsz, nq, Q_BLOCK, C_HEADS, z.shape[-1]), 1, 0)

    qc = jnp.pad(q_chunk, (0, pad), constant_values=PAD_CHUNK).reshape(nq, Q_BLOCK)

    def attend(args):
        qn, qp, qcb = args
        q_abs = jnp.einsum("bqhn,lhn->bqhl", qn, w_uk_h)
        s = (jnp.einsum("bqhl,bkl->bhqk", q_abs, keys_c, preferred_element_type=F32)
             + jnp.einsum("bqhr,bkr->bhqk", qp, keys_pe, preferred_element_type=F32)) * C_SCALE
        mask = k_chunk[None, :] <= qcb[:, None]
        p = jax.nn.softmax(jnp.where(mask, s, -jnp.inf), axis=-1).astype(keys_c.dtype)
        o_lat = jnp.einsum("bhqk,bkl->bqhl", p, keys_c)
        return jnp.einsum("bqhl,lhv->bqhv", o_lat, w_uv_h)

    o = lax.map(attend, (blocks(q_nope), blocks(q_pe), qc))
    o = jnp.moveaxis(o, 0, 1).reshape(bsz, nq * Q_BLOCK, C_HEADS * C_V)[:, :t]
    y = (o * jax.nn.silu(g)) @ w_out
    return y, c_kv, k_pe


def conformer_mixer(u, w_in, conv_w, conv_b, ln_g, ln_b, w_out, buf):
    a, b, g = jnp.split(u @ w_in, 3, axis=-1)
    v = a * jax.nn.sigmoid(b)
    ext = jnp.concatenate([buf.astype(v.dtype), v], axis=1)
    new_buf = ext[:, -(D_CONV - 1):]
    c = causal_dwconv(ext, conv_w, conv_b)
    c = jax.nn.silu(layernorm(c, ln_g, ln_b))
    y = (c * jax.nn.silu(g)) @ w_out
    return y, new_buf


def trunk(x, pos, q_chunk, k_chunk_cache, reset_first, st, p, lb):
    new = {"hgrn": [], "rg_h": [], "rg_conv": [], "mla_c": [], "mla_pe": [], "conf": []}
    for layer in range(DEPTH):
        kind, j = layer % N_MIXERS, layer // N_MIXERS
        u = rmsnorm(x, p["norm_pre"][layer])
        if kind == 0:
            y, s = hgrn2_mixer(u, p["a_w_in"][j], lb[j], p["a_norm_g"][j], p["a_w_out"][j], st["hgrn"][j])
            new["hgrn"].append(s)
        elif kind == 1:
            y, h, buf = rglru_mixer(u, p["b_w_in"][j], p["b_conv_w"][j], p["b_conv_b"][j], p["b_wa"][j],
                                    p["b_ba"][j], p["b_wx"][j], p["b_bx"][j], p["b_lambda"][j], p["b_w_out"][j],
                                    st["rg_h"][j], st["rg_conv"][j], reset_first)
            new["rg_h"].append(h)
            new["rg_conv"].append(buf)
        elif kind == 2:
            y, c, pe = mla_mixer(u, p["c_w_in"][j], p["c_q_norm"][j], p["c_kv_norm"][j], p["c_w_uq"][j],
                                 p["c_w_uk"][j], p["c_w_uv"][j], p["c_w_out"][j], st["mla_c"][j], st["mla_pe"][j],
                                 pos, q_chunk, k_chunk_cache)
            new["mla_c"].append(c)
            new["mla_pe"].append(pe)
        else:
            y, buf = conformer_mixer(u, p["d_w_in"][j], p["d_conv_w"][j], p["d_conv_b"][j], p["d_ln_g"][j],
                                     p["d_ln_b"][j], p["d_w_out"][j], st["conf"][j])
            new["conf"].append(buf)
        x = x + rmsnorm(y, p["norm_post"][layer])
    return x, new


def setup_inputs(seed: int = 0) -> dict:
    key = jax.random.key(seed)
    ks = iter(list(jax.random.split(key, 48)))

    def nrm(shape, scale):
        return jax.random.normal(next(ks), shape, F32) * scale

    def gain(shape):
        return 1.0 + nrm(shape, 0.05)

    u_lam = jax.random.uniform(next(ks), (NL_B, B_WIDTH), F32, minval=0.9, maxval=0.999)
    s_lam = u_lam ** (1.0 / B_C)
    b_lambda = jnp.log(s_lam) - jnp.log1p(-s_lam)
    return {
        "x_prompt": nrm((BATCH, SEQ, D_MODEL), 1.0),
        "x_sample": nrm((DEC_BATCH, DEC_SEQ, D_MODEL), 1.0),
        "state_hgrn": nrm((NL_A, DEC_BATCH, A_HEADS, A_DK, A_DV), 0.3),
        "state_rglru_h": nrm((NL_B, DEC_BATCH, B_WIDTH), 0.5),
        "state_rglru_conv": nrm((NL_B, DEC_BATCH, B_CONV - 1, B_WIDTH), 1.0),
        "cache_mla_latent": nrm((NL_C, DEC_BATCH, N_META + PAST_LEN, C_KV_LORA), 1.0),
        "cache_mla_rope": nrm((NL_C, DEC_BATCH, N_META + PAST_LEN, C_ROPE), 1.0),
        "state_conformer_conv": nrm((NL_D, DEC_BATCH, D_CONV - 1, D_WIDTH), 0.5),
        "meta_tokens": nrm((N_META, D_MODEL), 1.0),
        "norm_pre": gain((DEPTH, D_MODEL)),
        "norm_post": gain((DEPTH, D_MODEL)),
        "a_w_in": nrm((NL_A, D_MODEL, 4 * E_WIDTH), D_MODEL ** -0.5),
        "a_lb_logits": nrm((NL_A + 1, A_HEADS * A_DK), 0.5),
        "a_norm_g": gain((NL_A, A_DV)),
        "a_w_out": nrm((NL_A, E_WIDTH, D_MODEL), E_WIDTH ** -0.5),
        "b_w_in": nrm((NL_B, D_MODEL, 2 * B_WIDTH), D_MODEL ** -0.5),
        "b_conv_w": nrm((NL_B, B_CONV, B_WIDTH), B_CONV ** -0.5),
        "b_conv_b": nrm((NL_B, B_WIDTH), 0.01),
        "b_wa": nrm((NL_B, B_BLOCKS, B_BS, B_BS), B_BS ** -0.5),
        "b_ba": nrm((NL_B, B_WIDTH), 0.01),
        "b_wx": nrm((NL_B, B_BLOCKS, B_BS, B_BS), B_BS ** -0.5),
        "b_bx": nrm((NL_B, B_WIDTH), 0.01),
        "b_lambda": b_lambda,
        "b_w_out": nrm((NL_B, B_WIDTH, D_MODEL), B_WIDTH ** -0.5),
        "c_w_in": nrm((NL_C, D_MODEL, C_Q_LORA + C_KV_LORA + C_ROPE + C_HEADS * C_V), D_MODEL ** -0.5),
        "c_q_norm": gain((NL_C, C_Q_LORA)),
        "c_kv_norm": gain((NL_C, C_KV_LORA)),
        "c_w_uq": nrm((NL_C, C_Q_LORA, C_HEADS * (C_NOPE + C_ROPE)), C_Q_LORA ** -0.5),
        "c_w_uk": nrm((NL_C, C_KV_LORA, C_HEADS * C_NOPE), C_KV_LORA ** -0.5),
        "c_w_uv": nrm((NL_C, C_KV_LORA, C_HEADS * C_V), C_KV_LORA ** -0.5),
        "c_w_out": nrm((NL_C, C_HEADS * C_V, D_MODEL), (C_HEADS * C_V) ** -0.5),
        "d_w_in": nrm((NL_D, D_MODEL, 3 * D_WIDTH), D_MODEL ** -0.5),
        "d_conv_w": nrm((NL_D, D_CONV, D_WIDTH), D_CONV ** -0.5),
        "d_conv_b": nrm((NL_D, D_WIDTH), 0.01),
        "d_ln_g": gain((NL_D, D_WIDTH)),
        "d_ln_b": nrm((NL_D, D_WIDTH), 0.01),
        "d_w_out": nrm((NL_D, D_WIDTH, D_MODEL), D_WIDTH ** -0.5),
    }


def reference(x_prompt, x_sample, state_hgrn, state_rglru_h, state_rglru_conv, cache_mla_latent, cache_mla_rope,
              state_conformer_conv, meta_tokens, norm_pre, norm_post, a_w_in, a_lb_logits, a_norm_g, a_w_out,
              b_w_in, b_conv_w, b_conv_b, b_wa, b_ba, b_wx, b_bx, b_lambda, b_w_out,
              c_w_in, c_q_norm, c_kv_norm, c_w_uq, c_w_uk, c_w_uv, c_w_out,
              d_w_in, d_conv_w, d_conv_b, d_ln_g, d_ln_b, d_w_out):
    p = {"norm_pre": norm_pre, "norm_post": norm_post,
         "a_w_in": a_w_in, "a_norm_g": a_norm_g, "a_w_out": a_w_out,
         "b_w_in": b_w_in, "b_conv_w": b_conv_w, "b_conv_b": b_conv_b, "b_wa": b_wa, "b_ba": b_ba,
         "b_wx": b_wx, "b_bx": b_bx, "b_lambda": b_lambda, "b_w_out": b_w_out,
         "c_w_in": c_w_in, "c_q_norm": c_q_norm, "c_kv_norm": c_kv_norm, "c_w_uq": c_w_uq,
         "c_w_uk": c_w_uk, "c_w_uv": c_w_uv, "c_w_out": c_w_out,
         "d_w_in": d_w_in, "d_conv_w": d_conv_w, "d_conv_b": d_conv_b, "d_ln_g": d_ln_g, "d_ln_b": d_ln_b,
         "d_w_out": d_w_out}
    lb = jnp.cumsum(jax.nn.softmax(a_lb_logits.astype(F32), axis=0), axis=0)

    bp, dt = x_prompt.shape[0], x_prompt.dtype
    t_p = N_META + x_prompt.shape[1]
    xp = jnp.concatenate([jnp.broadcast_to(meta_tokens.astype(dt)[None], (bp, N_META, D_MODEL)), x_prompt], axis=1)
    pos_p = jnp.arange(t_p, dtype=jnp.int32)
    qc_p = jnp.concatenate([jnp.full((N_META,), -1, jnp.int32),
                            jnp.arange(x_prompt.shape[1], dtype=jnp.int32) // CHUNK])
    st_p = {"hgrn": [jnp.zeros((bp, A_HEADS, A_DK, A_DV), dt) for _ in range(NL_A)],
            "rg_h": [jnp.zeros((bp, B_WIDTH), dt) for _ in range(NL_B)],
            "rg_conv": [jnp.zeros((bp, B_CONV - 1, B_WIDTH), dt) for _ in range(NL_B)],
            "mla_c": [jnp.zeros((bp, 0, C_KV_LORA), dt) for _ in range(NL_C)],
            "mla_pe": [jnp.zeros((bp, 0, C_ROPE), dt) for _ in range(NL_C)],
            "conf": [jnp.zeros((bp, D_CONV - 1, D_WIDTH), dt) for _ in range(NL_D)]}
    yp, newp = trunk(xp, pos_p, qc_p, jnp.zeros((0,), jnp.int32), True, st_p, p, lb)

    past = cache_mla_latent.shape[2] - N_META
    t_s = x_sample.shape[1]
    pos_s = N_META + past + jnp.arange(t_s, dtype=jnp.int32)
    qc_s = jnp.full((t_s,), past // CHUNK, jnp.int32)
    kcc_s = jnp.concatenate([jnp.full((N_META,), -1, jnp.int32), jnp.arange(past, dtype=jnp.int32) // CHUNK])
    st_s = {"hgrn": [state_hgrn[j] for j in range(NL_A)],
            "rg_h": [state_rglru_h[j] for j in range(NL_B)],
            "rg_conv": [state_rglru_conv[j] for j in range(NL_B)],
            "mla_c": [cache_mla_latent[j] for j in range(NL_C)],
            "mla_pe": [cache_mla_rope[j] for j in range(NL_C)],
            "conf": [state_conformer_conv[j] for j in range(NL_D)]}
    ys, news = trunk(x_sample, pos_s, qc_s, kcc_s, False, st_s, p, lb)

    return (yp[:, N_META:], ys,
            jnp.stack(newp["hgrn"]), jnp.stack(news["hgrn"]),
            jnp.stack(newp["rg_h"]), jnp.stack(news["rg_h"]),
            jnp.stack(newp["rg_conv"]), jnp.stack(news["rg_conv"]),
            jnp.stack(newp["mla_c"]), jnp.stack(news["mla_c"]),
            jnp.stack(newp["mla_pe"]), jnp.stack(news["mla_pe"]),
            jnp.stack(newp["conf"]), jnp.stack(news["conf"]))
```

```python
import contextlib
import numpy as np
import concourse.bass as bass
import concourse.mybir as mybir
from concourse.bass_utils import run_bass_kernel_spmd

F32 = mybir.dt.float32
BF16 = mybir.dt.bfloat16
ALU = mybir.AluOpType
AF = mybir.ActivationFunctionType

EPS = 1e-6
C_SCALE = 192.0 ** -0.5
ENGS = ("pe", "act", "dve", "pool", "sp")


class Buf:
    __slots__ = ("name", "last_w", "readers", "dsem", "dcount", "ps", "last_by")

    def __init__(self, name, ps=False):
        self.name = name
        self.last_w = None
        self.readers = []
        self.dsem = None
        self.dcount = 0
        self.ps = ps
        self.last_by = {}


class Op:
    __slots__ = ("eng", "fn", "waits", "signal", "sigval", "dma_buf", "dma_val")

    def __init__(self, eng, fn):
        self.eng = eng
        self.fn = fn
        self.waits = []
        self.signal = False
        self.sigval = None
        self.dma_buf = None
        self.dma_val = None


class Sched:
    def __init__(self, nc):
        self.nc = nc
        self.ops = {e: [] for e in ENGS}
        self.nops = 0

    def _dep(self, op, tok):
        if tok is None:
            return
        if tok.dma_buf is not None:
            op.waits.append(("dma", tok.dma_buf, tok.dma_val))
            return
        if tok.eng == op.eng and op.eng == "pe":
            return
        tok.signal = True
        op.waits.append(tok)

    def add(self, eng, fn, reads=(), writes=(), dma_buf=None):
        op = Op(eng, fn)
        for b in reads:
            if b.ps:
                continue
            self._dep(op, b.last_w)
        for b in writes:
            if b.ps:
                continue
            self._dep(op, b.last_w)
            for r in b.readers:
                if r.dma_buf is None and r.eng == eng:
                    continue
                self._dep(op, r)
        seen = set()
        for b in list(reads) + list(writes):
            if not b.ps or id(b) in seen:
                continue
            seen.add(id(b))
            for e2, o2 in b.last_by.items():
                if e2 != eng:
                    self._dep(op, o2)
            b.last_by[eng] = op
        if dma_buf is not None:
            dma_buf.dcount += 16
            op.dma_buf = dma_buf
            op.dma_val = dma_buf.dcount
        for b in reads:
            if not b.ps:
                b.readers.append(op)
        for b in writes:
            if not b.ps:
                b.last_w = op
                b.readers = []
        self.ops[eng].append(op)
        self.nops += 1
        return op

    def emit(self):
        nc = self.nc
        with contextlib.ExitStack() as stack:
            esem = {e: stack.enter_context(nc.semaphore("s_" + e)) for e in ENGS if e != "sp"}
            for e in ENGS:
                n = 0
                for op in self.ops[e]:
                    if op.dma_buf is None and op.signal:
                        n += 1
                        op.sigval = n
            dbufs, seen = [], set()
            for e in ENGS:
                for op in self.ops[e]:
                    if op.dma_buf is not None and id(op.dma_buf) not in seen:
                        seen.add(id(op.dma_buf))
                        dbufs.append(op.dma_buf)
            for i, b in enumerate(dbufs):
                b.dsem = stack.enter_context(nc.semaphore("d%d_%s" % (i, b.name)))
            self.n_sems = len(dbufs) + 4
            block = stack.enter_context(nc.Block())
            handles = {"pe": "tensor", "act": "scalar", "dve": "vector", "pool": "gpsimd", "sp": "sync"}

            def make(e):
                def body(eng):
                    known = {}
                    for op in self.ops[e]:
                        need = {}
                        for w in op.waits:
                            if isinstance(w, Op):
                                s, v = esem[w.eng], w.sigval
                            else:
                                s, v = w[1].dsem, w[2]
                            k = id(s)
                            if known.get(k, 0) >= v:
                                continue
                            if k not in need or need[k][1] < v:
                                need[k] = (s, v)
                        for k, (s, v) in need.items():
                            eng.wait_ge(s, v)
                            known[k] = v
                        ins = op.fn(eng)
                        if op.dma_buf is not None:
                            ins.then_inc(op.dma_buf.dsem, 16)
                        elif op.signal:
                            ins.then_inc(esem[e], 1)
                    if e == "sp":
                        for b in dbufs:
                            if b.dcount:
                                eng.wait_ge(b.dsem, b.dcount)
                return body

            for e in ENGS:
                getattr(block, handles[e])(make(e))


PAR_FIELDS = [("gpre", 32), ("gpost", 32), ("lb0", 16), ("lb1", 16), ("ang", 1), ("bcw", 64), ("bcb", 16),
              ("bba", 16), ("bbx", 16), ("blam", 16), ("cqn", 4), ("ckvn", 2), ("dcw", 496), ("dcb", 16),
              ("dlg", 16), ("dlb", 16)]
PAR_OFF = {}
_o = 0
for _n, _w in PAR_FIELDS:
    PAR_OFF[_n] = _o
    _o += _w
NPAR = _o

CON_FIELDS = [("ident", 128), ("swap", 64), ("tri", 512), ("cmask", 2048), ("scanm", 512)]
CON_OFF = {}
_o = 0
for _n, _w in CON_FIELDS:
    CON_OFF[_n] = _o
    _o += _w
NCON = _o


def _pc(v):
    v = np.asarray(v, np.float32)
    lead = v.shape[:-1]
    c = v.shape[-1] // 128
    v = v.reshape(lead + (c, 128))
    return np.moveaxis(v, -1, 0)


def pack_params(inp):
    P = np.zeros((128, NPAR), np.float32)

    def put(name, arr):
        arr = np.ascontiguousarray(arr, np.float32).reshape(128, -1)
        P[:, PAR_OFF[name]:PAR_OFF[name] + arr.shape[1]] = arr

    put("gpre", _pc(inp["norm_pre"]))
    put("gpost", _pc(inp["norm_post"]))
    put("lb0", _pc(inp["a_lb_logits"][0]))
    put("lb1", _pc(inp["a_lb_logits"][1]))
    put("ang", np.asarray(inp["a_norm_g"][0]).reshape(128, 1))
    put("bcw", np.transpose(_pc(inp["b_conv_w"][0]), (0, 2, 1)))
    put("bcb", _pc(inp["b_conv_b"][0]))
    put("bba", _pc(inp["b_ba"][0]))
    put("bbx", _pc(inp["b_bx"][0]))
    put("blam", _pc(inp["b_lambda"][0]))
    put("cqn", _pc(inp["c_q_norm"][0]))
    put("ckvn", _pc(inp["c_kv_norm"][0]))
    put("dcw", np.transpose(_pc(inp["d_conv_w"][0]), (0, 2, 1)))
    put("dcb", _pc(inp["d_conv_b"][0]))
    put("dlg", _pc(inp["d_ln_g"][0]))
    put("dlb", _pc(inp["d_ln_b"][0]))
    return P


def pack_consts():
    C = np.zeros((128, NCON), np.float32)
    C[:, CON_OFF["ident"]:CON_OFF["ident"] + 128] = np.eye(128, dtype=np.float32)
    sw = np.zeros((64, 64), np.float32)
    for i in range(64):
        sw[(i + 32) % 64, i] = 1.0
    C[0:64, CON_OFF["swap"]:CON_OFF["swap"] + 64] = sw
    m = np.arange(32)[:, None]
    l = np.arange(32)[None, :]
    tri = (m <= l).astype(np.float32)
    C[0:32, CON_OFF["tri"]:CON_OFF["tri"] + 512] = np.tile(tri, (1, 16))
    k = np.arange(128)[:, None]
    q = np.arange(512)[None, :]
    for r in range(4):
        mk = ((2 * r + k // 64) <= (q // 64)).astype(np.float32)
        C[:, CON_OFF["cmask"] + r * 512:CON_OFF["cmask"] + (r + 1) * 512] = mk
    sm = np.ones((128, 512), np.float32)
    sm[:, 0::32] = 0.0
    C[:, CON_OFF["scanm"]:CON_OFF["scanm"] + 512] = sm
    return C


def rope_table(pos):
    pos = np.asarray(pos, np.float32)
    inv = (1.0 / (np.float32(10000.0) ** (np.arange(0, 64, 2, dtype=np.float32) / np.float32(64)))).astype(np.float32)
    ang = (pos[:, None] * inv[None, :]).astype(np.float32)
    cos = np.cos(ang).astype(np.float32).T
    sin = np.sin(ang).astype(np.float32).T
    out = np.zeros((64, 2, pos.shape[0]), np.float32)
    out[0:32, 0] = cos
    out[32:64, 0] = cos
    out[0:32, 1] = -sin
    out[32:64, 1] = sin
    return out


class ATile:
    def __init__(self, K, u0, k):
        self.K, self.u0, self.k = K, u0, k
        self.bufs = K.abufs[u0:u0 + k]

    def f(self):
        return self.K.A[:, self.u0 * 512:(self.u0 + self.k) * 512]

    def b(self):
        return self.f().bitcast(BF16)


class DT:
    def __init__(self, t, buf):
        self.t, self.buf = t, buf


class TileDesc:
    def __init__(self, kind, n, segs, seqs, seq=None, tidx=0):
        self.kind, self.n, self.segs, self.seqs, self.seq, self.tidx = kind, n, segs, seqs, seq, tidx


class Builder:
    def __init__(self, cfg):
        self.cfg = cfg
        self.NPS, self.NT, self.NS, self.PAST = cfg["NPS"], cfg["NT"], cfg["NS"], cfg["PAST"]
        self.LS = 64
        self.layers = cfg.get("layers", [0, 1, 2, 3])
        self.NKP = 16 + 512 * self.NT
        self.NKC = 16 + self.PAST
        self.NKS = self.NKC + self.LS
        self.NKMAX = max(self.NKP, self.NKS)
        self.NU = cfg.get("NU", 44)
        self.NW = 4
        self.nc = bass.Bass("TRN2", target_bir_lowering=False)
        self.S = Sched(self.nc)
        self.wi = 0

    def dram_in(self, name, shape):
        return self.nc.dram_tensor(name, list(shape), F32, kind="ExternalInput").ap()

    def dram_out(self, name, shape):
        return self.nc.dram_tensor(name, list(shape), F32, kind="ExternalOutput").ap()

    def sb(self, name, shape, dt=F32):
        t = self.st.enter_context(self.nc.sbuf_tensor("sb_" + name, list(shape), dt))
        return DT(t, Buf(name))

    def alloc(self, k):
        free = self.afree
        for u0 in range(0, self.NU - k + 1):
            if all(free[u0:u0 + k]):
                for i in range(u0, u0 + k):
                    free[i] = False
                return ATile(self, u0, k)
        raise RuntimeError("arena full (need %d units, free %d)" % (k, sum(free)))

    def free(self, *tiles):
        for t in tiles:
            for i in range(t.u0, t.u0 + t.k):
                assert not self.afree[i]
                self.afree[i] = True

    def mm(self, out, lhsT, rhs, R, W, start=True, stop=True):
        self.S.add("pe", lambda e: e.matmul(out, lhsT=lhsT, rhs=rhs, start=start, stop=stop), R, W)

    def tr(self, out, in_, R, W):
        ident = self.identbf.t[:, :]
        self.S.add("pe", lambda e: e.transpose(out, in_, ident), list(R) + [self.identbf.buf], W)

    def act(self, out, in_, func, R, W, scale=1.0, bias=0.0):
        self.S.add("act", lambda e: e.activation(out=out, in_=in_, func=func, bias=bias, scale=scale), R, W)

    def tt(self, eng, out, a, b, op, R, W):
        self.S.add(eng, lambda e: e.tensor_tensor(out=out, in0=a, in1=b, op=op), R, W)

    def ts(self, eng, out, a, s1, s2, op0, op1, R, W):
        if s2 is None:
            self.S.add(eng, lambda e: e.tensor_scalar(out=out, in0=a, scalar1=s1, scalar2=None, op0=op0), R, W)
        else:
            self.S.add(eng, lambda e: e.tensor_scalar(out=out, in0=a, scalar1=s1, scalar2=s2, op0=op0, op1=op1), R, W)

    def stt(self, out, a, s, b, op0, op1, R, W):
        self.S.add("dve", lambda e: e.scalar_tensor_tensor(out=out, in0=a, scalar=s, in1=b, op0=op0, op1=op1), R, W)

    def cp(self, eng, out, in_, R, W):
        if eng == "act":
            self.S.add("act", lambda e: e.activation(out=out, in_=in_, func=AF.Copy), R, W)
        else:
            self.S.add(eng, lambda e: e.tensor_copy(out=out, in_=in_), R, W)

    def scan(self, out, d0, d1, init, R, W):
        self.S.add("dve", lambda e: e.tensor_tensor_scan(out=out, data0=d0, data1=d1, initial=init,
                                                        op0=ALU.mult, op1=ALU.add), R, W)

    def memset(self, eng, ap, val, W):
        self.S.add(eng, lambda e: e.memset(ap, val), [], W)

    def dma(self, q, out, in_, R, W, dbuf):
        self.S.add(q, lambda e: e.dma_start(out=out, in_=in_), R, W, dma_buf=dbuf)

    def par(self, name, i=0, w=1):
        o = PAR_OFF[name] + i
        return self.params.t[:, o:o + w]

    def wl(self, name, kc, c0, ncol, k0=0):
        view = self.W[name].rearrange("(c p) m -> p c m", p=128)[:, k0:k0 + kc, c0:c0 + ncol]
        slot = self.wslots[self.wi % self.NW]
        self.wi += 1
        ap = slot.t[:, 0:kc * ncol].rearrange("p (a b) -> p a b", a=kc)
        self.dma("sp", ap, view, [self.Wbuf[name]], [slot.buf], slot.buf)
        return ap, slot.buf

    def bank(self, i):
        return self.PS[i], self.pbufs[i]

    def rstd_from(self, ps_ap, psbuf, n, inv_d, rows=128):
        r = self.alloc(1)
        ra = r.f()[0:rows, 0:n]
        self.act(ra, ps_ap, AF.Ln, [psbuf, self.epsb.buf], r.bufs, scale=inv_d, bias=self.epsb.t[0:rows, 0:1])
        self.act(ra, ra, AF.Exp, r.bufs, r.bufs, scale=-0.5)
        return r

    def prenorm(self, l, n):
        X = self.XT
        sq = self.alloc(4)
        for c in range(8):
            self.act(sq.b()[:, c * 512:c * 512 + n], X.t[:, c * 512:c * 512 + n], AF.Square, [X.buf], sq.bufs)
        ps, pb = self.bank(self.nb_next())
        for c in range(8):
            self.mm(ps[:, 0:n], self.onesbf.t[:, :], sq.b()[:, c * 512:c * 512 + n], sq.bufs + [self.onesbf.buf], [pb],
                    start=(c == 0), stop=(c == 7))
        r = self.rstd_from(ps[:, 0:n], pb, n, 1.0 / 1024)
        u = sq
        for c in range(8):
            self.stt(u.b()[:, c * 512:c * 512 + n], X.t[:, c * 512:c * 512 + n], self.par("gpre", l * 8 + c),
                     r.f()[:, 0:n], ALU.mult, ALU.mult, [X.buf, self.params.buf] + r.bufs, u.bufs)
        self.free(r)
        return u

    def nb_next(self):
        self._mb = (self._mb + 1) % 2
        return self._mb

    def outproj(self, l, hs, Wd, n):
        X = self.XT
        y = self.alloc(8)
        ysq = self.alloc(4)
        for g in range(4):
            w, wb = self.wl(Wd, 16, g * 256, 256)
            for mm_ in range(2):
                m = g * 2 + mm_
                ps, pb = self.bank(self.nb_next())
                for k in range(16):
                    self.mm(ps[:, 0:n], w[:, k, mm_ * 128:(mm_ + 1) * 128], hs.b()[:, k * 512:k * 512 + n],
                            hs.bufs + [wb], [pb], start=(k == 0), stop=(k == 15))
                ya = y.f()[:, m * 512:m * 512 + n]
                self.act(ya, ps[:, 0:n], AF.Copy, [pb], [y.bufs[m]])
                self.tt("pool", ysq.b()[:, m * 512:m * 512 + n], ya, ya, ALU.mult, [y.bufs[m]], ysq.bufs)
        ps, pb = self.bank(self.nb_next())
        for m in range(8):
            self.mm(ps[:, 0:n], self.onesbf.t[:, :], ysq.b()[:, m * 512:m * 512 + n], ysq.bufs + [self.onesbf.buf],
                    [pb], start=(m == 0), stop=(m == 7))
        r = self.rstd_from(ps[:, 0:n], pb, n, 1.0 / 1024)
        for m in range(8):
            ya = y.f()[:, m * 512:m * 512 + n]
            self.stt(ya, ya, self.par("gpost", l * 8 + m), r.f()[:, 0:n], ALU.mult, ALU.mult,
                     [y.bufs[m], self.params.buf] + r.bufs, [y.bufs[m]])
            xa = X.t[:, m * 512:m * 512 + n]
            self.tt("pool", xa, xa, ya, ALU.add, [X.buf, y.bufs[m]], [X.buf])
        self.free(r, y, ysq)

    def layer0(self, T):
        n = T.n
        bl = min(32, n)
        nb = n // bl
        Wd = "a_w_in"
        u = self.prenorm(0, n)
        hs = self.alloc(8)
        pbuf = self.params.buf
        for half in range(2):
            QT, KT, KP, VV = (self.alloc(4) for _ in range(4))
            EL = self.alloc(1)
            SG = self.alloc(4)
            for pr in range(4):
                h0 = half * 8 + pr * 2
                hhs = [pr * 2, pr * 2 + 1]
                wq, wqb = self.wl(Wd, 8, h0 * 128, 256)
                wg, wgb = self.wl(Wd, 8, 6144 + h0 * 128, 256)
                wf, wfb = self.wl(Wd, 8, 2048 + h0 * 128, 256)
                wi, wib = self.wl(Wd, 8, 4096 + h0 * 128, 256)
                t1 = [self.alloc(1), self.alloc(1)]
                kk = [self.alloc(1), self.alloc(1)]
                fg = [self.alloc(1), self.alloc(1)]
                cum = [self.alloc(1), self.alloc(1)]
                for hp in range(2):
                    ps, pb = self.proj8(wq, wqb, hp * 128, u, n)
                    self.act(t1[hp].f()[:, 0:n], ps[:, 0:n], AF.Silu, [pb], t1[hp].bufs)
                for hp in range(2):
                    ps, pb = self.proj8(wg, wgb, hp * 128, u, n)
                    self.act(SG.b()[:, hhs[hp] * 512:hhs[hp] * 512 + n], ps[:, 0:n], AF.Silu, [pb], SG.bufs)
                for hp in range(2):
                    ps, pb = self.proj8(wf, wfb, hp * 128, u, n)
                    self.act(kk[hp].f()[:, 0:n], ps[:, 0:n], AF.Sigmoid, [pb], kk[hp].bufs, scale=-1.0)
                for hp in range(2):
                    ps, pb = self.proj8(wi, wib, hp * 128, u, n)
                    self.act(VV.b()[:, hhs[hp] * 512:hhs[hp] * 512 + n], ps[:, 0:n], AF.Copy, [pb], VV.bufs)
                for hp in range(2):
                    h = h0 + hp
                    self.ts("dve", fg[hp].f()[:, 0:n], kk[hp].f()[:, 0:n], self.noml.t[:, h:h + 1], 1.0, ALU.mult, ALU.add,
                            kk[hp].bufs + [self.noml.buf], fg[hp].bufs)
                for hp in range(2):
                    self.act(fg[hp].f()[:, 0:n], fg[hp].f()[:, 0:n], AF.Ln, fg[hp].bufs, fg[hp].bufs)
                for hp in range(2):
                    self.scan(cum[hp].f()[:, 0:n], self.scanm.t[:, 0:n], fg[hp].f()[:, 0:n], 0.0,
                              fg[hp].bufs + [self.scanm.buf], cum[hp].bufs)
                for hp in range(2):
                    ee = fg[hp]
                    self.act(ee.f()[:, 0:n], cum[hp].f()[:, 0:n], AF.Exp, cum[hp].bufs, ee.bufs)
                    self.act(cum[hp].f()[:, 0:n], cum[hp].f()[:, 0:n], AF.Exp, cum[hp].bufs, cum[hp].bufs, scale=-1.0)
                for hp in range(2):
                    h = h0 + hp
                    hh = hhs[hp]
                    cs = slice(hh * 512, hh * 512 + n)
                    ee = fg[hp]
                    self.tt("dve", QT.b()[:, cs], t1[hp].f()[:, 0:n], ee.f()[:, 0:n], ALU.mult, t1[hp].bufs + ee.bufs, QT.bufs)
                    self.stt(kk[hp].f()[:, 0:n], kk[hp].f()[:, 0:n], self.oml.t[:, h:h + 1], cum[hp].f()[:, 0:n], ALU.mult, ALU.mult,
                             kk[hp].bufs + cum[hp].bufs + [self.oml.buf], kk[hp].bufs)
                    self.cp("pool", KT.b()[:, cs], kk[hp].f()[:, 0:n], kk[hp].bufs, KT.bufs)
                    elast = ee.f()[:, bl - 1:n:bl]
                    self.tt("dve", KP.b()[:, cs].rearrange("p (b t) -> p b t", t=bl),
                            kk[hp].f()[:, 0:n].rearrange("p (b t) -> p b t", t=bl),
                            elast.unsqueeze(2).broadcast_to([128, nb, bl]), ALU.mult, kk[hp].bufs + ee.bufs, KP.bufs)
                    self.cp("pool", EL.f()[:, hh * 32:hh * 32 + nb], elast, ee.bufs, EL.bufs)
                self.free(*t1, *kk, *fg, *cum)
            for (c0, L, sq) in T.segs:
                nbk = L // bl
                if T.kind == "sample":
                    self.load_S(sq, half)
                att = self.alloc(1)
                for qd in range(2):
                    ps, pb = self.bank(qd)
                    for hq in range(4):
                        hh = qd * 4 + hq
                        for b in range(nbk):
                            cc = hh * 512 + c0 + b * bl
                            oa = (hq * nbk + b) * 32
                            self.mm(ps[0:bl, oa:oa + bl], KT.b()[:, cc:cc + bl], QT.b()[:, cc:cc + bl],
                                    KT.bufs + QT.bufs, [pb])
                    w = 4 * nbk * 32
                    self.tt("dve", att.b()[0:bl, qd * 512:qd * 512 + w], ps[0:bl, 0:w], self.tri.t[0:bl, 0:w], ALU.mult,
                            [pb, self.tri.buf], att.bufs)
                vks = [[self.alloc(1), self.alloc(1)], [self.alloc(1), self.alloc(1)]]
                ops_ = [self.bank(6), self.bank(7)]

                def t_stage(b, qd):
                    tp, tpb = self.bank(2 + qd)
                    tpv = tp[:, :].bitcast(BF16)
                    for hq in range(4):
                        hh = qd * 4 + hq
                        cc = hh * 512 + c0 + b * bl
                        self.tr(tpv[0:bl, (hq * 2) * 128:(hq * 2 + 1) * 128], VV.b()[:, cc:cc + bl], VV.bufs, [tpb])
                        self.tr(tpv[0:bl, (hq * 2 + 1) * 128:(hq * 2 + 2) * 128], KP.b()[:, cc:cc + bl], KP.bufs, [tpb])
                    vk = vks[qd][b % 2]
                    self.cp("act", vk.b()[0:bl, 0:1024], tpv[0:bl, 0:1024], [tpb], vk.bufs)

                def c_stage(b, qd):
                    gq = half * 2 + qd
                    vk = vks[qd][b % 2]
                    op_, opb = ops_[qd]
                    Sb = self.Sbf[gq]
                    for hq in range(4):
                        hh = qd * 4 + hq
                        cc = hh * 512 + c0 + b * bl
                        oc = (hq * nbk + b) * bl
                        oa = (hq * nbk + b) * 32
                        self.mm(op_[:, oc:oc + bl], Sb.t[:, hq * 128:(hq + 1) * 128], QT.b()[:, cc:cc + bl],
                                [Sb.buf] + QT.bufs, [opb], start=True, stop=False)
                        self.mm(op_[:, oc:oc + bl], vk.b()[0:bl, (hq * 2) * 128:(hq * 2 + 1) * 128],
                                att.b()[0:bl, qd * 512 + oa:qd * 512 + oa + bl], vk.bufs + att.bufs, [opb],
                                start=False, stop=True)
                    sp_, spb = self.bank(4 + qd)
                    for hq in range(4):
                        self.mm(sp_[:, hq * 128:(hq + 1) * 128], vk.b()[0:bl, (hq * 2 + 1) * 128:(hq * 2 + 2) * 128],
                                vk.b()[0:bl, (hq * 2) * 128:(hq * 2 + 1) * 128], vk.bufs, [spb])
                    S3 = self.S32[gq]
                    blk = (c0 // bl) + b
                    dec = EL.f()[:, qd * 128:(qd + 1) * 128].rearrange("p (h b) -> p h b", h=4)[:, :, blk:blk + 1]
                    s3v = S3.t[:, :].rearrange("p (h v) -> p h v", h=4)
                    self.tt("dve", s3v, s3v, dec.broadcast_to([128, 4, 128]), ALU.mult, [S3.buf] + EL.bufs, [S3.buf])
                    self.tt("dve", S3.t[:, :], S3.t[:, :], sp_[:, :], ALU.add, [S3.buf, spb], [S3.buf])
                    self.cp("pool", Sb.t[:, :], S3.t[:, :], [S3.buf], [Sb.buf])

                t_stage(0, 0)
                t_stage(0, 1)
                for b in range(nbk):
                    if b + 1 < nbk:
                        t_stage(b + 1, 0)
                        t_stage(b + 1, 1)
                    c_stage(b, 0)
                    c_stage(b, 1)
                for qd in range(2):
                    op_, opb = ops_[qd]
                    w = 4 * L
                    osq = self.alloc(1)
                    self.act(osq.b()[:, 0:w], op_[:, 0:w], AF.Square, [opb], osq.bufs)
                    ss, ssb = self.bank(qd)
                    self.mm(ss[:, 0:w], self.onesbf.t[:, :], osq.b()[:, 0:w], osq.bufs + [self.onesbf.buf], [ssb])
                    r = self.rstd_from(ss[:, 0:w], ssb, w, 1.0 / 128)
                    self.stt(r.f()[:, 0:w], op_[:, 0:w], self.par("ang"), r.f()[:, 0:w],
                             ALU.mult, ALU.mult, [opb] + r.bufs + [pbuf], r.bufs)
                    hq0 = half * 8 + qd * 4
                    hsv = hs.b()[:, hq0 * 512:(hq0 + 4) * 512].rearrange("p (h t) -> p h t", h=4)[:, :, c0:c0 + L]
                    sgv = SG.b()[:, qd * 2048:(qd + 1) * 2048].rearrange("p (h t) -> p h t", h=4)[:, :, c0:c0 + L]
                    self.tt("dve", hsv, r.f()[:, 0:w].rearrange("p (h t) -> p h t", h=4), sgv, ALU.mult, r.bufs + SG.bufs, hs.bufs)
                    self.free(osq, r)
                if T.kind == "sample":
                    self.store_S(self.NPS + sq, half)
                self.free(vks[0][0], vks[0][1], vks[1][0], vks[1][1], att)
            self.free(QT, KT, KP, VV, EL, SG)
        self.free(u)
        self.outproj(0, hs, "a_w_out", n)
        self.free(hs)

    def load_S(self, s, half):
        for qd in range(2):
            gq = half * 2 + qd
            S3 = self.S32[gq]
            self.dma("pool", S3.t[:, :].rearrange("p (h v) -> p h v", h=4), self.I["st_hgrn"][s, :, gq * 4:(gq + 1) * 4, :],
                     [], [S3.buf], S3.buf)
            self.cp("act", self.Sbf[gq].t[:, :], S3.t[:, :], [S3.buf], [self.Sbf[gq].buf])

    def store_S(self, oidx, half):
        for qd in range(2):
            gq = half * 2 + qd
            S3 = self.S32[gq]
            self.dma("pool", self.O["o_hgrn"][oidx, :, gq * 4:(gq + 1) * 4, :], S3.t[:, :].rearrange("p (h v) -> p h v", h=4),
                     [S3.buf], [], S3.buf)


    def proj8(self, w, wb, c0, u, n, M=128):
        ps, pb = self.bank(self.nb_next())
        for k in range(8):
            self.mm(ps[0:M, 0:n], w[:, k, c0:c0 + M], u.b()[:, k * 512:k * 512 + n], u.bufs + [wb], [pb],
                    start=(k == 0), stop=(k == 7))
        return ps, pb

    def layer1(self, T):
        n = T.n
        Wd = "b_w_in"
        pbuf = self.params.buf
        u = self.prenorm(1, n)
        hs = self.alloc(8)
        nseg = len(T.seqs)
        L = T.seqs[0][1]
        sample = T.kind == "sample"
        HW = 3 + L
        def stage_a(g):
            wx_, wxb = self.wl(Wd, 8, g * 256, 256)
            xbufs, xvs = [], []
            for m in range(2):
                xbuf = self.alloc(2)
                xv = xbuf.f()[:, 0:nseg * HW].rearrange("p (s t) -> p s t", s=nseg)
                ps, pb = self.proj8(wx_, wxb, m * 128, u, n)
                self.act(xv[:, :, 3:3 + L], ps[:, 0:n].rearrange("p (s t) -> p s t", s=nseg), AF.Copy, [pb], xbuf.bufs)
                xbufs.append(xbuf)
                xvs.append(xv)
            xcs, xcb = [], []
            for m in range(2):
                j = g * 2 + m
                xbuf, xv = xbufs[m], xvs[m]
                if sample:
                    hal = self.hal1S.t[:, :].rearrange("p (s j k) -> p s j k", s=nseg, j=16)[:, :, j, :]
                    halb = self.hal1S.buf
                else:
                    hal = self.hal1.t[:, j * 3:(j + 1) * 3].unsqueeze(1)
                    halb = self.hal1.buf
                self.cp("pool", xv[:, :, 0:3], hal, [halb], xbuf.bufs)
                xc = self.alloc(1)
                xcv = xc.f()[:, 0:n].rearrange("p (s t) -> p s t", s=nseg)
                self.ts("dve", xcv, xv[:, :, 3:3 + L], self.par("bcw", j * 4 + 3), self.par("bcb", j), ALU.mult, ALU.add,
                        xbuf.bufs + [pbuf], xc.bufs)
                for k in range(3):
                    self.stt(xcv, xv[:, :, k:k + L], self.par("bcw", j * 4 + k), xcv, ALU.mult, ALU.add,
                             xbuf.bufs + xc.bufs + [pbuf], xc.bufs)
                self.cp("pool", hal, xv[:, :, L:L + 3], xbuf.bufs, [halb])
                xb16 = self.alloc(1)
                self.cp("pool", xb16.b()[:, 0:n], xc.f()[:, 0:n], xc.bufs, xb16.bufs)
                xcs.append(xc)
                xcb.append(xb16)
            self.free(*xbufs)
            return xcs, xcb

        def stage_b(g, st):
            xcs, xcb = st
            J = [2 * g, 2 * g + 1]
            wa, wab = self.wl("b_wa", 2, 0, 256, k0=g * 2)
            wxg, wxgb = self.wl("b_wx", 2, 0, 256, k0=g * 2)
            rs, igs, as_ = [], [], []
            for m in range(2):
                j = J[m]
                r, ig = self.alloc(1), self.alloc(1)
                for (wm, wmb, dst, bias) in ((wa, wab, r, "bba"), (wxg, wxgb, ig, "bbx")):
                    ps, pb = self.bank(self.nb_next())
                    for jj in range(2):
                        self.mm(ps[:, 0:n], wm[:, jj, m * 128:(m + 1) * 128], xcb[jj].b()[:, 0:n],
                                xcb[jj].bufs + [wmb], [pb], start=(jj == 0), stop=(jj == 1))
                    self.act(dst.f()[:, 0:n], ps[:, 0:n], AF.Sigmoid, [pb, pbuf], dst.bufs, bias=self.par(bias, j))
                rs.append(r)
                igs.append(ig)
            for m in range(2):
                j, r = J[m], rs[m]
                a = self.alloc(1)
                self.act(a.f()[:, 0:n], r.f()[:, 0:n], AF.Exp, r.bufs + [self.cneg.buf], a.bufs, scale=self.cneg.t[:, j:j + 1])
                self.act(r.f()[:, 0:n], r.f()[:, 0:n], AF.Exp, r.bufs + [self.cneg.buf], r.bufs, scale=self.cneg.t[:, 16 + j:17 + j])
                self.act(r.f()[:, 0:n], r.f()[:, 0:n], AF.Ln, r.bufs + [self.oneb.buf], r.bufs, scale=-1.0, bias=self.oneb.t[:, 0:1])
                self.act(r.f()[:, 0:n], r.f()[:, 0:n], AF.Exp, r.bufs, r.bufs, scale=0.5)
                if T.kind == "meta":
                    self.memset("dve", r.f()[:, 0:1], 1.0, r.bufs)
                as_.append(a)
            for m in range(2):
                j, r, ig, a = J[m], rs[m], igs[m], as_[m]
                bt = self.alloc(1)
                self.tt("dve", bt.f()[:, 0:n], r.f()[:, 0:n], ig.f()[:, 0:n], ALU.mult, r.bufs + ig.bufs, bt.bufs)
                self.tt("dve", bt.f()[:, 0:n], bt.f()[:, 0:n], xcs[m].f()[:, 0:n], ALU.mult, bt.bufs + xcs[m].bufs, bt.bufs)
                hh = ig
                for si, (c0, Ls, sq) in enumerate(T.seqs):
                    if sample:
                        st_ = self.hstS.t[:, sq * 16 + j:sq * 16 + j + 1]
                        stb = self.hstS.buf
                    else:
                        st_ = self.hst.t[:, j:j + 1]
                        stb = self.hst.buf
                    self.scan(hh.f()[:, c0:c0 + Ls], a.f()[:, c0:c0 + Ls], bt.f()[:, c0:c0 + Ls], st_,
                              a.bufs + bt.bufs + [stb], hh.bufs)
                    self.cp("pool", st_, hh.f()[:, c0 + Ls - 1:c0 + Ls], hh.bufs, [stb])
                self.free(bt, r, xcs[m], xcb[m])
            wg_, wgb = self.wl(Wd, 8, 2048 + g * 256, 256)
            for m in range(2):
                j, hh, a = J[m], igs[m], as_[m]
                ps, pb = self.proj8(wg_, wgb, m * 128, u, n)
                self.act(a.f()[:, 0:n], ps[:, 0:n], AF.Silu, [pb], a.bufs)
                self.tt("dve", hs.b()[:, j * 512:j * 512 + n], hh.f()[:, 0:n], a.f()[:, 0:n], ALU.mult, hh.bufs + a.bufs, hs.bufs)
                self.free(hh, a)

        st_prev = stage_a(0)
        for g in range(8):
            st_next = stage_a(g + 1) if g + 1 < 8 else None
            stage_b(g, st_prev)
            st_prev = st_next
        self.free(u)
        self.outproj(1, hs, "b_w_out", n)
        self.free(hs)

    def layer3(self, T):
        n = T.n
        Wd = "d_w_in"
        pbuf = self.params.buf
        u = self.prenorm(3, n)
        nseg = len(T.seqs)
        L = T.seqs[0][1]
        sample = T.kind == "sample"
        HW = 30 + L
        c32 = self.alloc(16)
        sum_ps, sumb = self.bank(2)
        sq_ps, sqb = self.bank(3)
        wts = {}

        def stage_a(j):
            g, m = j // 2, j % 2
            if m == 0:
                wts[g] = (self.wl(Wd, 8, g * 256, 256), self.wl(Wd, 8, 2048 + g * 256, 256))
            (wa, wab), (wb_, wbb) = wts[g]
            sg = self.alloc(1)
            ps, pb = self.proj8(wb_, wbb, m * 128, u, n)
            self.act(sg.f()[:, 0:n], ps[:, 0:n], AF.Sigmoid, [pb], sg.bufs)
            vbuf = self.alloc(2)
            vv = vbuf.f()[:, 0:nseg * HW].rearrange("p (s t) -> p s t", s=nseg)
            ps, pb = self.proj8(wa, wab, m * 128, u, n)
            self.tt("dve", vv[:, :, 30:30 + L], ps[:, 0:n].rearrange("p (s t) -> p s t", s=nseg),
                    sg.f()[:, 0:n].rearrange("p (s t) -> p s t", s=nseg), ALU.mult, [pb] + sg.bufs, vbuf.bufs)
            if sample:
                hal = self.hal3S.t[:, :].rearrange("p (s j k) -> p s j k", s=nseg, j=16)[:, :, j, :]
                halb = self.hal3S.buf
            else:
                hal = self.hal3.t[:, j * 30:(j + 1) * 30].unsqueeze(1)
                halb = self.hal3.buf
            self.cp("pool", vv[:, :, 0:30], hal, [halb], vbuf.bufs)
            vbf = self.alloc(1)
            self.cp("pool", vbf.b()[:, 0:nseg * HW], vbuf.f()[:, 0:nseg * HW], vbuf.bufs, vbf.bufs)
            self.cp("pool", hal, vv[:, :, L:L + 30], vbuf.bufs, [halb])
            D = self.alloc(4)
            dv = D.b()[:, 0:31 * 128].rearrange("p (k c) -> p k c", k=31)
            self.tt("dve", dv, self.identbf.t[:, :].unsqueeze(1).broadcast_to([128, 31, 128]),
                    self.par("dcw", j * 31, 31).unsqueeze(2).broadcast_to([128, 31, 128]), ALU.mult,
                    [self.identbf.buf, pbuf], D.bufs)
            self.free(sg, vbuf)
            return (vbf, D, dv)

        def stage_b(j, st):
            vbf, D, dv = st
            cps, cpb = self.bank(4 + (j % 2))
            vb3 = vbf.b()[:, 0:nseg * HW].rearrange("p (s t) -> p s t", s=nseg)
            for si in range(nseg):
                for k in range(31):
                    self.mm(cps[:, si * L:(si + 1) * L], dv[:, k, :], vb3[:, si, k:k + L], D.bufs + vbf.bufs, [cpb],
                            start=(k == 0), stop=(k == 30))
            cj = c32.f()[:, j * 512:j * 512 + n]
            self.ts("dve", cj, cps[:, 0:n], self.par("dcb", j), None, ALU.add, None, [cpb, pbuf], [c32.bufs[j]])
            cb = self.alloc(1)
            self.cp("pool", cb.b()[:, 0:n], cj, [c32.bufs[j]], cb.bufs)
            self.tt("pool", cb.b()[:, 512:512 + n], cj, cj, ALU.mult, [c32.bufs[j]], cb.bufs)
            self.mm(sum_ps[:, 0:n], self.onesbf.t[:, :], cb.b()[:, 0:n], cb.bufs + [self.onesbf.buf], [sumb],
                    start=(j == 0), stop=(j == 15))
            self.mm(sq_ps[:, 0:n], self.onesbf.t[:, :], cb.b()[:, 512:512 + n], cb.bufs + [self.onesbf.buf], [sqb],
                    start=(j == 0), stop=(j == 15))
            self.free(vbf, D, cb)

        st_prev = stage_a(0)
        for j in range(16):
            st_next = stage_a(j + 1) if j + 1 < 16 else None
            stage_b(j, st_prev)
            st_prev = st_next
        mean, rstd, nmr = (self.alloc(1) for _ in range(3))
        self.act(mean.f()[:, 0:n], sum_ps[:, 0:n], AF.Copy, [sumb], mean.bufs, scale=1.0 / 2048)
        self.tt("dve", nmr.f()[:, 0:n], mean.f()[:, 0:n], mean.f()[:, 0:n], ALU.mult, mean.bufs, nmr.bufs)
        self.stt(rstd.f()[:, 0:n], sq_ps[:, 0:n], 1.0 / 2048, nmr.f()[:, 0:n], ALU.mult, ALU.subtract, [sqb] + nmr.bufs, rstd.bufs)
        self.act(rstd.f()[:, 0:n], rstd.f()[:, 0:n], AF.Ln, rstd.bufs + [self.epsb.buf], rstd.bufs, bias=self.epsb.t[:, 0:1])
        self.act(rstd.f()[:, 0:n], rstd.f()[:, 0:n], AF.Exp, rstd.bufs, rstd.bufs, scale=-0.5)
        self.stt(nmr.f()[:, 0:n], mean.f()[:, 0:n], -1.0, rstd.f()[:, 0:n], ALU.mult, ALU.mult, mean.bufs + rstd.bufs, nmr.bufs)
        hs = self.alloc(8)
        for g in range(8):
            wg, wgb = self.wl(Wd, 8, 4096 + g * 256, 256)
            for m in range(2):
                j = g * 2 + m
                cj = c32.f()[:, j * 512:j * 512 + n]
                t = self.alloc(1)
                sg = self.alloc(1)
                self.tt("dve", t.f()[:, 0:n], cj, rstd.f()[:, 0:n], ALU.mult, [c32.bufs[j]] + rstd.bufs, t.bufs)
                self.tt("dve", t.f()[:, 0:n], t.f()[:, 0:n], nmr.f()[:, 0:n], ALU.add, t.bufs + nmr.bufs, t.bufs)
                self.act(t.f()[:, 0:n], t.f()[:, 0:n], AF.Silu, t.bufs + [pbuf], t.bufs, scale=self.par("dlg", j), bias=self.par("dlb", j))
                ps, pb = self.proj8(wg, wgb, m * 128, u, n)
                self.act(sg.f()[:, 0:n], ps[:, 0:n], AF.Silu, [pb], sg.bufs)
                self.tt("dve", hs.b()[:, j * 512:j * 512 + n], t.f()[:, 0:n], sg.f()[:, 0:n], ALU.mult, t.bufs + sg.bufs, hs.bufs)
                self.free(t, sg)
        self.free(mean, rstd, nmr, c32, u)
        self.outproj(3, hs, "d_w_out", n)
        self.free(hs)


    def layer2(self, T):
        n = T.n
        Wd = "c_w_in"
        pbuf = self.params.buf
        NKM = self.NKMAX
        KC = self.KC
        sample = T.kind == "sample"
        u = self.prenorm(2, n)
        RP = self.ROPE
        if T.kind == "meta":
            src = self.I["ropeP"][:, :, 0:16]
        elif T.kind == "frame":
            src = self.I["ropeP"][:, :, 16 + 512 * T.tidx:16 + 512 * (T.tidx + 1)]
        else:
            src = self.I["ropeS"][:, :, :]
        rpv = RP.t[:, :].rearrange("p (a t) -> p a t", a=2)
        self.dma("pool", rpv[:, :, 0:n], src, [], [RP.buf], RP.buf)
        rq = self.alloc(2)
        rqv = rq.f()[0:64, :].rearrange("p (a t) -> p a t", a=2)
        self.ts("dve", rqv[:, :, 0:n], rpv[:, :, 0:n], C_SCALE, None, ALU.mult, None, [RP.buf], rq.bufs)

        def rms_chunks(w, wb, nch, gname, inv_d, dst_f32, dst_bufs, dst_bf):
            raw = self.alloc(nch)
            sq = self.alloc((nch + 1) // 2)
            for c in range(nch):
                ps, pb = self.proj8(w, wb, c * 128, u, n)
                ra = raw.f()[:, c * 512:c * 512 + n]
                self.act(ra, ps[:, 0:n], AF.Copy, [pb], [raw.bufs[c]])
                self.tt("pool", sq.b()[:, c * 512:c * 512 + n], ra, ra, ALU.mult, [raw.bufs[c]], sq.bufs)
            ps, pb = self.bank(self.nb_next())
            for c in range(nch):
                self.mm(ps[:, 0:n], self.onesbf.t[:, :], sq.b()[:, c * 512:c * 512 + n], sq.bufs + [self.onesbf.buf], [pb],
                        start=(c == 0), stop=(c == nch - 1))
            r = self.rstd_from(ps[:, 0:n], pb, n, inv_d)
            for c in range(nch):
                ra = raw.f()[:, c * 512:c * 512 + n]
                if dst_f32 is not None:
                    self.stt(dst_f32(c), ra, self.par(gname, c), r.f()[:, 0:n], ALU.mult, ALU.mult,
                             [raw.bufs[c], pbuf] + r.bufs, dst_bufs)
                    self.cp("pool", dst_bf(c), dst_f32(c), dst_bufs, dst_bf.bufs)
                else:
                    self.stt(dst_bf(c), ra, self.par(gname, c), r.f()[:, 0:n], ALU.mult, ALU.mult,
                             [raw.bufs[c], pbuf] + r.bufs, dst_bf.bufs)
            self.free(raw, sq, r)

        qn = self.alloc(2)
        w, wb = self.wl(Wd, 8, 0, 512)
        dq = lambda c: qn.b()[:, c * 512:c * 512 + n]
        dq.bufs = qn.bufs
        rms_chunks(w, wb, 4, "cqn", 1.0 / 512, None, None, dq)
        CKV = self.CKV
        ckb = self.alloc(1)
        w, wb = self.wl(Wd, 8, 512, 256)
        df = lambda c: CKV.t[:, c * 512:c * 512 + n]
        db = lambda c: ckb.b()[:, c * 512:c * 512 + n]
        db.bufs = ckb.bufs
        rms_chunks(w, wb, 2, "ckvn", 1.0 / 256, df, [CKV.buf], db)
        KPE = self.KPE
        w, wb = self.wl(Wd, 8, 768, 64)
        ps, pb = self.proj8(w, wb, 0, u, n, M=64)
        kp = self.alloc(1)
        kpb = self.alloc(1)
        self.act(kp.f()[0:64, 0:n], ps[0:64, 0:n], AF.Copy, [pb], kp.bufs)
        self.cp("pool", kpb.b()[0:64, 0:n], kp.f()[0:64, 0:n], kp.bufs, kpb.bufs)
        ps2, pb2 = self.bank(self.nb_next())
        self.mm(ps2[0:64, 0:n], self.swapbf.t[:, :], kpb.b()[0:64, 0:n], kpb.bufs + [self.swapbf.buf], [pb2])
        self.tt("dve", kp.f()[0:64, 0:n], kp.f()[0:64, 0:n], rpv[:, 0, 0:n], ALU.mult, kp.bufs + [RP.buf], kp.bufs)
        self.tt("dve", KPE.t[:, 0:n], ps2[0:64, 0:n], rpv[:, 1, 0:n], ALU.mult, [pb2, RP.buf], [KPE.buf])
        self.tt("dve", KPE.t[:, 0:n], KPE.t[:, 0:n], kp.f()[0:64, 0:n], ALU.add, [KPE.buf] + kp.bufs, [KPE.buf])
        self.cp("pool", kpb.b()[0:64, 0:n], KPE.t[:, 0:n], [KPE.buf], kpb.bufs)
        self.free(kp)
        ckv3 = CKV.t[:, :].rearrange("p (c t) -> p c t", c=2)
        if T.kind == "meta":
            for s in range(self.NPS):
                self.dma("pool", self.O["o_lat_p"][s, :, :, 0:16], ckv3[:, :, 0:16], [CKV.buf], [], CKV.buf)
                self.dma("pool", self.O["o_rope_p"][s, :, 0:16], KPE.t[:, 0:16], [KPE.buf], [], KPE.buf)
            k0 = 0
        elif T.kind == "frame":
            k0 = 16 + 512 * T.tidx
            self.dma("pool", self.O["o_lat_p"][T.seq, :, :, k0:k0 + 512], ckv3, [CKV.buf], [], CKV.buf)
            self.dma("pool", self.O["o_rope_p"][T.seq, :, k0:k0 + 512], KPE.t[:, :], [KPE.buf], [], KPE.buf)
        else:
            self.dma("pool", self.O["o_lat_s"][:, :, :], ckv3[:, :, 0:n], [CKV.buf], [], CKV.buf)
            self.dma("pool", self.O["o_rope_s"][:, :], KPE.t[:, 0:n], [KPE.buf], [], KPE.buf)
        lat = lambda c, a, b: KC.t[:, c * NKM + a:c * NKM + b]
        kpe = lambda a, b: self.KPT.t[:, a:b]
        if not sample:
            for c in range(2):
                self.cp("pool", lat(c, k0, k0 + n), ckb.b()[:, c * 512:c * 512 + n], ckb.bufs, [KC.buf])
            self.cp("pool", kpe(k0, k0 + n), kpb.b()[0:64, 0:n], kpb.bufs, [KC.buf])
        wuk = self.alloc(4)
        wuv = self.alloc(4)
        wukv = wuk.b()[:, 0:4096].rearrange("p (c m) -> p c m", c=2)
        wuvv = wuv.b()[:, 0:4096].rearrange("p (c m) -> p c m", c=2)
        self.dma("sp", wukv, self.W["c_w_uk"].rearrange("(c p) m -> p c m", p=128), [self.Wbuf["c_w_uk"]], wuk.bufs, wuk.bufs[0])
        self.dma("sp", wuvv, self.W["c_w_uv"].rearrange("(c p) m -> p c m", p=128), [self.Wbuf["c_w_uv"]], wuv.bufs, wuv.bufs[0])
        hs = self.alloc(8)
        o_ps, opb = self.bank(6)
        d_ps, dpb = self.bank(7)
        for (c0, L, sq) in (T.seqs if not sample else []):
            if T.kind == "meta":
                kts = [(0, 16, None)]
            elif T.kind == "frame":
                kts = [(0, 16, None)]
                for i in range(4 * T.tidx + 4):
                    kts.append((16 + 128 * i, 128, (i - 4 * T.tidx) if i >= 4 * T.tidx else None))
            else:
                NKC = self.NKC
                self.dma("pool", KC.t[:, 0:2 * NKM].rearrange("p (c k) -> p c k", c=2)[:, :, 0:NKC], self.I["c_lat"][sq],
                         [], [KC.buf], KC.buf)
                self.dma("pool", self.KPT.t[:, 0:NKC], self.I["c_rope"][sq], [], [KC.buf], KC.buf)
                for c in range(2):
                    self.cp("pool", lat(c, NKC, NKC + L), ckb.b()[:, c * 512 + c0:c * 512 + c0 + L], ckb.bufs, [KC.buf])
                self.cp("pool", kpe(NKC, NKC + L), kpb.b()[0:64, c0:c0 + L], kpb.bufs, [KC.buf])
                tot = NKC + L
                kts = [(a, min(128, tot - a), None) for a in range(0, tot, 128)]
            for hp in range(8):
                wq, wqb = self.wl("c_w_uq", 4, hp * 384, 384)
                if hp % 2 == 0:
                    wg, wgb = self.wl(Wd, 8, 832 + (hp // 2) * 512, 512)
                for hq in range(2):
                    h = hp * 2 + hq
                    qnb, qpb, qrb, sg = (self.alloc(1) for _ in range(4))
                    qp = self.alloc(2)
                    ps, pb = self.bank(self.nb_next())
                    for k in range(4):
                        self.mm(ps[:, 0:L], wq[:, k, hq * 192:hq * 192 + 128], qn.b()[:, k * 512 + c0:k * 512 + c0 + L],
                                qn.bufs + [wqb], [pb], start=(k == 0), stop=(k == 3))
                    self.act(qnb.b()[:, 0:L], ps[:, 0:L], AF.Copy, [pb], qnb.bufs, scale=C_SCALE)
                    ps, pb = self.bank(self.nb_next())
                    for k in range(4):
                        self.mm(ps[0:64, 0:L], wq[:, k, hq * 192 + 128:hq * 192 + 192], qn.b()[:, k * 512 + c0:k * 512 + c0 + L],
                                qn.bufs + [wqb], [pb], start=(k == 0), stop=(k == 3))
                    self.act(qpb.b()[0:64, 0:L], ps[0:64, 0:L], AF.Copy, [pb], qpb.bufs)
                    self.tt("dve", qp.f()[0:64, 0:L], ps[0:64, 0:L], rqv[:, 0, c0:c0 + L], ALU.mult, [pb] + rq.bufs, qp.bufs)
                    ps2, pb2 = self.bank(self.nb_next())
                    self.mm(ps2[0:64, 0:L], self.swapbf.t[:, :], qpb.b()[0:64, 0:L], qpb.bufs + [self.swapbf.buf], [pb2])
                    self.tt("dve", qp.f()[0:64, 512:512 + L], ps2[0:64, 0:L], rqv[:, 1, c0:c0 + L], ALU.mult, [pb2] + rq.bufs, qp.bufs)
                    self.tt("dve", qrb.b()[0:64, 0:L], qp.f()[0:64, 0:L], qp.f()[0:64, 512:512 + L], ALU.add, qp.bufs, qrb.bufs)
                    gl = (hp % 2) * 256 + hq * 128
                    ps, pb = self.bank(self.nb_next())
                    for k in range(8):
                        self.mm(ps[:, 0:L], wg[:, k, gl:gl + 128], u.b()[:, k * 512 + c0:k * 512 + c0 + L], u.bufs + [wgb], [pb],
                                start=(k == 0), stop=(k == 7))
                    self.act(sg.f()[:, 0:L], ps[:, 0:L], AF.Exp, [pb], sg.bufs, scale=-1.0)
                    self.act(sg.f()[:, 0:L], sg.f()[:, 0:L], AF.Ln, sg.bufs + [self.oneb.buf], sg.bufs, bias=self.oneb.t[:, 0:1])
                    self.act(sg.f()[:, 0:L], sg.f()[:, 0:L], AF.Exp, sg.bufs, sg.bufs, scale=-1.0)
                    self.tt("dve", sg.f()[:, 0:L], ps[:, 0:L], sg.f()[:, 0:L], ALU.mult, [pb] + sg.bufs, sg.bufs)
                    ntile = len(kts)
                    for sg0 in range(0, ntile, 16):
                        grp = kts[sg0:sg0 + 16]
                        KN = self.alloc(2)
                        VV = self.alloc(2)
                        gbase = grp[0][0]
                        gend = grp[-1][0] + grp[-1][1]
                        for qi, g0 in enumerate(range(gbase, gend, 512)):
                            gw = min(512, gend - g0)
                            kn_ps, knb_ = self.bank(2 + (qi % 2))
                            for c in range(2):
                                self.mm(kn_ps[:, 0:gw], wukv[:, c, h * 128:(h + 1) * 128], lat(c, g0, g0 + gw), wuk.bufs + [KC.buf], [knb_],
                                        start=(c == 0), stop=(c == 1))
                            self.cp("act" if qi % 2 == 0 else "dve", KN.b()[:, g0 - gbase:g0 - gbase + gw], kn_ps[:, 0:gw], [knb_], KN.bufs)
                        for qi in range(0, len(grp), 4):
                            sub = grp[qi:qi + 4]
                            v_ps, vpb = self.bank(2 + ((qi // 4) % 2))
                            for j, (a, kw, mi) in enumerate(sub):
                                for c in range(2):
                                    self.mm(v_ps[0:kw, j * 128:(j + 1) * 128], lat(c, a, a + kw), wuvv[:, c, h * 128:(h + 1) * 128],
                                            wuv.bufs + [KC.buf], [vpb], start=(c == 0), stop=(c == 1))
                            w_ = len(sub) * 128
                            self.cp("dve" if (qi // 4) % 2 == 0 else "act", VV.b()[:, qi * 128:qi * 128 + w_], v_ps[:, 0:w_], [vpb], VV.bufs)

                        def stage_a(li):
                            a, kw, mi = grp[li]
                            s_ps, spb = self.bank(4 + (li % 2))
                            self.mm(s_ps[0:kw, 0:L], KN.b()[:, a - gbase:a - gbase + kw], qnb.b()[:, 0:L], KN.bufs + qnb.bufs, [spb],
                                    start=True, stop=False)
                            self.mm(s_ps[0:kw, 0:L], kpe(a, a + kw), qrb.b()[0:64, 0:L], [KC.buf] + qrb.bufs, [spb],
                                    start=False, stop=(mi is None))
                            if mi is not None:
                                self.mm(s_ps[0:kw, 0:L], self.identbf.t[0:kw, 0:kw], self.negm.t[0:kw, mi * 512:mi * 512 + L],
                                        [self.identbf.buf, self.negm.buf], [spb], start=False, stop=True)
                            P = self.alloc(1)
                            self.act(P.b()[0:kw, 0:L], s_ps[0:kw, 0:L], AF.Exp, [spb], P.bufs)
                            return P

                        def stage_b(li, P):
                            a, kw, mi = grp[li]
                            gi = sg0 + li
                            first, last = gi == 0, gi == ntile - 1
                            self.mm(o_ps[:, 0:L], VV.b()[0:kw, li * 128:(li + 1) * 128], P.b()[0:kw, 0:L], VV.bufs + P.bufs, [opb],
                                    start=first, stop=last)
                            self.mm(d_ps[:, 0:L], self.onesbf.t[0:kw, :], P.b()[0:kw, 0:L], P.bufs + [self.onesbf.buf], [dpb],
                                    start=first, stop=last)
                            self.free(P)
                        Pprev = stage_a(0)
                        for li in range(len(grp)):
                            Pn = stage_a(li + 1) if li + 1 < len(grp) else None
                            stage_b(li, Pprev)
                            Pprev = Pn
                        self.free(KN, VV)
                    rd = self.alloc(1)
                    self.act(rd.f()[:, 0:L], d_ps[:, 0:L], AF.Ln, [dpb], rd.bufs)
                    self.act(rd.f()[:, 0:L], rd.f()[:, 0:L], AF.Exp, rd.bufs, rd.bufs, scale=-1.0)
                    self.tt("dve", rd.f()[:, 0:L], o_ps[:, 0:L], rd.f()[:, 0:L], ALU.mult, [opb] + rd.bufs, rd.bufs)
                    self.tt("dve", hs.b()[:, h * 512 + c0:h * 512 + c0 + L], rd.f()[:, 0:L], sg.f()[:, 0:L], ALU.mult, rd.bufs + sg.bufs, hs.bufs)
                    self.free(qnb, qp, qpb, qrb, sg, rd)
        if sample:
            self.l2_sample(T, u, qn, rq, rqv, ckb, kpb, wuk, wukv, wuv, wuvv, hs, Wd, lat, kpe)
            self.free(u, rq, qn, ckb, kpb, wuv)
        else:
            self.free(u, rq, qn, ckb, kpb, wuk, wuv)
        self.outproj(2, hs, "c_w_out", n)
        self.free(hs)


    def l2_sample(self, T, u, qn, rq, rqv, ckb, kpb, wuk, wukv, wuv, wuvv, hs, Wd, lat, kpe):
        n, NS, NKC, NKM, KC = T.n, self.NS, self.NKC, self.NKMAX, self.KC
        WT = self.alloc(4)
        for h in range(16):
            tp, tpb = self.bank(2 + (h % 2))
            tpv = tp[:, :].bitcast(BF16)
            for c in range(2):
                self.tr(tpv[:, c * 128:(c + 1) * 128], wukv[:, c, h * 128:(h + 1) * 128], wuk.bufs, [tpb])
            self.cp("act" if h % 2 == 0 else "dve", WT.b()[:, h * 256:(h + 1) * 256], tpv[:, 0:256], [tpb], WT.bufs)
        self.free(wuk)
        QA = [self.alloc(4), self.alloc(4)]
        QR = self.alloc(4)
        qav = [q.b()[:, 0:NS * 1024].rearrange("p (s h q) -> p s h q", s=NS, h=16) for q in QA]
        qrv = QR.b()[0:64, 0:NS * 1024].rearrange("p (s h q) -> p s h q", s=NS, h=16)
        for hp in range(8):
            wq, wqb = self.wl("c_w_uq", 4, hp * 384, 384)
            for hq in range(2):
                h = hp * 2 + hq
                qnb, qpb = self.alloc(1), self.alloc(1)
                qp = self.alloc(2)
                ps, pb = self.bank(self.nb_next())
                for k in range(4):
                    self.mm(ps[:, 0:n], wq[:, k, hq * 192:hq * 192 + 128], qn.b()[:, k * 512:k * 512 + n],
                            qn.bufs + [wqb], [pb], start=(k == 0), stop=(k == 3))
                self.act(qnb.b()[:, 0:n], ps[:, 0:n], AF.Copy, [pb], qnb.bufs, scale=C_SCALE)
                for c in range(2):
                    ps, pb = self.bank(self.nb_next())
                    self.mm(ps[:, 0:n], WT.b()[:, h * 256 + c * 128:h * 256 + (c + 1) * 128], qnb.b()[:, 0:n], WT.bufs + qnb.bufs, [pb])
                    self.cp("act" if c == 0 else "dve", qav[c][:, :, h, :], ps[:, 0:n].rearrange("p (s q) -> p s q", s=NS), [pb], QA[c].bufs)
                ps, pb = self.bank(self.nb_next())
                for k in range(4):
                    self.mm(ps[0:64, 0:n], wq[:, k, hq * 192 + 128:hq * 192 + 192], qn.b()[:, k * 512:k * 512 + n],
                            qn.bufs + [wqb], [pb], start=(k == 0), stop=(k == 3))
                self.act(qp.f()[0:64, 0:n], ps[0:64, 0:n], AF.Copy, [pb], qp.bufs)
                self.cp("pool", qpb.b()[0:64, 0:n], qp.f()[0:64, 0:n], qp.bufs, qpb.bufs)
                ps2, pb2 = self.bank(self.nb_next())
                self.mm(ps2[0:64, 0:n], self.swapbf.t[:, :], qpb.b()[0:64, 0:n], qpb.bufs + [self.swapbf.buf], [pb2])
                self.tt("dve", qp.f()[0:64, 0:n], qp.f()[0:64, 0:n], rqv[:, 0, 0:n], ALU.mult, qp.bufs + rq.bufs, qp.bufs)
                self.tt("dve", qp.f()[0:64, 512:512 + n], ps2[0:64, 0:n], rqv[:, 1, 0:n], ALU.mult, [pb2] + rq.bufs, qp.bufs)
                self.tt("dve", qrv[:, :, h, :], qp.f()[0:64, 0:n].rearrange("p (s q) -> p s q", s=NS),
                        qp.f()[0:64, 512:512 + n].rearrange("p (s q) -> p s q", s=NS), ALU.add, qp.bufs, QR.bufs)
                self.free(qnb, qpb, qp)
        self.free(WT)
        ob = [self.bank(0), self.bank(1)]
        d_ps, dpb = self.bank(2)
        for s_ in range(NS):
            c0 = 64 * s_
            self.dma("pool", KC.t[:, 0:2 * NKM].rearrange("p (c k) -> p c k", c=2)[:, :, 0:NKC], self.I["c_lat"][s_],
                     [], [KC.buf], KC.buf)
            self.dma("pool", self.KPT.t[:, 0:NKC], self.I["c_rope"][s_], [], [KC.buf], KC.buf)
            for c in range(2):
                self.cp("pool", lat(c, NKC, NKC + 64), ckb.b()[:, c * 512 + c0:c * 512 + c0 + 64], ckb.bufs, [KC.buf])
            self.cp("pool", kpe(NKC, NKC + 64), kpb.b()[0:64, c0:c0 + 64], kpb.bufs, [KC.buf])
            tot = NKC + 64
            kts = [(a, min(128, tot - a)) for a in range(0, tot, 128)]
            for half in range(2):
                hsl = slice(half * 8, half * 8 + 8)
                qa = [qav[c][:, s_, hsl, :] for c in range(2)]
                qr = qrv[:, s_, hsl, :]

                def stage_a(ki):
                    a, kw = kts[ki]
                    s_ps, spb = self.bank(4 + (ki % 2))
                    for c in range(2):
                        self.mm(s_ps[0:kw, :], lat(c, a, a + kw), qa[c], [KC.buf] + QA[c].bufs, [spb], start=(c == 0), stop=False)
                    self.mm(s_ps[0:kw, :], kpe(a, a + kw), qr, [KC.buf] + QR.bufs, [spb], start=False, stop=True)
                    P = self.alloc(1)
                    self.act(P.b()[0:kw, 0:512], s_ps[0:kw, :], AF.Exp, [spb], P.bufs)
                    tp, tpb = self.bank(3)
                    tpv = tp[:, :].bitcast(BF16)
                    for c in range(2):
                        self.tr(tpv[0:kw, c * 128:(c + 1) * 128], lat(c, a, a + kw), [KC.buf], [tpb])
                    LT = self.alloc(1)
                    self.cp("dve", LT.b()[0:kw, 0:256], tpv[0:kw, 0:256], [tpb], LT.bufs)
                    return P, LT

                def stage_b(ki, st):
                    P, LT = st
                    a, kw = kts[ki]
                    first, last = ki == 0, ki == len(kts) - 1
                    for c in range(2):
                        self.mm(ob[c][0][:, :], LT.b()[0:kw, c * 128:(c + 1) * 128], P.b()[0:kw, 0:512], LT.bufs + P.bufs, [ob[c][1]],
                                start=first, stop=last)
                    self.mm(d_ps[:, :], self.onesbf.t[0:kw, :], P.b()[0:kw, 0:512], P.bufs + [self.onesbf.buf], [dpb], start=first, stop=last)
                    self.free(P, LT)
                st = stage_a(0)
                for ki in range(len(kts)):
                    nx = stage_a(ki + 1) if ki + 1 < len(kts) else None
                    stage_b(ki, st)
                    st = nx
                rd = self.alloc(1)
                self.act(rd.f()[:, :], d_ps[:, :], AF.Ln, [dpb], rd.bufs)
                self.act(rd.f()[:, :], rd.f()[:, :], AF.Exp, rd.bufs, rd.bufs, scale=-1.0)
                OL = self.alloc(1)
                for c in range(2):
                    self.tt("dve", OL.b()[:, c * 512:(c + 1) * 512], ob[c][0][:, :], rd.f()[:, :], ALU.mult, [ob[c][1]] + rd.bufs, OL.bufs)
                po, pob = self.bank(6)
                for hq in range(8):
                    h = half * 8 + hq
                    for c in range(2):
                        self.mm(po[:, hq * 64:(hq + 1) * 64], wuvv[:, c, h * 128:(h + 1) * 128], OL.b()[:, c * 512 + hq * 64:c * 512 + (hq + 1) * 64],
                                wuv.bufs + OL.bufs, [pob], start=(c == 0), stop=(c == 1))
                hsv = hs.b()[:, half * 8 * 512:(half * 8 + 8) * 512].rearrange("p (h t) -> p h t", h=8)[:, :, c0:c0 + 64]
                self.cp("act", hsv, po[:, :].rearrange("p (h t) -> p h t", h=8), [pob], hs.bufs)
                self.free(rd, OL)
        self.free(QA[0], QA[1], QR)
        for g4 in range(4):
            wg, wgb = self.wl(Wd, 8, 832 + g4 * 512, 512)
            for i in range(4):
                h = g4 * 4 + i
                ps, pb = self.proj8(wg, wgb, i * 128, u, n)
                sg = self.alloc(1)
                self.act(sg.f()[:, 0:n], ps[:, 0:n], AF.Silu, [pb], sg.bufs)
                hv = hs.b()[:, h * 512:h * 512 + n]
                self.tt("dve", hv, hv, sg.f()[:, 0:n], ALU.mult, hs.bufs + sg.bufs, hs.bufs)
                self.free(sg)

    def alloc_states(self):
        NS = self.NS
        self.hst = self.sb("hst", [128, 16])
        self.hal1 = self.sb("hal1", [128, 48])
        self.hal3 = self.sb("hal3", [128, 480])
        self.hstm = self.sb("hstm", [128, 16])
        self.hal1m = self.sb("hal1m", [128, 48])
        self.hal3m = self.sb("hal3m", [128, 480])
        self.hstS = self.sb("hstS", [128, NS * 16])
        self.hal1S = self.sb("hal1S", [128, NS * 48])
        self.hal3S = self.sb("hal3S", [128, NS * 480])
        self.cneg = self.sb("cneg", [128, 32])
        self.oneb = self.sb("oneb", [128, 1])
        self.KC = self.sb("kc", [128, 2 * self.NKMAX], BF16)
        self.KPT = DT(self.st.enter_context(self.nc.sbuf_tensor("sb_kpt", [64, self.NKMAX], BF16)), self.KC.buf)
        self.CKV = self.sb("ckv", [128, 1024])
        self.KPE = self.sb("kpe", [64, 512])
        self.ROPE = self.sb("rope", [64, 1024])

    def prologue_rest(self):
        self.memset("pool", self.oneb.t[:, :], 1.0, [self.oneb.buf])
        for t in (self.hst, self.hal1, self.hal3):
            self.memset("pool", t.t[:, :], 0.0, [t.buf])
        c = self.cneg
        self.act(c.t[:, 0:16], self.par("blam", 0, 16), AF.Exp, [self.params.buf], [c.buf], scale=-1.0)
        self.act(c.t[:, 0:16], c.t[:, 0:16], AF.Ln, [c.buf, self.oneb.buf], [c.buf], bias=self.oneb.t[:, 0:1])
        self.ts("dve", c.t[:, 16:32], c.t[:, 0:16], -16.0, None, ALU.mult, None, [c.buf], [c.buf])
        self.ts("dve", c.t[:, 0:16], c.t[:, 0:16], -8.0, None, ALU.mult, None, [c.buf], [c.buf])

    def save_meta(self):
        for q in range(4):
            self.dma("pool", self.S32m[q], self.S32[q].t[:, :], [self.S32[q].buf], [], self.S32[q].buf)
        for a, b in ((self.hstm, self.hst), (self.hal1m, self.hal1), (self.hal3m, self.hal3)):
            self.cp("pool", a.t[:, :], b.t[:, :], [b.buf], [a.buf])

    def restore_meta(self, s):
        for q in range(4):
            self.dma("pool", self.S32[q].t[:, :], self.S32m[q], [], [self.S32[q].buf], self.S32[q].buf)
            self.cp("act", self.Sbf[q].t[:, :], self.S32[q].t[:, :], [self.S32[q].buf], [self.Sbf[q].buf])
        for a, b in ((self.hstm, self.hst), (self.hal1m, self.hal1), (self.hal3m, self.hal3)):
            self.cp("pool", b.t[:, :], a.t[:, :], [a.buf], [b.buf])

    def store_seq(self, s):
        if 0 in self.layers:
            self.store_S(s, 0)
            self.store_S(s, 1)
        if 1 in self.layers:
            self.dma("pool", self.O["o_h"][:, s, :], self.hst.t[:, :], [self.hst.buf], [], self.hst.buf)
            self.dma("pool", self.O["o_c1"][:, s, :, :], self.hal1.t[:, :].rearrange("p (j k) -> p j k", j=16), [self.hal1.buf], [], self.hal1.buf)
        if 3 in self.layers:
            self.dma("pool", self.O["o_c3"][:, s, :, :], self.hal3.t[:, :].rearrange("p (j k) -> p j k", j=16), [self.hal3.buf], [], self.hal3.buf)

    def load_sample_states(self):
        NS = self.NS
        self.dma("pool", self.hstS.t[:, :].rearrange("p (s j) -> p s j", s=NS), self.I["st_h"][:, :, :], [], [self.hstS.buf], self.hstS.buf)
        self.dma("pool", self.hal1S.t[:, :].rearrange("p (s j k) -> p s j k", s=NS, j=16), self.I["st_c1"][:, :, :, :], [], [self.hal1S.buf], self.hal1S.buf)
        self.dma("pool", self.hal3S.t[:, :].rearrange("p (s j k) -> p s j k", s=NS, j=16), self.I["st_c3"][:, :, :, :], [], [self.hal3S.buf], self.hal3S.buf)

    def store_sample_states(self):
        NS, NPS = self.NS, self.NPS
        if 1 in self.layers:
            self.dma("pool", self.O["o_h"][:, NPS:NPS + NS, :], self.hstS.t[:, :].rearrange("p (s j) -> p s j", s=NS), [self.hstS.buf], [], self.hstS.buf)
            self.dma("pool", self.O["o_c1"][:, NPS:NPS + NS, :, :], self.hal1S.t[:, :].rearrange("p (s j k) -> p s j k", s=NS, j=16), [self.hal1S.buf], [], self.hal1S.buf)
        if 3 in self.layers:
            self.dma("pool", self.O["o_c3"][:, NPS:NPS + NS, :, :], self.hal3S.t[:, :].rearrange("p (s j k) -> p s j k", s=NS, j=16), [self.hal3S.buf], [], self.hal3S.buf)

    def declare_io(self):
        NPS, NT, NS = self.NPS, self.NT, self.NS
        n_s = NS * 64
        I, O, W = {}, {}, {}
        self.W32, self.Wbuf = {}, {}
        I["xp"] = self.dram_in("xp", [NPS, NT, 128, 8, 512])
        I["xm"] = self.dram_in("xm", [128, 8, 16])
        I["xs"] = self.dram_in("xs", [128, 8, n_s])
        I["st_hgrn"] = self.dram_in("st_hgrn", [NS, 128, 16, 128])
        I["st_h"] = self.dram_in("st_h", [128, NS, 16])
        I["st_c1"] = self.dram_in("st_c1", [128, NS, 16, 3])
        I["st_c3"] = self.dram_in("st_c3", [128, NS, 16, 30])
        I["c_lat"] = self.dram_in("c_lat", [NS, 128, 2, self.NKC])
        I["c_rope"] = self.dram_in("c_rope", [NS, 64, self.NKC])
        I["params"] = self.dram_in("params", [128, NPAR])
        I["consts"] = self.dram_in("consts", [128, NCON])
        I["ropeP"] = self.dram_in("ropeP", [64, 2, self.NKP])
        I["ropeS"] = self.dram_in("ropeS", [64, 2, n_s])
        for name, shp in [("a_w_in", [1024, 8192]), ("a_w_out", [2048, 1024]), ("b_w_in", [1024, 4096]),
                          ("b_wa", [8, 256, 256]), ("b_wx", [8, 256, 256]), ("b_w_out", [2048, 1024]),
                          ("c_w_in", [1024, 2880]), ("c_w_uq", [512, 3072]), ("c_w_uk", [256, 2048]),
                          ("c_w_uv", [256, 2048]), ("c_w_out", [2048, 1024]), ("d_w_in", [1024, 6144]),
                          ("d_w_out", [2048, 1024])]:
            if len(shp) == 3:
                shp = [shp[0] * shp[1], shp[2]]
            self.W32[name] = self.dram_in(name, shp)
            W[name] = self.nc.dram_tensor(name + "_bf", list(shp), BF16, kind="Internal").ap()
            self.Wbuf[name] = Buf("W" + name)
        O["yp"] = self.dram_out("yp", [NPS, NT, 128, 8, 512])
        O["ys"] = self.dram_out("ys", [128, 8, n_s])
        O["o_hgrn"] = self.dram_out("o_hgrn", [NPS + NS, 128, 16, 128])
        O["o_h"] = self.dram_out("o_h", [128, NPS + NS, 16])
        O["o_c1"] = self.dram_out("o_c1", [128, NPS + NS, 16, 3])
        O["o_c3"] = self.dram_out("o_c3", [128, NPS + NS, 16, 30])
        O["o_lat_p"] = self.dram_out("o_lat_p", [NPS, 128, 2, self.NKP])
        O["o_lat_s"] = self.dram_out("o_lat_s", [128, 2, n_s])
        O["o_rope_p"] = self.dram_out("o_rope_p", [NPS, 64, self.NKP])
        O["o_rope_s"] = self.dram_out("o_rope_s", [64, n_s])
        self.I, self.O, self.W = I, O, W

    def build(self):
        nc = self.nc
        NPS, NT, NS = self.NPS, self.NT, self.NS
        self.declare_io()
        with contextlib.ExitStack() as st:
            self.st = st
            self.A = st.enter_context(nc.sbuf_tensor("arena", [128, self.NU * 512], F32))
            self.abufs = [Buf("a%d" % i) for i in range(self.NU)]
            self.afree = [True] * self.NU
            self.PS = [st.enter_context(nc.psum_tensor("ps%d" % i, [128, 512], F32)) for i in range(8)]
            self.pbufs = [Buf("ps%d" % i, ps=True) for i in range(8)]
            self._mb = 0
            self.XT = self.sb("xt", [128, 4096])
            self.wslots = [self.sb("w%d" % i, [128, 4096], BF16) for i in range(self.NW)]
            self.params = self.sb("params", [128, NPAR])
            self.identbf = self.sb("identbf", [128, 128], BF16)
            self.swapbf = self.sb("swapbf", [64, 64], BF16)
            self.tri = self.sb("tri", [32, 512])
            self.scanm = self.sb("scanm", [128, 512])
            self.negm = self.sb("negm", [128, 2048], BF16)
            self.onesbf = self.sb("onesbf", [128, 128], BF16)
            self.epsb = self.sb("epsb", [128, 1])
            self.oml = self.sb("oml", [128, 16])
            self.noml = self.sb("noml", [128, 16])
            self.S32 = [self.sb("s32_%d" % i, [128, 512]) for i in range(4)]
            self.Sbf = [self.sb("sbf_%d" % i, [128, 512], BF16) for i in range(4)]
            self.S32m = self.nc.dram_tensor("s32m", [4, 128, 512], F32, kind="Internal").ap()
            self.alloc_states()
            xv = self.XT.t[:, :].rearrange("p (c t) -> p c t", c=8)
            self.dma("pool", xv[:, :, 0:16], self.I["xm"][:, :, :], [], [self.XT.buf], self.XT.buf)
            self.prologue()
            Tm = TileDesc("meta", 16, [(0, 16, None)], [(0, 16, None)])
            self.run_layers(Tm)
            self.save_meta()
            for s in range(NPS):
                self.restore_meta(s)
                for t in range(NT):
                    T = TileDesc("frame", 512, [(128 * i, 128, s) for i in range(4)], [(0, 512, s)], seq=s, tidx=t)
                    self.dma("pool", xv, self.I["xp"][s, t], [], [self.XT.buf], self.XT.buf)
                    self.run_layers(T)
                    self.dma("pool", self.O["yp"][s, t], xv, [self.XT.buf], [], self.XT.buf)
                self.store_seq(s)
            n_s = NS * 64
            Ts = TileDesc("sample", n_s, [(64 * i, 64, i) for i in range(NS)], [(64 * i, 64, i) for i in range(NS)])
            self.dma("pool", xv[:, :, 0:n_s], self.I["xs"][:, :, :], [], [self.XT.buf], self.XT.buf)
            self.load_sample_states()
            self.run_layers(Ts)
            self.dma("pool", self.O["ys"][:, :, :], xv[:, :, 0:n_s], [self.XT.buf], [], self.XT.buf)
            self.store_sample_states()
            with nc.allow_low_precision("bf16 matmul operands, fp32 accumulation"):
                self.S.emit()
        return nc

    def convert_weights(self):
        for name in ["a_w_in", "a_w_out", "b_w_in", "b_wa", "b_wx", "b_w_out", "c_w_in", "c_w_uq", "c_w_uk", "c_w_uv",
                     "c_w_out", "d_w_in", "d_w_out"]:
            src, dst, wb = self.W32[name], self.W[name], self.Wbuf[name]
            rows = src.shape[0]
            for r0 in range(0, rows, 128):
                self.dma("pool", dst[r0:r0 + 128, :], src[r0:r0 + 128, :], [], [wb], wb)

    def run_layers(self, T):
        for l in self.layers:
            getattr(self, "layer%d" % l)(T)

    def prologue(self):
        self.dma("sp", self.params.t[:, :], self.I["params"][:, :], [], [self.params.buf], self.params.buf)
        cs = self.alloc(7)
        self.dma("sp", cs.f()[:, 0:NCON], self.I["consts"][:, :], [], cs.bufs, cs.bufs[0])

        def c(name, w, rows=128):
            return cs.f()[0:rows, CON_OFF[name]:CON_OFF[name] + w]
        self.cp("dve", self.identbf.t[:, :], c("ident", 128), cs.bufs, [self.identbf.buf])
        self.cp("dve", self.swapbf.t[:, :], c("swap", 64, 64), cs.bufs, [self.swapbf.buf])
        self.cp("dve", self.tri.t[:, :], c("tri", 512, 32), cs.bufs, [self.tri.buf])
        self.cp("dve", self.scanm.t[:, :], c("scanm", 512), cs.bufs, [self.scanm.buf])
        self.ts("dve", self.negm.t[:, :], c("cmask", 2048), -1.0, 30000.0, ALU.add, ALU.mult, cs.bufs, [self.negm.buf])
        self.free(cs)
        self.memset("dve", self.onesbf.t[:, :], 1.0, [self.onesbf.buf])
        self.memset("dve", self.epsb.t[:, :], EPS, [self.epsb.buf])
        self.tt("dve", self.oml.t[:, :], self.par("lb1", 0, 16), self.par("lb0", 0, 16), ALU.subtract,
                [self.params.buf], [self.oml.buf])
        self.act(self.oml.t[:, :], self.oml.t[:, :], AF.Sigmoid, [self.oml.buf], [self.oml.buf])
        self.ts("dve", self.noml.t[:, :], self.oml.t[:, :], -1.0, None, ALU.mult, None, [self.oml.buf], [self.noml.buf])
        for q in range(4):
            self.memset("pool", self.S32[q].t[:, :], 0.0, [self.S32[q].buf])
            self.memset("pool", self.Sbf[q].t[:, :], 0.0, [self.Sbf[q].buf])
        self.prologue_rest()
        self.convert_weights()


def core_inputs(inp, cfg, core, shared):
    NPS, NT, NS, PAST = cfg["NPS"], cfg["NT"], cfg["NS"], cfg["PAST"]
    f = np.float32
    d = dict(shared)
    xp = np.asarray(inp["x_prompt"][core * NPS:(core + 1) * NPS], f)
    d["xp"] = np.ascontiguousarray(xp.reshape(NPS, NT, 512, 8, 128).transpose(0, 1, 4, 3, 2))
    xs = np.asarray(inp["x_sample"][core * NS:(core + 1) * NS], f)
    d["xs"] = np.ascontiguousarray(xs.reshape(NS * 64, 8, 128).transpose(2, 1, 0))
    sl = slice(core * NS, (core + 1) * NS)
    d["st_hgrn"] = np.ascontiguousarray(np.asarray(inp["state_hgrn"][0][sl], f).transpose(0, 2, 1, 3))
    d["st_h"] = np.ascontiguousarray(np.asarray(inp["state_rglru_h"][0][sl], f).reshape(NS, 16, 128).transpose(2, 0, 1))
    d["st_c1"] = np.ascontiguousarray(np.asarray(inp["state_rglru_conv"][0][sl], f).reshape(NS, 3, 16, 128).transpose(3, 0, 2, 1))
    d["st_c3"] = np.ascontiguousarray(np.asarray(inp["state_conformer_conv"][0][sl], f).reshape(NS, 30, 16, 128).transpose(3, 0, 2, 1))
    cl = np.asarray(inp["cache_mla_latent"][0][sl], f)
    d["c_lat"] = np.ascontiguousarray(cl.reshape(NS, -1, 2, 128).transpose(0, 3, 2, 1))
    d["c_rope"] = np.ascontiguousarray(np.asarray(inp["cache_mla_rope"][0][sl], f).transpose(0, 2, 1))
    return d


def shared_inputs(inp, cfg):
    NT, NS, PAST = cfg["NT"], cfg["NS"], cfg["PAST"]
    f = np.float32
    d = {}
    d["xm"] = np.ascontiguousarray(np.asarray(inp["meta_tokens"], f).reshape(16, 8, 128).transpose(2, 1, 0))
    d["params"] = pack_params(inp)
    d["consts"] = pack_consts()
    d["ropeP"] = rope_table(np.arange(16 + 512 * NT))
    rs = rope_table(16 + PAST + np.arange(64))
    d["ropeS"] = np.ascontiguousarray(np.tile(rs, (1, 1, NS)))
    for name in ["a_w_in", "a_w_out", "b_w_in", "b_wa", "b_wx", "b_w_out", "c_w_in", "c_w_uq", "c_w_uk", "c_w_uv",
                 "c_w_out", "d_w_in", "d_w_out"]:
        w = np.asarray(inp[name][0], f)
        d[name] = np.ascontiguousarray(w.reshape(-1, w.shape[-1]))
    return d


def assemble(results, cfg):
    NPS, NT, NS = cfg["NPS"], cfg["NT"], cfg["NS"]
    ncore = len(results)
    T = 512 * NT
    cat = lambda xs: np.concatenate(xs, axis=0)
    yp = cat([r["yp"].transpose(0, 1, 4, 3, 2).reshape(NPS, T, 1024) for r in results])
    ys = cat([r["ys"].transpose(2, 1, 0).reshape(NS, 64, 1024) for r in results])
    hg = [r["o_hgrn"].transpose(0, 2, 1, 3) for r in results]
    hgp = cat([h[:NPS] for h in hg])[None]
    hgs = cat([h[NPS:] for h in hg])[None]
    oh = [r["o_h"].transpose(1, 2, 0).reshape(NPS + NS, 2048) for r in results]
    ohp = cat([h[:NPS] for h in oh])[None]
    ohs = cat([h[NPS:] for h in oh])[None]
    c1 = [r["o_c1"].transpose(1, 3, 2, 0).reshape(NPS + NS, 3, 2048) for r in results]
    c1p = cat([h[:NPS] for h in c1])[None]
    c1s = cat([h[NPS:] for h in c1])[None]
    c3 = [r["o_c3"].transpose(1, 3, 2, 0).reshape(NPS + NS, 30, 2048) for r in results]
    c3p = cat([h[:NPS] for h in c3])[None]
    c3s = cat([h[NPS:] for h in c3])[None]
    latp = cat([r["o_lat_p"].transpose(0, 3, 2, 1).reshape(NPS, 16 + T, 256) for r in results])[None]
    lats = cat([r["o_lat_s"].transpose(2, 1, 0).reshape(NS, 64, 256) for r in results])[None]
    ropp = cat([r["o_rope_p"].transpose(0, 2, 1) for r in results])[None]
    rops = cat([r["o_rope_s"].T.reshape(NS, 64, 64) for r in results])[None]
    outs = (yp, ys, hgp, hgs, ohp, ohs, c1p, c1s, latp, lats, ropp, rops, c3p, c3s)
    return tuple(np.ascontiguousarray(o, dtype=np.float32) for o in outs)


def run(inp, cfg, ncore):
    b = Builder(cfg)
    nc = b.build()
    shared = shared_inputs(inp, cfg)
    in_maps = [core_inputs(inp, cfg, c, shared) for c in range(ncore)]
    res = run_bass_kernel_spmd(nc, in_maps, core_ids=list(range(ncore)))
    return assemble(res.results, cfg)


def kernel(**inputs):
    cfg = dict(NPS=4, NT=4, NS=4, PAST=4096)
    return run(inputs, cfg, 8)
```

```python
import contextlib
import numpy as np
import concourse.bass as bass
import concourse.mybir as mybir
from concourse.bass_utils import run_bass_kernel_spmd

F32 = mybir.dt.float32
BF16 = mybir.dt.bfloat16
ALU = mybir.AluOpType
AF = mybir.ActivationFunctionType

EPS = 1e-6
C_SCALE = 192.0 ** -0.5
ENGS = ("pe", "act", "dve", "pool", "sp")


class Buf:
    __slots__ = ("name", "last_w", "readers", "dsem", "dcount", "ps", "last_by")

    def __init__(self, name, ps=False):
        self.name = name
        self.last_w = None
        self.readers = []
        self.dsem = None
        self.dcount = 0
        self.ps = ps
        self.last_by = {}


class Op:
    __slots__ = ("eng", "fn", "waits", "signal", "sigval", "dma_buf", "dma_val")

    def __init__(self, eng, fn):
        self.eng = eng
        self.fn = fn
        self.waits = []
        self.signal = False
        self.sigval = None
        self.dma_buf = None
        self.dma_val = None


class Sched:
    def __init__(self, nc):
        self.nc = nc
        self.ops = {e: [] for e in ENGS}
        self.nops = 0

    def _dep(self, op, tok):
        if tok is None:
            return
        if tok.dma_buf is not None:
            op.waits.append(("dma", tok.dma_buf, tok.dma_val))
            return
        if tok.eng == op.eng and op.eng == "pe":
            return
        tok.signal = True
        op.waits.append(tok)

    def add(self, eng, fn, reads=(), writes=(), dma_buf=None):
        op = Op(eng, fn)
        for b in reads:
            if b.ps:
                continue
            self._dep(op, b.last_w)
        for b in writes:
            if b.ps:
                continue
            self._dep(op, b.last_w)
            for r in b.readers:
                if r.dma_buf is None and r.eng == eng:
                    continue
                self._dep(op, r)
        seen = set()
        for b in list(reads) + list(writes):
            if not b.ps or id(b) in seen:
                continue
            seen.add(id(b))
            for e2, o2 in b.last_by.items():
                if e2 != eng:
                    self._dep(op, o2)
            b.last_by[eng] = op
        if dma_buf is not None:
            dma_buf.dcount += 16
            op.dma_buf = dma_buf
            op.dma_val = dma_buf.dcount
        for b in reads:
            if not b.ps:
                b.readers.append(op)
        for b in writes:
            if not b.ps:
                b.last_w = op
                b.readers = []
        self.ops[eng].append(op)
        self.nops += 1
        return op

    def emit(self):
        nc = self.nc
        with contextlib.ExitStack() as stack:
            esem = {e: stack.enter_context(nc.semaphore("s_" + e)) for e in ENGS if e != "sp"}
            for e in ENGS:
                n = 0
                for op in self.ops[e]:
                    if op.dma_buf is None and op.signal:
                        n += 1
                        op.sigval = n
            dbufs, seen = [], set()
            for e in ENGS:
                for op in self.ops[e]:
                    if op.dma_buf is not None and id(op.dma_buf) not in seen:
                        seen.add(id(op.dma_buf))
                        dbufs.append(op.dma_buf)
            for i, b in enumerate(dbufs):
                b.dsem = stack.enter_context(nc.semaphore("d%d_%s" % (i, b.name)))
            self.n_sems = len(dbufs) + 4
            block = stack.enter_context(nc.Block())
            handles = {"pe": "tensor", "act": "scalar", "dve": "vector", "pool": "gpsimd", "sp": "sync"}

            def make(e):
                def body(eng):
                    known = {}
                    for op in self.ops[e]:
                        need = {}
                        for w in op.waits:
                            if isinstance(w, Op):
                                s, v = esem[w.eng], w.sigval
                            else:
                                s, v = w[1].dsem, w[2]
                            k = id(s)
                            if known.get(k, 0) >= v:
                                continue
                            if k not in need or need[k][1] < v:
                                need[k] = (s, v)
                        for k, (s, v) in need.items():
                            eng.wait_ge(s, v)
                            known[k] = v
                        ins = op.fn(eng)
                        if op.dma_buf is not None:
                            ins.then_inc(op.dma_buf.dsem, 16)
                        elif op.signal:
                            ins.then_inc(esem[e], 1)
                    if e == "sp":
                        for b in dbufs:
                            if b.dcount:
                                eng.wait_ge(b.dsem, b.dcount)
                return body

            for e in ENGS:
                getattr(block, handles[e])(make(e))


PAR_FIELDS = [("gpre", 32), ("gpost", 32), ("lb0", 16), ("lb1", 16), ("ang", 1), ("bcw", 64), ("bcb", 16),
              ("bba", 16), ("bbx", 16), ("blam", 16), ("cqn", 4), ("ckvn", 2), ("dcw", 496), ("dcb", 16),
              ("dlg", 16), ("dlb", 16)]
PAR_OFF = {}
_o = 0
for _n, _w in PAR_FIELDS:
    PAR_OFF[_n] = _o
    _o += _w
NPAR = _o

CON_FIELDS = [("ident", 128), ("swap", 64), ("tri", 512), ("cmask", 2048), ("scanm", 512)]
CON_OFF = {}
_o = 0
for _n, _w in CON_FIELDS:
    CON_OFF[_n] = _o
    _o += _w
NCON = _o


def _pc(v):
    v = np.asarray(v, np.float32)
    lead = v.shape[:-1]
    c = v.shape[-1] // 128
    v = v.reshape(lead + (c, 128))
    return np.moveaxis(v, -1, 0)


def pack_params(inp):
    P = np.zeros((128, NPAR), np.float32)

    def put(name, arr):
        arr = np.ascontiguousarray(arr, np.float32).reshape(128, -1)
        P[:, PAR_OFF[name]:PAR_OFF[name] + arr.shape[1]] = arr

    put("gpre", _pc(inp["norm_pre"]))
    put("gpost", _pc(inp["norm_post"]))
    put("lb0", _pc(inp["a_lb_logits"][0]))
    put("lb1", _pc(inp["a_lb_logits"][1]))
    put("ang", np.asarray(inp["a_norm_g"][0]).reshape(128, 1))
    put("bcw", np.transpose(_pc(inp["b_conv_w"][0]), (0, 2, 1)))
    put("bcb", _pc(inp["b_conv_b"][0]))
    put("bba", _pc(inp["b_ba"][0]))
    put("bbx", _pc(inp["b_bx"][0]))
    put("blam", _pc(inp["b_lambda"][0]))
    put("cqn", _pc(inp["c_q_norm"][0]))
    put("ckvn", _pc(inp["c_kv_norm"][0]))
    put("dcw", np.transpose(_pc(inp["d_conv_w"][0]), (0, 2, 1)))
    put("dcb", _pc(inp["d_conv_b"][0]))
    put("dlg", _pc(inp["d_ln_g"][0]))
    put("dlb", _pc(inp["d_ln_b"][0]))
    return P


def pack_consts():
    C = np.zeros((128, NCON), np.float32)
    C[:, CON_OFF["ident"]:CON_OFF["ident"] + 128] = np.eye(128, dtype=np.float32)
    sw = np.zeros((64, 64), np.float32)
    for i in range(64):
        sw[(i + 32) % 64, i] = 1.0
    C[0:64, CON_OFF["swap"]:CON_OFF["swap"] + 64] = sw
    m = np.arange(64)[:, None]
    l = np.arange(64)[None, :]
    tri = (m <= l).astype(np.float32)
    C[0:64, CON_OFF["tri"]:CON_OFF["tri"] + 512] = np.tile(tri, (1, 8))
    k = np.arange(128)[:, None]
    q = np.arange(512)[None, :]
    for r in range(4):
        mk = ((2 * r + k // 64) <= (q // 64)).astype(np.float32)
        C[:, CON_OFF["cmask"] + r * 512:CON_OFF["cmask"] + (r + 1) * 512] = mk
    sm = np.ones((128, 512), np.float32)
    sm[:, 0::64] = 0.0
    C[:, CON_OFF["scanm"]:CON_OFF["scanm"] + 512] = sm
    return C


def rope_table(pos):
    pos = np.asarray(pos, np.float32)
    inv = (1.0 / (np.float32(10000.0) ** (np.arange(0, 64, 2, dtype=np.float32) / np.float32(64)))).astype(np.float32)
    ang = (pos[:, None] * inv[None, :]).astype(np.float32)
    cos = np.cos(ang).astype(np.float32).T
    sin = np.sin(ang).astype(np.float32).T
    out = np.zeros((64, 2, pos.shape[0]), np.float32)
    out[0:32, 0] = cos
    out[32:64, 0] = cos
    out[0:32, 1] = -sin
    out[32:64, 1] = sin
    return out


class ATile:
    def __init__(self, K, u0, k):
        self.K, self.u0, self.k = K, u0, k
        self.bufs = K.abufs[u0:u0 + k]

    def f(self):
        return self.K.A[:, self.u0 * 512:(self.u0 + self.k) * 512]

    def b(self):
        return self.f().bitcast(BF16)


class DT:
    def __init__(self, t, buf):
        self.t, self.buf = t, buf


class TileDesc:
    def __init__(self, kind, n, segs, seqs, seq=None, tidx=0):
        self.kind, self.n, self.segs, self.seqs, self.seq, self.tidx = kind, n, segs, seqs, seq, tidx


class Builder:
    def __init__(self, cfg):
        self.cfg = cfg
        self.NPS, self.NT, self.NS, self.PAST = cfg["NPS"], cfg["NT"], cfg["NS"], cfg["PAST"]
        self.LS = 64
        self.layers = cfg.get("layers", [0, 1, 2, 3])
        self.NKP = 16 + 512 * self.NT
        self.NKC = 16 + self.PAST
        self.NKS = self.NKC + self.LS
        self.NKMAX = max(self.NKP, self.NKS)
        self.NU = cfg.get("NU", 44)
        self.NW = 4
        self.nc = bass.Bass("TRN2", target_bir_lowering=False)
        self.S = Sched(self.nc)
        self.wi = 0

    def dram_in(self, name, shape):
        return self.nc.dram_tensor(name, list(shape), F32, kind="ExternalInput").ap()

    def dram_out(self, name, shape):
        return self.nc.dram_tensor(name, list(shape), F32, kind="ExternalOutput").ap()

    def sb(self, name, shape, dt=F32):
        t = self.st.enter_context(self.nc.sbuf_tensor("sb_" + name, list(shape), dt))
        return DT(t, Buf(name))

    def alloc(self, k):
        free = self.afree
        for u0 in range(0, self.NU - k + 1):
            if all(free[u0:u0 + k]):
                for i in range(u0, u0 + k):
                    free[i] = False
                return ATile(self, u0, k)
        raise RuntimeError("arena full (need %d units, free %d)" % (k, sum(free)))

    def free(self, *tiles):
        for t in tiles:
            for i in range(t.u0, t.u0 + t.k):
                assert not self.afree[i]
                self.afree[i] = True

    def mm(self, out, lhsT, rhs, R, W, start=True, stop=True):
        self.S.add("pe", lambda e: e.matmul(out, lhsT=lhsT, rhs=rhs, start=start, stop=stop), R, W)

    def tr(self, out, in_, R, W):
        ident = self.identbf.t[:, :]
        self.S.add("pe", lambda e: e.transpose(out, in_, ident), list(R) + [self.identbf.buf], W)

    def act(self, out, in_, func, R, W, scale=1.0, bias=0.0):
        self.S.add("act", lambda e: e.activation(out=out, in_=in_, func=func, bias=bias, scale=scale), R, W)

    def tt(self, eng, out, a, b, op, R, W):
        self.S.add(eng, lambda e: e.tensor_tensor(out=out, in0=a, in1=b, op=op), R, W)

    def ts(self, eng, out, a, s1, s2, op0, op1, R, W):
        if s2 is None:
            self.S.add(eng, lambda e: e.tensor_scalar(out=out, in0=a, scalar1=s1, scalar2=None, op0=op0), R, W)
        else:
            self.S.add(eng, lambda e: e.tensor_scalar(out=out, in0=a, scalar1=s1, scalar2=s2, op0=op0, op1=op1), R, W)

    def stt(self, out, a, s, b, op0, op1, R, W):
        self.S.add("dve", lambda e: e.scalar_tensor_tensor(out=out, in0=a, scalar=s, in1=b, op0=op0, op1=op1), R, W)

    def cp(self, eng, out, in_, R, W):
        if eng == "act":
            self.S.add("act", lambda e: e.activation(out=out, in_=in_, func=AF.Copy), R, W)
        else:
            self.S.add(eng, lambda e: e.tensor_copy(out=out, in_=in_), R, W)

    def scan(self, out, d0, d1, init, R, W):
        self.S.add("dve", lambda e: e.tensor_tensor_scan(out=out, data0=d0, data1=d1, initial=init,
                                                        op0=ALU.mult, op1=ALU.add), R, W)

    def memset(self, eng, ap, val, W):
        self.S.add(eng, lambda e: e.memset(ap, val), [], W)

    def dma(self, q, out, in_, R, W, dbuf):
        self.S.add(q, lambda e: e.dma_start(out=out, in_=in_), R, W, dma_buf=dbuf)

    def par(self, name, i=0, w=1):
        o = PAR_OFF[name] + i
        return self.params.t[:, o:o + w]

    def wl(self, name, kc, c0, ncol, k0=0):
        view = self.W[name].rearrange("(c p) m -> p c m", p=128)[:, k0:k0 + kc, c0:c0 + ncol]
        slot = self.wslots[self.wi % self.NW]
        self.wi += 1
        ap = slot.t[:, 0:kc * ncol].rearrange("p (a b) -> p a b", a=kc)
        self.dma("sp", ap, view, [self.Wbuf[name]], [slot.buf], slot.buf)
        return ap, slot.buf

    def bank(self, i):
        return self.PS[i], self.pbufs[i]

    def rstd_from(self, ps_ap, psbuf, n, inv_d, rows=128):
        r = self.alloc(1)
        ra = r.f()[0:rows, 0:n]
        self.act(ra, ps_ap, AF.Ln, [psbuf, self.epsb.buf], r.bufs, scale=inv_d, bias=self.epsb.t[0:rows, 0:1])
        self.act(ra, ra, AF.Exp, r.bufs, r.bufs, scale=-0.5)
        return r

    def prenorm(self, l, n):
        X = self.XT
        sq = self.alloc(4)
        for c in range(8):
            self.act(sq.b()[:, c * 512:c * 512 + n], X.t[:, c * 512:c * 512 + n], AF.Square, [X.buf], sq.bufs)
        ps, pb = self.bank(self.nb_next())
        for c in range(8):
            self.mm(ps[:, 0:n], self.onesbf.t[:, :], sq.b()[:, c * 512:c * 512 + n], sq.bufs + [self.onesbf.buf], [pb],
                    start=(c == 0), stop=(c == 7))
        r = self.rstd_from(ps[:, 0:n], pb, n, 1.0 / 1024)
        u = sq
        for c in range(8):
            self.stt(u.b()[:, c * 512:c * 512 + n], X.t[:, c * 512:c * 512 + n], self.par("gpre", l * 8 + c),
                     r.f()[:, 0:n], ALU.mult, ALU.mult, [X.buf, self.params.buf] + r.bufs, u.bufs)
        self.free(r)
        return u

    def nb_next(self):
        self._mb = (self._mb + 1) % 2
        return self._mb

    def outproj(self, l, hs, Wd, n):
        X = self.XT
        y = self.alloc(8)
        ysq = self.alloc(4)
        for g in range(4):
            w, wb = self.wl(Wd, 16, g * 256, 256)
            for mm_ in range(2):
                m = g * 2 + mm_
                ps, pb = self.bank(self.nb_next())
                for k in range(16):
                    self.mm(ps[:, 0:n], w[:, k, mm_ * 128:(mm_ + 1) * 128], hs.b()[:, k * 512:k * 512 + n],
                            hs.bufs + [wb], [pb], start=(k == 0), stop=(k == 15))
                ya = y.f()[:, m * 512:m * 512 + n]
                self.act(ya, ps[:, 0:n], AF.Copy, [pb], [y.bufs[m]])
                self.tt("pool", ysq.b()[:, m * 512:m * 512 + n], ya, ya, ALU.mult, [y.bufs[m]], ysq.bufs)
        ps, pb = self.bank(self.nb_next())
        for m in range(8):
            self.mm(ps[:, 0:n], self.onesbf.t[:, :], ysq.b()[:, m * 512:m * 512 + n], ysq.bufs + [self.onesbf.buf],
                    [pb], start=(m == 0), stop=(m == 7))
        r = self.rstd_from(ps[:, 0:n], pb, n, 1.0 / 1024)
        for m in range(8):
            ya = y.f()[:, m * 512:m * 512 + n]
            self.stt(ya, ya, self.par("gpost", l * 8 + m), r.f()[:, 0:n], ALU.mult, ALU.mult,
                     [y.bufs[m], self.params.buf] + r.bufs, [y.bufs[m]])
            xa = X.t[:, m * 512:m * 512 + n]
            self.tt("pool", xa, xa, ya, ALU.add, [X.buf, y.bufs[m]], [X.buf])
        self.free(r, y, ysq)

    def layer0(self, T):
        n = T.n
        bl = min(64, n)
        nb = n // bl
        Wd = "a_w_in"
        u = self.prenorm(0, n)
        hs = self.alloc(8)
        pbuf = self.params.buf
        for half in range(2):
            QT, KT, KP, VV = (self.alloc(4) for _ in range(4))
            EL = self.alloc(1)
            SG = self.alloc(4)
            for pr in range(4):
                h0 = half * 8 + pr * 2
                hhs = [pr * 2, pr * 2 + 1]
                wq, wqb = self.wl(Wd, 8, h0 * 128, 256)
                wg, wgb = self.wl(Wd, 8, 6144 + h0 * 128, 256)
                wf, wfb = self.wl(Wd, 8, 2048 + h0 * 128, 256)
                wi, wib = self.wl(Wd, 8, 4096 + h0 * 128, 256)
                t1 = [self.alloc(1), self.alloc(1)]
                kk = [self.alloc(1), self.alloc(1)]
                fg = [self.alloc(1), self.alloc(1)]
                cum = [self.alloc(1), self.alloc(1)]
                for hp in range(2):
                    ps, pb = self.proj8(wq, wqb, hp * 128, u, n)
                    self.act(t1[hp].f()[:, 0:n], ps[:, 0:n], AF.Silu, [pb], t1[hp].bufs)
                for hp in range(2):
                    ps, pb = self.proj8(wg, wgb, hp * 128, u, n)
                    self.act(SG.b()[:, hhs[hp] * 512:hhs[hp] * 512 + n], ps[:, 0:n], AF.Silu, [pb], SG.bufs)
                for hp in range(2):
                    ps, pb = self.proj8(wf, wfb, hp * 128, u, n)
                    self.act(kk[hp].f()[:, 0:n], ps[:, 0:n], AF.Sigmoid, [pb], kk[hp].bufs, scale=-1.0)
                for hp in range(2):
                    ps, pb = self.proj8(wi, wib, hp * 128, u, n)
                    self.act(VV.b()[:, hhs[hp] * 512:hhs[hp] * 512 + n], ps[:, 0:n], AF.Copy, [pb], VV.bufs)
                for hp in range(2):
                    h = h0 + hp
                    self.ts("dve", fg[hp].f()[:, 0:n], kk[hp].f()[:, 0:n], self.noml.t[:, h:h + 1], 1.0, ALU.mult, ALU.add,
                            kk[hp].bufs + [self.noml.buf], fg[hp].bufs)
                for hp in range(2):
                    self.act(fg[hp].f()[:, 0:n], fg[hp].f()[:, 0:n], AF.Ln, fg[hp].bufs, fg[hp].bufs)
                for hp in range(2):
                    self.scan(cum[hp].f()[:, 0:n], self.scanm.t[:, 0:n], fg[hp].f()[:, 0:n], 0.0,
                              fg[hp].bufs + [self.scanm.buf], cum[hp].bufs)
                for hp in range(2):
                    ee = fg[hp]
                    self.act(ee.f()[:, 0:n], cum[hp].f()[:, 0:n], AF.Exp, cum[hp].bufs, ee.bufs)
                    self.act(cum[hp].f()[:, 0:n], cum[hp].f()[:, 0:n], AF.Exp, cum[hp].bufs, cum[hp].bufs, scale=-1.0)
                for hp in range(2):
                    h = h0 + hp
                    hh = hhs[hp]
                    cs = slice(hh * 512, hh * 512 + n)
                    ee = fg[hp]
                    self.tt("dve", QT.b()[:, cs], t1[hp].f()[:, 0:n], ee.f()[:, 0:n], ALU.mult, t1[hp].bufs + ee.bufs, QT.bufs)
                    self.stt(kk[hp].f()[:, 0:n], kk[hp].f()[:, 0:n], self.oml.t[:, h:h + 1], cum[hp].f()[:, 0:n], ALU.mult, ALU.mult,
                             kk[hp].bufs + cum[hp].bufs + [self.oml.buf], kk[hp].bufs)
                    self.cp("pool", KT.b()[:, cs], kk[hp].f()[:, 0:n], kk[hp].bufs, KT.bufs)
                    elast = ee.f()[:, bl - 1:n:bl]
                    self.tt("dve", KP.b()[:, cs].rearrange("p (b t) -> p b t", t=bl),
                            kk[hp].f()[:, 0:n].rearrange("p (b t) -> p b t", t=bl),
                            elast.unsqueeze(2).broadcast_to([128, nb, bl]), ALU.mult, kk[hp].bufs + ee.bufs, KP.bufs)
                    self.cp("pool", EL.f()[:, hh * 32:hh * 32 + nb], elast, ee.bufs, EL.bufs)
                self.free(*t1, *kk, *fg, *cum)
            for (c0, L, sq) in T.segs:
                nbk = L // bl
                if T.kind == "sample":
                    self.load_S(sq, half)
                att = self.alloc(1)
                for qd in range(2):
                    ps, pb = self.bank(qd)
                    for hq in range(4):
                        hh = qd * 4 + hq
                        for b in range(nbk):
                            cc = hh * 512 + c0 + b * bl
                            oa = (hq * nbk + b) * 64
                            self.mm(ps[0:bl, oa:oa + bl], KT.b()[:, cc:cc + bl], QT.b()[:, cc:cc + bl],
                                    KT.bufs + QT.bufs, [pb])
                    w = 4 * nbk * 64
                    self.tt("dve", att.b()[0:bl, qd * 512:qd * 512 + w], ps[0:bl, 0:w], self.tri.t[0:bl, 0:w], ALU.mult,
                            [pb, self.tri.buf], att.bufs)
                vks = [[self.alloc(1), self.alloc(1)], [self.alloc(1), self.alloc(1)]]
                ops_ = [self.bank(6), self.bank(7)]

                def t_stage(b, qd):
                    tp, tpb = self.bank(2 + qd)
                    tpv = tp[:, :].bitcast(BF16)
                    for hq in range(4):
                        hh = qd * 4 + hq
                        cc = hh * 512 + c0 + b * bl
                        self.tr(tpv[0:bl, (hq * 2) * 128:(hq * 2 + 1) * 128], VV.b()[:, cc:cc + bl], VV.bufs, [tpb])
                        self.tr(tpv[0:bl, (hq * 2 + 1) * 128:(hq * 2 + 2) * 128], KP.b()[:, cc:cc + bl], KP.bufs, [tpb])
                    vk = vks[qd][b % 2]
                    self.cp("act", vk.b()[0:bl, 0:1024], tpv[0:bl, 0:1024], [tpb], vk.bufs)

                def c_stage(b, qd):
                    gq = half * 2 + qd
                    vk = vks[qd][b % 2]
                    op_, opb = ops_[qd]
                    Sb = self.Sbf[gq]
                    for hq in range(4):
                        hh = qd * 4 + hq
                        cc = hh * 512 + c0 + b * bl
                        oc = (hq * nbk + b) * bl
                        oa = (hq * nbk + b) * 64
                        self.mm(op_[:, oc:oc + bl], Sb.t[:, hq * 128:(hq + 1) * 128], QT.b()[:, cc:cc + bl],
                                [Sb.buf] + QT.bufs, [opb], start=True, stop=False)
                        self.mm(op_[:, oc:oc + bl], vk.b()[0:bl, (hq * 2) * 128:(hq * 2 + 1) * 128],
                                att.b()[0:bl, qd * 512 + oa:qd * 512 + oa + bl], vk.bufs + att.bufs, [opb],
                                start=False, stop=True)
                    sp_, spb = self.bank(4 + qd)
                    for hq in range(4):
                        self.mm(sp_[:, hq * 128:(hq + 1) * 128], vk.b()[0:bl, (hq * 2 + 1) * 128:(hq * 2 + 2) * 128],
                                vk.b()[0:bl, (hq * 2) * 128:(hq * 2 + 1) * 128], vk.bufs, [spb])
                    S3 = self.S32[gq]
                    blk = (c0 // bl) + b
                    dec = EL.f()[:, qd * 128:(qd + 1) * 128].rearrange("p (h b) -> p h b", h=4)[:, :, blk:blk + 1]
                    s3v = S3.t[:, :].rearrange("p (h v) -> p h v", h=4)
                    self.tt("dve", s3v, s3v, dec.broadcast_to([128, 4, 128]), ALU.mult, [S3.buf] + EL.bufs, [S3.buf])
                    self.tt("dve", S3.t[:, :], S3.t[:, :], sp_[:, :], ALU.add, [S3.buf, spb], [S3.buf])
                    self.cp("pool", Sb.t[:, :], S3.t[:, :], [S3.buf], [Sb.buf])

                t_stage(0, 0)
                t_stage(0, 1)
                for b in range(nbk):
                    if b + 1 < nbk:
                        t_stage(b + 1, 0)
                        t_stage(b + 1, 1)
                    c_stage(b, 0)
                    c_stage(b, 1)
                for qd in range(2):
                    op_, opb = ops_[qd]
                    w = 4 * L
                    osq = self.alloc(1)
                    self.act(osq.b()[:, 0:w], op_[:, 0:w], AF.Square, [opb], osq.bufs)
                    ss, ssb = self.bank(qd)
                    self.mm(ss[:, 0:w], self.onesbf.t[:, :], osq.b()[:, 0:w], osq.bufs + [self.onesbf.buf], [ssb])
                    r = self.rstd_from(ss[:, 0:w], ssb, w, 1.0 / 128)
                    self.stt(r.f()[:, 0:w], op_[:, 0:w], self.par("ang"), r.f()[:, 0:w],
                             ALU.mult, ALU.mult, [opb] + r.bufs + [pbuf], r.bufs)
                    hq0 = half * 8 + qd * 4
                    hsv = hs.b()[:, hq0 * 512:(hq0 + 4) * 512].rearrange("p (h t) -> p h t", h=4)[:, :, c0:c0 + L]
                    sgv = SG.b()[:, qd * 2048:(qd + 1) * 2048].rearrange("p (h t) -> p h t", h=4)[:, :, c0:c0 + L]
                    self.tt("dve", hsv, r.f()[:, 0:w].rearrange("p (h t) -> p h t", h=4), sgv, ALU.mult, r.bufs + SG.bufs, hs.bufs)
                    self.free(osq, r)
                if T.kind == "sample":
                    self.store_S(self.NPS + sq, half)
                self.free(vks[0][0], vks[0][1], vks[1][0], vks[1][1], att)
            self.free(QT, KT, KP, VV, EL, SG)
        self.free(u)
        self.outproj(0, hs, "a_w_out", n)
        self.free(hs)

    def load_S(self, s, half):
        for qd in range(2):
            gq = half * 2 + qd
            S3 = self.S32[gq]
            self.dma("pool", S3.t[:, :].rearrange("p (h v) -> p h v", h=4), self.I["st_hgrn"][s, :, gq * 4:(gq + 1) * 4, :],
                     [], [S3.buf], S3.buf)
            self.cp("act", self.Sbf[gq].t[:, :], S3.t[:, :], [S3.buf], [self.Sbf[gq].buf])

    def store_S(self, oidx, half):
        for qd in range(2):
            gq = half * 2 + qd
            S3 = self.S32[gq]
            self.dma("pool", self.O["o_hgrn"][oidx, :, gq * 4:(gq + 1) * 4, :], S3.t[:, :].rearrange("p (h v) -> p h v", h=4),
                     [S3.buf], [], S3.buf)


    def proj8(self, w, wb, c0, u, n, M=128):
        ps, pb = self.bank(self.nb_next())
        for k in range(8):
            self.mm(ps[0:M, 0:n], w[:, k, c0:c0 + M], u.b()[:, k * 512:k * 512 + n], u.bufs + [wb], [pb],
                    start=(k == 0), stop=(k == 7))
        return ps, pb

    def layer1(self, T):
        n = T.n
        Wd = "b_w_in"
        pbuf = self.params.buf
        u = self.prenorm(1, n)
        hs = self.alloc(8)
        nseg = len(T.seqs)
        L = T.seqs[0][1]
        sample = T.kind == "sample"
        HW = 3 + L
        def stage_a(g):
            wx_, wxb = self.wl(Wd, 8, g * 256, 256)
            xbufs, xvs = [], []
            for m in range(2):
                xbuf = self.alloc(2)
                xv = xbuf.f()[:, 0:nseg * HW].rearrange("p (s t) -> p s t", s=nseg)
                ps, pb = self.proj8(wx_, wxb, m * 128, u, n)
                self.act(xv[:, :, 3:3 + L], ps[:, 0:n].rearrange("p (s t) -> p s t", s=nseg), AF.Copy, [pb], xbuf.bufs)
                xbufs.append(xbuf)
                xvs.append(xv)
            xcs, xcb = [], []
            for m in range(2):
                j = g * 2 + m
                xbuf, xv = xbufs[m], xvs[m]
                if sample:
                    hal = self.hal1S.t[:, :].rearrange("p (s j k) -> p s j k", s=nseg, j=16)[:, :, j, :]
                    halb = self.hal1S.buf
                else:
                    hal = self.hal1.t[:, j * 3:(j + 1) * 3].unsqueeze(1)
                    halb = self.hal1.buf
                self.cp("pool", xv[:, :, 0:3], hal, [halb], xbuf.bufs)
                xc = self.alloc(1)
                xcv = xc.f()[:, 0:n].rearrange("p (s t) -> p s t", s=nseg)
                self.ts("dve", xcv, xv[:, :, 3:3 + L], self.par("bcw", j * 4 + 3), self.par("bcb", j), ALU.mult, ALU.add,
                        xbuf.bufs + [pbuf], xc.bufs)
                for k in range(3):
                    self.stt(xcv, xv[:, :, k:k + L], self.par("bcw", j * 4 + k), xcv, ALU.mult, ALU.add,
                             xbuf.bufs + xc.bufs + [pbuf], xc.bufs)
                self.cp("pool", hal, xv[:, :, L:L + 3], xbuf.bufs, [halb])
                xb16 = self.alloc(1)
                self.cp("pool", xb16.b()[:, 0:n], xc.f()[:, 0:n], xc.bufs, xb16.bufs)
                xcs.append(xc)
                xcb.append(xb16)
            self.free(*xbufs)
            return xcs, xcb

        def stage_b(g, st):
            xcs, xcb = st
            J = [2 * g, 2 * g + 1]
            wa, wab = self.wl("b_wa", 2, 0, 256, k0=g * 2)
            wxg, wxgb = self.wl("b_wx", 2, 0, 256, k0=g * 2)
            rs, igs, as_ = [], [], []
            for m in range(2):
                j = J[m]
                r, ig = self.alloc(1), self.alloc(1)
                for (wm, wmb, dst, bias) in ((wa, wab, r, "bba"), (wxg, wxgb, ig, "bbx")):
                    ps, pb = self.bank(self.nb_next())
                    for jj in range(2):
                        self.mm(ps[:, 0:n], wm[:, jj, m * 128:(m + 1) * 128], xcb[jj].b()[:, 0:n],
                                xcb[jj].bufs + [wmb], [pb], start=(jj == 0), stop=(jj == 1))
                    self.act(dst.f()[:, 0:n], ps[:, 0:n], AF.Sigmoid, [pb, pbuf], dst.bufs, bias=self.par(bias, j))
                rs.append(r)
                igs.append(ig)
            for m in range(2):
                j, r = J[m], rs[m]
                a = self.alloc(1)
                self.act(a.f()[:, 0:n], r.f()[:, 0:n], AF.Exp, r.bufs + [self.cneg.buf], a.bufs, scale=self.cneg.t[:, j:j + 1])
                self.act(r.f()[:, 0:n], r.f()[:, 0:n], AF.Exp, r.bufs + [self.cneg.buf], r.bufs, scale=self.cneg.t[:, 16 + j:17 + j])
                self.act(r.f()[:, 0:n], r.f()[:, 0:n], AF.Ln, r.bufs + [self.oneb.buf], r.bufs, scale=-1.0, bias=self.oneb.t[:, 0:1])
                self.act(r.f()[:, 0:n], r.f()[:, 0:n], AF.Exp, r.bufs, r.bufs, scale=0.5)
                if T.kind == "meta":
                    self.memset("dve", r.f()[:, 0:1], 1.0, r.bufs)
                as_.append(a)
            for m in range(2):
                j, r, ig, a = J[m], rs[m], igs[m], as_[m]
                bt = self.alloc(1)
                self.tt("dve", bt.f()[:, 0:n], r.f()[:, 0:n], ig.f()[:, 0:n], ALU.mult, r.bufs + ig.bufs, bt.bufs)
                self.tt("dve", bt.f()[:, 0:n], bt.f()[:, 0:n], xcs[m].f()[:, 0:n], ALU.mult, bt.bufs + xcs[m].bufs, bt.bufs)
                hh = ig
                for si, (c0, Ls, sq) in enumerate(T.seqs):
                    if sample:
                        st_ = self.hstS.t[:, sq * 16 + j:sq * 16 + j + 1]
                        stb = self.hstS.buf
                    else:
                        st_ = self.hst.t[:, j:j + 1]
                        stb = self.hst.buf
                    self.scan(hh.f()[:, c0:c0 + Ls], a.f()[:, c0:c0 + Ls], bt.f()[:, c0:c0 + Ls], st_,
                              a.bufs + bt.bufs + [stb], hh.bufs)
                    self.cp("pool", st_, hh.f()[:, c0 + Ls - 1:c0 + Ls], hh.bufs, [stb])
                self.free(bt, r, xcs[m], xcb[m])
            wg_, wgb = self.wl(Wd, 8, 2048 + g * 256, 256)
            for m in range(2):
                j, hh, a = J[m], igs[m], as_[m]
                ps, pb = self.proj8(wg_, wgb, m * 128, u, n)
                self.act(a.f()[:, 0:n], ps[:, 0:n], AF.Silu, [pb], a.bufs)
                self.tt("dve", hs.b()[:, j * 512:j * 512 + n], hh.f()[:, 0:n], a.f()[:, 0:n], ALU.mult, hh.bufs + a.bufs, hs.bufs)
                self.free(hh, a)

        st_prev = stage_a(0)
        for g in range(8):
            st_next = stage_a(g + 1) if g + 1 < 8 else None
            stage_b(g, st_prev)
            st_prev = st_next
        self.free(u)
        self.outproj(1, hs, "b_w_out", n)
        self.free(hs)

    def layer3(self, T):
        n = T.n
        Wd = "d_w_in"
        pbuf = self.params.buf
        u = self.prenorm(3, n)
        nseg = len(T.seqs)
        L = T.seqs[0][1]
        sample = T.kind == "sample"
        HW = 30 + L
        c32 = self.alloc(16)
        sum_ps, sumb = self.bank(2)
        sq_ps, sqb = self.bank(3)
        wts = {}

        def stage_a(j):
            g, m = j // 2, j % 2
            if m == 0:
                wts[g] = (self.wl(Wd, 8, g * 256, 256), self.wl(Wd, 8, 2048 + g * 256, 256))
            (wa, wab), (wb_, wbb) = wts[g]
            sg = self.alloc(1)
            ps, pb = self.proj8(wb_, wbb, m * 128, u, n)
            self.act(sg.f()[:, 0:n], ps[:, 0:n], AF.Sigmoid, [pb], sg.bufs)
            vbuf = self.alloc(2)
            vv = vbuf.f()[:, 0:nseg * HW].rearrange("p (s t) -> p s t", s=nseg)
            ps, pb = self.proj8(wa, wab, m * 128, u, n)
            self.tt("dve", vv[:, :, 30:30 + L], ps[:, 0:n].rearrange("p (s t) -> p s t", s=nseg),
                    sg.f()[:, 0:n].rearrange("p (s t) -> p s t", s=nseg), ALU.mult, [pb] + sg.bufs, vbuf.bufs)
            if sample:
                hal = self.hal3S.t[:, :].rearrange("p (s j k) -> p s j k", s=nseg, j=16)[:, :, j, :]
                halb = self.hal3S.buf
            else:
                hal = self.hal3.t[:, j * 30:(j + 1) * 30].unsqueeze(1)
                halb = self.hal3.buf
            self.cp("pool", vv[:, :, 0:30], hal, [halb], vbuf.bufs)
            vbf = self.alloc(1)
            self.cp("pool", vbf.b()[:, 0:nseg * HW], vbuf.f()[:, 0:nseg * HW], vbuf.bufs, vbf.bufs)
            self.cp("pool", hal, vv[:, :, L:L + 30], vbuf.bufs, [halb])
            D = self.alloc(4)
            dv = D.b()[:, 0:31 * 128].rearrange("p (k c) -> p k c", k=31)
            self.tt("dve", dv, self.identbf.t[:, :].unsqueeze(1).broadcast_to([128, 31, 128]),
                    self.par("dcw", j * 31, 31).unsqueeze(2).broadcast_to([128, 31, 128]), ALU.mult,
                    [self.identbf.buf, pbuf], D.bufs)
            self.free(sg, vbuf)
            return (vbf, D, dv)

        def stage_b(j, st):
            vbf, D, dv = st
            cps, cpb = self.bank(4 + (j % 2))
            vb3 = vbf.b()[:, 0:nseg * HW].rearrange("p (s t) -> p s t", s=nseg)
            for si in range(nseg):
                for k in range(31):
                    self.mm(cps[:, si * L:(si + 1) * L], dv[:, k, :], vb3[:, si, k:k + L], D.bufs + vbf.bufs, [cpb],
                            start=(k == 0), stop=(k == 30))
            cj = c32.f()[:, j * 512:j * 512 + n]
            self.ts("dve", cj, cps[:, 0:n], self.par("dcb", j), None, ALU.add, None, [cpb, pbuf], [c32.bufs[j]])
            cb = self.alloc(1)
            self.cp("pool", cb.b()[:, 0:n], cj, [c32.bufs[j]], cb.bufs)
            self.tt("pool", cb.b()[:, 512:512 + n], cj, cj, ALU.mult, [c32.bufs[j]], cb.bufs)
            self.mm(sum_ps[:, 0:n], self.onesbf.t[:, :], cb.b()[:, 0:n], cb.bufs + [self.onesbf.buf], [sumb],
                    start=(j == 0), stop=(j == 15))
            self.mm(sq_ps[:, 0:n], self.onesbf.t[:, :], cb.b()[:, 512:512 + n], cb.bufs + [self.onesbf.buf], [sqb],
                    start=(j == 0), stop=(j == 15))
            self.free(vbf, D, cb)

        st_prev = stage_a(0)
        for j in range(16):
            st_next = stage_a(j + 1) if j + 1 < 16 else None
            stage_b(j, st_prev)
            st_prev = st_next
        mean, rstd, nmr = (self.alloc(1) for _ in range(3))
        self.act(mean.f()[:, 0:n], sum_ps[:, 0:n], AF.Copy, [sumb], mean.bufs, scale=1.0 / 2048)
        self.tt("dve", nmr.f()[:, 0:n], mean.f()[:, 0:n], mean.f()[:, 0:n], ALU.mult, mean.bufs, nmr.bufs)
        self.stt(rstd.f()[:, 0:n], sq_ps[:, 0:n], 1.0 / 2048, nmr.f()[:, 0:n], ALU.mult, ALU.subtract, [sqb] + nmr.bufs, rstd.bufs)
        self.act(rstd.f()[:, 0:n], rstd.f()[:, 0:n], AF.Ln, rstd.bufs + [self.epsb.buf], rstd.bufs, bias=self.epsb.t[:, 0:1])
        self.act(rstd.f()[:, 0:n], rstd.f()[:, 0:n], AF.Exp, rstd.bufs, rstd.bufs, scale=-0.5)
        self.stt(nmr.f()[:, 0:n], mean.f()[:, 0:n], -1.0, rstd.f()[:, 0:n], ALU.mult, ALU.mult, mean.bufs + rstd.bufs, nmr.bufs)
        hs = self.alloc(8)
        for g in range(8):
            wg, wgb = self.wl(Wd, 8, 4096 + g * 256, 256)
            for m in range(2):
                j = g * 2 + m
                cj = c32.f()[:, j * 512:j * 512 + n]
                t = self.alloc(1)
                sg = self.alloc(1)
                self.tt("dve", t.f()[:, 0:n], cj, rstd.f()[:, 0:n], ALU.mult, [c32.bufs[j]] + rstd.bufs, t.bufs)
                self.tt("dve", t.f()[:, 0:n], t.f()[:, 0:n], nmr.f()[:, 0:n], ALU.add, t.bufs + nmr.bufs, t.bufs)
                self.act(t.f()[:, 0:n], t.f()[:, 0:n], AF.Silu, t.bufs + [pbuf], t.bufs, scale=self.par("dlg", j), bias=self.par("dlb", j))
                ps, pb = self.proj8(wg, wgb, m * 128, u, n)
                self.act(sg.f()[:, 0:n], ps[:, 0:n], AF.Silu, [pb], sg.bufs)
                self.tt("dve", hs.b()[:, j * 512:j * 512 + n], t.f()[:, 0:n], sg.f()[:, 0:n], ALU.mult, t.bufs + sg.bufs, hs.bufs)
                self.free(t, sg)
        self.free(mean, rstd, nmr, c32, u)
        self.outproj(3, hs, "d_w_out", n)
        self.free(hs)


    def layer2(self, T):
        n = T.n
        Wd = "c_w_in"
        pbuf = self.params.buf
        NKM = self.NKMAX
        KC = self.KC
        sample = T.kind == "sample"
        u = self.prenorm(2, n)
        RP = self.ROPE
        if T.kind == "meta":
            src = self.I["ropeP"][:, :, 0:16]
        elif T.kind == "frame":
            src = self.I["ropeP"][:, :, 16 + 512 * T.tidx:16 + 512 * (T.tidx + 1)]
        else:
            src = self.I["ropeS"][:, :, :]
        rpv = RP.t[:, :].rearrange("p (a t) -> p a t", a=2)
        self.dma("pool", rpv[:, :, 0:n], src, [], [RP.buf], RP.buf)
        rq = self.alloc(2)
        rqv = rq.f()[0:64, :].rearrange("p (a t) -> p a t", a=2)
        self.ts("dve", rqv[:, :, 0:n], rpv[:, :, 0:n], C_SCALE, None, ALU.mult, None, [RP.buf], rq.bufs)

        def rms_chunks(w, wb, nch, gname, inv_d, dst_f32, dst_bufs, dst_bf):
            raw = self.alloc(nch)
            sq = self.alloc((nch + 1) // 2)
            for c in range(nch):
                ps, pb = self.proj8(w, wb, c * 128, u, n)
                ra = raw.f()[:, c * 512:c * 512 + n]
                self.act(ra, ps[:, 0:n], AF.Copy, [pb], [raw.bufs[c]])
                self.tt("pool", sq.b()[:, c * 512:c * 512 + n], ra, ra, ALU.mult, [raw.bufs[c]], sq.bufs)
            ps, pb = self.bank(self.nb_next())
            for c in range(nch):
                self.mm(ps[:, 0:n], self.onesbf.t[:, :], sq.b()[:, c * 512:c * 512 + n], sq.bufs + [self.onesbf.buf], [pb],
                        start=(c == 0), stop=(c == nch - 1))
            r = self.rstd_from(ps[:, 0:n], pb, n, inv_d)
            for c in range(nch):
                ra = raw.f()[:, c * 512:c * 512 + n]
                if dst_f32 is not None:
                    self.stt(dst_f32(c), ra, self.par(gname, c), r.f()[:, 0:n], ALU.mult, ALU.mult,
                             [raw.bufs[c], pbuf] + r.bufs, dst_bufs)
                    self.cp("pool", dst_bf(c), dst_f32(c), dst_bufs, dst_bf.bufs)
                else:
                    self.stt(dst_bf(c), ra, self.par(gname, c), r.f()[:, 0:n], ALU.mult, ALU.mult,
                             [raw.bufs[c], pbuf] + r.bufs, dst_bf.bufs)
            self.free(raw, sq, r)

        qn = self.alloc(2)
        w, wb = self.wl(Wd, 8, 0, 512)
        dq = lambda c: qn.b()[:, c * 512:c * 512 + n]
        dq.bufs = qn.bufs
        rms_chunks(w, wb, 4, "cqn", 1.0 / 512, None, None, dq)
        CKV = self.CKV
        ckb = self.alloc(1)
        w, wb = self.wl(Wd, 8, 512, 256)
        df = lambda c: CKV.t[:, c * 512:c * 512 + n]
        db = lambda c: ckb.b()[:, c * 512:c * 512 + n]
        db.bufs = ckb.bufs
        rms_chunks(w, wb, 2, "ckvn", 1.0 / 256, df, [CKV.buf], db)
        KPE = self.KPE
        w, wb = self.wl(Wd, 8, 768, 64)
        ps, pb = self.proj8(w, wb, 0, u, n, M=64)
        kp = self.alloc(1)
        kpb = self.alloc(1)
        self.act(kp.f()[0:64, 0:n], ps[0:64, 0:n], AF.Copy, [pb], kp.bufs)
        self.cp("pool", kpb.b()[0:64, 0:n], kp.f()[0:64, 0:n], kp.bufs, kpb.bufs)
        ps2, pb2 = self.bank(self.nb_next())
        self.mm(ps2[0:64, 0:n], self.swapbf.t[:, :], kpb.b()[0:64, 0:n], kpb.bufs + [self.swapbf.buf], [pb2])
        self.tt("dve", kp.f()[0:64, 0:n], kp.f()[0:64, 0:n], rpv[:, 0, 0:n], ALU.mult, kp.bufs + [RP.buf], kp.bufs)
        self.tt("dve", KPE.t[:, 0:n], ps2[0:64, 0:n], rpv[:, 1, 0:n], ALU.mult, [pb2, RP.buf], [KPE.buf])
        self.tt("dve", KPE.t[:, 0:n], KPE.t[:, 0:n], kp.f()[0:64, 0:n], ALU.add, [KPE.buf] + kp.bufs, [KPE.buf])
        self.cp("pool", kpb.b()[0:64, 0:n], KPE.t[:, 0:n], [KPE.buf], kpb.bufs)
        self.free(kp)
        ckv3 = CKV.t[:, :].rearrange("p (c t) -> p c t", c=2)
        if T.kind == "meta":
            for s in range(self.NPS):
                self.dma("pool", self.O["o_lat_p"][s, :, :, 0:16], ckv3[:, :, 0:16], [CKV.buf], [], CKV.buf)
                self.dma("pool", self.O["o_rope_p"][s, :, 0:16], KPE.t[:, 0:16], [KPE.buf], [], KPE.buf)
            k0 = 0
        elif T.kind == "frame":
            k0 = 16 + 512 * T.tidx
            self.dma("pool", self.O["o_lat_p"][T.seq, :, :, k0:k0 + 512], ckv3, [CKV.buf], [], CKV.buf)
            self.dma("pool", self.O["o_rope_p"][T.seq, :, k0:k0 + 512], KPE.t[:, :], [KPE.buf], [], KPE.buf)
        else:
            self.dma("pool", self.O["o_lat_s"][:, :, :], ckv3[:, :, 0:n], [CKV.buf], [], CKV.buf)
            self.dma("pool", self.O["o_rope_s"][:, :], KPE.t[:, 0:n], [KPE.buf], [], KPE.buf)
        lat = lambda c, a, b: KC.t[:, c * NKM + a:c * NKM + b]
        kpe = lambda a, b: self.KPT.t[:, a:b]
        if not sample:
            for c in range(2):
                self.cp("pool", lat(c, k0, k0 + n), ckb.b()[:, c * 512:c * 512 + n], ckb.bufs, [KC.buf])
            self.cp("pool", kpe(k0, k0 + n), kpb.b()[0:64, 0:n], kpb.bufs, [KC.buf])
        wuk = self.alloc(4)
        wuv = self.alloc(4)
        wukv = wuk.b()[:, 0:4096].rearrange("p (c m) -> p c m", c=2)
        wuvv = wuv.b()[:, 0:4096].rearrange("p (c m) -> p c m", c=2)
        self.dma("sp", wukv, self.W["c_w_uk"].rearrange("(c p) m -> p c m", p=128), [self.Wbuf["c_w_uk"]], wuk.bufs, wuk.bufs[0])
        self.dma("sp", wuvv, self.W["c_w_uv"].rearrange("(c p) m -> p c m", p=128), [self.Wbuf["c_w_uv"]], wuv.bufs, wuv.bufs[0])
        hs = self.alloc(8)
        o_ps, opb = self.bank(6)
        d_ps, dpb = self.bank(7)
        for (c0, L, sq) in (T.seqs if not sample else []):
            if T.kind == "meta":
                kts = [(0, 16, None)]
            elif T.kind == "frame":
                kts = [(0, 16, None)]
                for i in range(4 * T.tidx + 4):
                    kts.append((16 + 128 * i, 128, (i - 4 * T.tidx) if i >= 4 * T.tidx else None))
            else:
                NKC = self.NKC
                self.dma("pool", KC.t[:, 0:2 * NKM].rearrange("p (c k) -> p c k", c=2)[:, :, 0:NKC], self.I["c_lat"][sq],
                         [], [KC.buf], KC.buf)
                self.dma("pool", self.KPT.t[:, 0:NKC], self.I["c_rope"][sq], [], [KC.buf], KC.buf)
                for c in range(2):
                    self.cp("pool", lat(c, NKC, NKC + L), ckb.b()[:, c * 512 + c0:c * 512 + c0 + L], ckb.bufs, [KC.buf])
                self.cp("pool", kpe(NKC, NKC + L), kpb.b()[0:64, c0:c0 + L], kpb.bufs, [KC.buf])
                tot = NKC + L
                kts = [(a, min(128, tot - a), None) for a in range(0, tot, 128)]
            for hp in range(8):
                wq, wqb = self.wl("c_w_uq", 4, hp * 384, 384)
                if hp % 2 == 0:
                    wg, wgb = self.wl(Wd, 8, 832 + (hp // 2) * 512, 512)
                for hq in range(2):
                    h = hp * 2 + hq
                    qnb, qpb, qrb, sg = (self.alloc(1) for _ in range(4))
                    qp = self.alloc(2)
                    ps, pb = self.bank(self.nb_next())
                    for k in range(4):
                        self.mm(ps[:, 0:L], wq[:, k, hq * 192:hq * 192 + 128], qn.b()[:, k * 512 + c0:k * 512 + c0 + L],
                                qn.bufs + [wqb], [pb], start=(k == 0), stop=(k == 3))
                    self.act(qnb.b()[:, 0:L], ps[:, 0:L], AF.Copy, [pb], qnb.bufs, scale=C_SCALE)
                    ps, pb = self.bank(self.nb_next())
                    for k in range(4):
                        self.mm(ps[0:64, 0:L], wq[:, k, hq * 192 + 128:hq * 192 + 192], qn.b()[:, k * 512 + c0:k * 512 + c0 + L],
                                qn.bufs + [wqb], [pb], start=(k == 0), stop=(k == 3))
                    self.act(qpb.b()[0:64, 0:L], ps[0:64, 0:L], AF.Copy, [pb], qpb.bufs)
                    self.tt("dve", qp.f()[0:64, 0:L], ps[0:64, 0:L], rqv[:, 0, c0:c0 + L], ALU.mult, [pb] + rq.bufs, qp.bufs)
                    ps2, pb2 = self.bank(self.nb_next())
                    self.mm(ps2[0:64, 0:L], self.swapbf.t[:, :], qpb.b()[0:64, 0:L], qpb.bufs + [self.swapbf.buf], [pb2])
                    self.tt("dve", qp.f()[0:64, 512:512 + L], ps2[0:64, 0:L], rqv[:, 1, c0:c0 + L], ALU.mult, [pb2] + rq.bufs, qp.bufs)
                    self.tt("dve", qrb.b()[0:64, 0:L], qp.f()[0:64, 0:L], qp.f()[0:64, 512:512 + L], ALU.add, qp.bufs, qrb.bufs)
                    gl = (hp % 2) * 256 + hq * 128
                    ps, pb = self.bank(self.nb_next())
                    for k in range(8):
                        self.mm(ps[:, 0:L], wg[:, k, gl:gl + 128], u.b()[:, k * 512 + c0:k * 512 + c0 + L], u.bufs + [wgb], [pb],
                                start=(k == 0), stop=(k == 7))
                    self.act(sg.f()[:, 0:L], ps[:, 0:L], AF.Exp, [pb], sg.bufs, scale=-1.0)
                    self.act(sg.f()[:, 0:L], sg.f()[:, 0:L], AF.Ln, sg.bufs + [self.oneb.buf], sg.bufs, bias=self.oneb.t[:, 0:1])
                    self.act(sg.f()[:, 0:L], sg.f()[:, 0:L], AF.Exp, sg.bufs, sg.bufs, scale=-1.0)
                    self.tt("dve", sg.f()[:, 0:L], ps[:, 0:L], sg.f()[:, 0:L], ALU.mult, [pb] + sg.bufs, sg.bufs)
                    ntile = len(kts)
                    for sg0 in range(0, ntile, 16):
                        grp = kts[sg0:sg0 + 16]
                        KN = self.alloc(2)
                        VV = self.alloc(2)
                        gbase = grp[0][0]
                        gend = grp[-1][0] + grp[-1][1]
                        for qi, g0 in enumerate(range(gbase, gend, 512)):
                            gw = min(512, gend - g0)
                            kn_ps, knb_ = self.bank(2 + (qi % 2))
                            for c in range(2):
                                self.mm(kn_ps[:, 0:gw], wukv[:, c, h * 128:(h + 1) * 128], lat(c, g0, g0 + gw), wuk.bufs + [KC.buf], [knb_],
                                        start=(c == 0), stop=(c == 1))
                            self.cp("act" if qi % 2 == 0 else "dve", KN.b()[:, g0 - gbase:g0 - gbase + gw], kn_ps[:, 0:gw], [knb_], KN.bufs)
                        for qi in range(0, len(grp), 4):
                            sub = grp[qi:qi + 4]
                            v_ps, vpb = self.bank(2 + ((qi // 4) % 2))
                            for j, (a, kw, mi) in enumerate(sub):
                                for c in range(2):
                                    self.mm(v_ps[0:kw, j * 128:(j + 1) * 128], lat(c, a, a + kw), wuvv[:, c, h * 128:(h + 1) * 128],
                                            wuv.bufs + [KC.buf], [vpb], start=(c == 0), stop=(c == 1))
                            w_ = len(sub) * 128
                            self.cp("dve" if (qi // 4) % 2 == 0 else "act", VV.b()[:, qi * 128:qi * 128 + w_], v_ps[:, 0:w_], [vpb], VV.bufs)

                        def stage_a(li):
                            a, kw, mi = grp[li]
                            s_ps, spb = self.bank(4 + (li % 2))
                            self.mm(s_ps[0:kw, 0:L], KN.b()[:, a - gbase:a - gbase + kw], qnb.b()[:, 0:L], KN.bufs + qnb.bufs, [spb],
                                    start=True, stop=False)
                            self.mm(s_ps[0:kw, 0:L], kpe(a, a + kw), qrb.b()[0:64, 0:L], [KC.buf] + qrb.bufs, [spb],
                                    start=False, stop=(mi is None))
                            if mi is not None:
                                self.mm(s_ps[0:kw, 0:L], self.identbf.t[0:kw, 0:kw], self.negm.t[0:kw, mi * 512:mi * 512 + L],
                                        [self.identbf.buf, self.negm.buf], [spb], start=False, stop=True)
                            P = self.alloc(1)
                            self.act(P.b()[0:kw, 0:L], s_ps[0:kw, 0:L], AF.Exp, [spb], P.bufs)
                            return P

                        def stage_b(li, P):
                            a, kw, mi = grp[li]
                            gi = sg0 + li
                            first, last = gi == 0, gi == ntile - 1
                            self.mm(o_ps[:, 0:L], VV.b()[0:kw, li * 128:(li + 1) * 128], P.b()[0:kw, 0:L], VV.bufs + P.bufs, [opb],
                                    start=first, stop=last)
                            self.mm(d_ps[:, 0:L], self.onesbf.t[0:kw, :], P.b()[0:kw, 0:L], P.bufs + [self.onesbf.buf], [dpb],
                                    start=first, stop=last)
                            self.free(P)
                        Pprev = stage_a(0)
                        for li in range(len(grp)):
                            Pn = stage_a(li + 1) if li + 1 < len(grp) else None
                            stage_b(li, Pprev)
                            Pprev = Pn
                        self.free(KN, VV)
                    rd = self.alloc(1)
                    self.act(rd.f()[:, 0:L], d_ps[:, 0:L], AF.Ln, [dpb], rd.bufs)
                    self.act(rd.f()[:, 0:L], rd.f()[:, 0:L], AF.Exp, rd.bufs, rd.bufs, scale=-1.0)
                    self.tt("dve", rd.f()[:, 0:L], o_ps[:, 0:L], rd.f()[:, 0:L], ALU.mult, [opb] + rd.bufs, rd.bufs)
                    self.tt("dve", hs.b()[:, h * 512 + c0:h * 512 + c0 + L], rd.f()[:, 0:L], sg.f()[:, 0:L], ALU.mult, rd.bufs + sg.bufs, hs.bufs)
                    self.free(qnb, qp, qpb, qrb, sg, rd)
        if sample:
            self.l2_sample(T, u, qn, rq, rqv, ckb, kpb, wuk, wukv, wuv, wuvv, hs, Wd, lat, kpe)
            self.free(u, rq, qn, ckb, kpb, wuv)
        else:
            self.free(u, rq, qn, ckb, kpb, wuk, wuv)
        self.outproj(2, hs, "c_w_out", n)
        self.free(hs)


    def l2_sample(self, T, u, qn, rq, rqv, ckb, kpb, wuk, wukv, wuv, wuvv, hs, Wd, lat, kpe):
        n, NS, NKC, NKM, KC = T.n, self.NS, self.NKC, self.NKMAX, self.KC
        WT = self.alloc(4)
        for h in range(16):
            tp, tpb = self.bank(2 + (h % 2))
            tpv = tp[:, :].bitcast(BF16)
            for c in range(2):
                self.tr(tpv[:, c * 128:(c + 1) * 128], wukv[:, c, h * 128:(h + 1) * 128], wuk.bufs, [tpb])
            self.cp("act" if h % 2 == 0 else "dve", WT.b()[:, h * 256:(h + 1) * 256], tpv[:, 0:256], [tpb], WT.bufs)
        self.free(wuk)
        QA = [self.alloc(4), self.alloc(4)]
        QR = self.alloc(4)
        qav = [q.b()[:, 0:NS * 1024].rearrange("p (s h q) -> p s h q", s=NS, h=16) for q in QA]
        qrv = QR.b()[0:64, 0:NS * 1024].rearrange("p (s h q) -> p s h q", s=NS, h=16)
        for hp in range(8):
            wq, wqb = self.wl("c_w_uq", 4, hp * 384, 384)
            for hq in range(2):
                h = hp * 2 + hq
                qnb, qpb = self.alloc(1), self.alloc(1)
                qp = self.alloc(2)
                ps, pb = self.bank(self.nb_next())
                for k in range(4):
                    self.mm(ps[:, 0:n], wq[:, k, hq * 192:hq * 192 + 128], qn.b()[:, k * 512:k * 512 + n],
                            qn.bufs + [wqb], [pb], start=(k == 0), stop=(k == 3))
                self.act(qnb.b()[:, 0:n], ps[:, 0:n], AF.Copy, [pb], qnb.bufs, scale=C_SCALE)
                for c in range(2):
                    ps, pb = self.bank(self.nb_next())
                    self.mm(ps[:, 0:n], WT.b()[:, h * 256 + c * 128:h * 256 + (c + 1) * 128], qnb.b()[:, 0:n], WT.bufs + qnb.bufs, [pb])
                    self.cp("act" if c == 0 else "dve", qav[c][:, :, h, :], ps[:, 0:n].rearrange("p (s q) -> p s q", s=NS), [pb], QA[c].bufs)
                ps, pb = self.bank(self.nb_next())
                for k in range(4):
                    self.mm(ps[0:64, 0:n], wq[:, k, hq * 192 + 128:hq * 192 + 192], qn.b()[:, k * 512:k * 512 + n],
                            qn.bufs + [wqb], [pb], start=(k == 0), stop=(k == 3))
                self.act(qp.f()[0:64, 0:n], ps[0:64, 0:n], AF.Copy, [pb], qp.bufs)
                self.cp("pool", qpb.b()[0:64, 0:n], qp.f()[0:64, 0:n], qp.bufs, qpb.bufs)
                ps2, pb2 = self.bank(self.nb_next())
                self.mm(ps2[0:64, 0:n], self.swapbf.t[:, :], qpb.b()[0:64, 0:n], qpb.bufs + [self.swapbf.buf], [pb2])
                self.tt("dve", qp.f()[0:64, 0:n], qp.f()[0:64, 0:n], rqv[:, 0, 0:n], ALU.mult, qp.bufs + rq.bufs, qp.bufs)
                self.tt("dve", qp.f()[0:64, 512:512 + n], ps2[0:64, 0:n], rqv[:, 1, 0:n], ALU.mult, [pb2] + rq.bufs, qp.bufs)
                self.tt("dve", qrv[:, :, h, :], qp.f()[0:64, 0:n].rearrange("p (s q) -> p s q", s=NS),
                        qp.f()[0:64, 512:512 + n].rearrange("p (s q) -> p s q", s=NS), ALU.add, qp.bufs, QR.bufs)
                self.free(qnb, qpb, qp)
        self.free(WT)
        ob = [self.bank(0), self.bank(1)]
        d_ps, dpb = self.bank(2)
        for s_ in range(NS):
            c0 = 64 * s_
            self.dma("pool", KC.t[:, 0:2 * NKM].rearrange("p (c k) -> p c k", c=2)[:, :, 0:NKC], self.I["c_lat"][s_],
                     [], [KC.buf], KC.buf)
            self.dma("pool", self.KPT.t[:, 0:NKC], self.I["c_rope"][s_], [], [KC.buf], KC.buf)
            for c in range(2):
                self.cp("pool", lat(c, NKC, NKC + 64), ckb.b()[:, c * 512 + c0:c * 512 + c0 + 64], ckb.bufs, [KC.buf])
            self.cp("pool", kpe(NKC, NKC + 64), kpb.b()[0:64, c0:c0 + 64], kpb.bufs, [KC.buf])
            tot = NKC + 64
            kts = [(a, min(128, tot - a)) for a in range(0, tot, 128)]
            for half in range(2):
                hsl = slice(half * 8, half * 8 + 8)
                qa = [qav[c][:, s_, hsl, :] for c in range(2)]
                qr = qrv[:, s_, hsl, :]

                def stage_a(ki):
                    a, kw = kts[ki]
                    s_ps, spb = self.bank(4 + (ki % 2))
                    for c in range(2):
                        self.mm(s_ps[0:kw, :], lat(c, a, a + kw), qa[c], [KC.buf] + QA[c].bufs, [spb], start=(c == 0), stop=False)
                    self.mm(s_ps[0:kw, :], kpe(a, a + kw), qr, [KC.buf] + QR.bufs, [spb], start=False, stop=True)
                    P = self.alloc(1)
                    self.act(P.b()[0:kw, 0:512], s_ps[0:kw, :], AF.Exp, [spb], P.bufs)
                    tp, tpb = self.bank(3)
                    tpv = tp[:, :].bitcast(BF16)
                    for c in range(2):
                        self.tr(tpv[0:kw, c * 128:(c + 1) * 128], lat(c, a, a + kw), [KC.buf], [tpb])
                    LT = self.alloc(1)
                    self.cp("dve", LT.b()[0:kw, 0:256], tpv[0:kw, 0:256], [tpb], LT.bufs)
                    return P, LT

                def stage_b(ki, st):
                    P, LT = st
                    a, kw = kts[ki]
                    first, last = ki == 0, ki == len(kts) - 1
                    for c in range(2):
                        self.mm(ob[c][0][:, :], LT.b()[0:kw, c * 128:(c + 1) * 128], P.b()[0:kw, 0:512], LT.bufs + P.bufs, [ob[c][1]],
                                start=first, stop=last)
                    self.mm(d_ps[:, :], self.onesbf.t[0:kw, :], P.b()[0:kw, 0:512], P.bufs + [self.onesbf.buf], [dpb], start=first, stop=last)
                    self.free(P, LT)
                st = stage_a(0)
                for ki in range(len(kts)):
                    nx = stage_a(ki + 1) if ki + 1 < len(kts) else None
                    stage_b(ki, st)
                    st = nx
                rd = self.alloc(1)
                self.act(rd.f()[:, :], d_ps[:, :], AF.Ln, [dpb], rd.bufs)
                self.act(rd.f()[:, :], rd.f()[:, :], AF.Exp, rd.bufs, rd.bufs, scale=-1.0)
                OL = self.alloc(1)
                for c in range(2):
                    self.tt("dve", OL.b()[:, c * 512:(c + 1) * 512], ob[c][0][:, :], rd.f()[:, :], ALU.mult, [ob[c][1]] + rd.bufs, OL.bufs)
                po, pob = self.bank(6)
                for hq in range(8):
                    h = half * 8 + hq
                    for c in range(2):
                        self.mm(po[:, hq * 64:(hq + 1) * 64], wuvv[:, c, h * 128:(h + 1) * 128], OL.b()[:, c * 512 + hq * 64:c * 512 + (hq + 1) * 64],
                                wuv.bufs + OL.bufs, [pob], start=(c == 0), stop=(c == 1))
                hsv = hs.b()[:, half * 8 * 512:(half * 8 + 8) * 512].rearrange("p (h t) -> p h t", h=8)[:, :, c0:c0 + 64]
                self.cp("act", hsv, po[:, :].rearrange("p (h t) -> p h t", h=8), [pob], hs.bufs)
                self.free(rd, OL)
        self.free(QA[0], QA[1], QR)
        for g4 in range(4):
            wg, wgb = self.wl(Wd, 8, 832 + g4 * 512, 512)
            for i in range(4):
                h = g4 * 4 + i
                ps, pb = self.proj8(wg, wgb, i * 128, u, n)
                sg = self.alloc(1)
                self.act(sg.f()[:, 0:n], ps[:, 0:n], AF.Silu, [pb], sg.bufs)
                hv = hs.b()[:, h * 512:h * 512 + n]
                self.tt("dve", hv, hv, sg.f()[:, 0:n], ALU.mult, hs.bufs + sg.bufs, hs.bufs)
                self.free(sg)

    def alloc_states(self):
        NS = self.NS
        self.hst = self.sb("hst", [128, 16])
        self.hal1 = self.sb("hal1", [128, 48])
        self.hal3 = self.sb("hal3", [128, 480])
        self.hstm = self.sb("hstm", [128, 16])
        self.hal1m = self.sb("hal1m", [128, 48])
        self.hal3m = self.sb("hal3m", [128, 480])
        self.hstS = self.sb("hstS", [128, NS * 16])
        self.hal1S = self.sb("hal1S", [128, NS * 48])
        self.hal3S = self.sb("hal3S", [128, NS * 480])
        self.cneg = self.sb("cneg", [128, 32])
        self.oneb = self.sb("oneb", [128, 1])
        self.KC = self.sb("kc", [128, 2 * self.NKMAX], BF16)
        self.KPT = DT(self.st.enter_context(self.nc.sbuf_tensor("sb_kpt", [64, self.NKMAX], BF16)), self.KC.buf)
        self.CKV = self.sb("ckv", [128, 1024])
        self.KPE = self.sb("kpe", [64, 512])
        self.ROPE = self.sb("rope", [64, 1024])

    def prologue_rest(self):
        self.memset("pool", self.oneb.t[:, :], 1.0, [self.oneb.buf])
        for t in (self.hst, self.hal1, self.hal3):
            self.memset("pool", t.t[:, :], 0.0, [t.buf])
        c = self.cneg
        self.act(c.t[:, 0:16], self.par("blam", 0, 16), AF.Exp, [self.params.buf], [c.buf], scale=-1.0)
        self.act(c.t[:, 0:16], c.t[:, 0:16], AF.Ln, [c.buf, self.oneb.buf], [c.buf], bias=self.oneb.t[:, 0:1])
        self.ts("dve", c.t[:, 16:32], c.t[:, 0:16], -16.0, None, ALU.mult, None, [c.buf], [c.buf])
        self.ts("dve", c.t[:, 0:16], c.t[:, 0:16], -8.0, None, ALU.mult, None, [c.buf], [c.buf])

    def save_meta(self):
        for q in range(4):
            self.dma("pool", self.S32m[q], self.S32[q].t[:, :], [self.S32[q].buf], [], self.S32[q].buf)
        for a, b in ((self.hstm, self.hst), (self.hal1m, self.hal1), (self.hal3m, self.hal3)):
            self.cp("pool", a.t[:, :], b.t[:, :], [b.buf], [a.buf])

    def restore_meta(self, s):
        for q in range(4):
            self.dma("pool", self.S32[q].t[:, :], self.S32m[q], [], [self.S32[q].buf], self.S32[q].buf)
            self.cp("act", self.Sbf[q].t[:, :], self.S32[q].t[:, :], [self.S32[q].buf], [self.Sbf[q].buf])
        for a, b in ((self.hstm, self.hst), (self.hal1m, self.hal1), (self.hal3m, self.hal3)):
            self.cp("pool", b.t[:, :], a.t[:, :], [a.buf], [b.buf])

    def store_seq(self, s):
        if 0 in self.layers:
            self.store_S(s, 0)
            self.store_S(s, 1)
        if 1 in self.layers:
            self.dma("pool", self.O["o_h"][:, s, :], self.hst.t[:, :], [self.hst.buf], [], self.hst.buf)
            self.dma("pool", self.O["o_c1"][:, s, :, :], self.hal1.t[:, :].rearrange("p (j k) -> p j k", j=16), [self.hal1.buf], [], self.hal1.buf)
        if 3 in self.layers:
            self.dma("pool", self.O["o_c3"][:, s, :, :], self.hal3.t[:, :].rearrange("p (j k) -> p j k", j=16), [self.hal3.buf], [], self.hal3.buf)

    def load_sample_states(self):
        NS = self.NS
        self.dma("pool", self.hstS.t[:, :].rearrange("p (s j) -> p s j", s=NS), self.I["st_h"][:, :, :], [], [self.hstS.buf], self.hstS.buf)
        self.dma("pool", self.hal1S.t[:, :].rearrange("p (s j k) -> p s j k", s=NS, j=16), self.I["st_c1"][:, :, :, :], [], [self.hal1S.buf], self.hal1S.buf)
        self.dma("pool", self.hal3S.t[:, :].rearrange("p (s j k) -> p s j k", s=NS, j=16), self.I["st_c3"][:, :, :, :], [], [self.hal3S.buf], self.hal3S.buf)

    def store_sample_states(self):
        NS, NPS = self.NS, self.NPS
        if 1 in self.layers:
            self.dma("pool", self.O["o_h"][:, NPS:NPS + NS, :], self.hstS.t[:, :].rearrange("p (s j) -> p s j", s=NS), [self.hstS.buf], [], self.hstS.buf)
            self.dma("pool", self.O["o_c1"][:, NPS:NPS + NS, :, :], self.hal1S.t[:, :].rearrange("p (s j k) -> p s j k", s=NS, j=16), [self.hal1S.buf], [], self.hal1S.buf)
        if 3 in self.layers:
            self.dma("pool", self.O["o_c3"][:, NPS:NPS + NS, :, :], self.hal3S.t[:, :].rearrange("p (s j k) -> p s j k", s=NS, j=16), [self.hal3S.buf], [], self.hal3S.buf)

    def declare_io(self):
        NPS, NT, NS = self.NPS, self.NT, self.NS
        n_s = NS * 64
        I, O, W = {}, {}, {}
        self.W32, self.Wbuf = {}, {}
        I["xp"] = self.dram_in("xp", [NPS, NT, 128, 8, 512])
        I["xm"] = self.dram_in("xm", [128, 8, 16])
        I["xs"] = self.dram_in("xs", [128, 8, n_s])
        I["st_hgrn"] = self.dram_in("st_hgrn", [NS, 128, 16, 128])
        I["st_h"] = self.dram_in("st_h", [128, NS, 16])
        I["st_c1"] = self.dram_in("st_c1", [128, NS, 16, 3])
        I["st_c3"] = self.dram_in("st_c3", [128, NS, 16, 30])
        I["c_lat"] = self.dram_in("c_lat", [NS, 128, 2, self.NKC])
        I["c_rope"] = self.dram_in("c_rope", [NS, 64, self.NKC])
        I["params"] = self.dram_in("params", [128, NPAR])
        I["consts"] = self.dram_in("consts", [128, NCON])
        I["ropeP"] = self.dram_in("ropeP", [64, 2, self.NKP])
        I["ropeS"] = self.dram_in("ropeS", [64, 2, n_s])
        for name, shp in [("a_w_in", [1024, 8192]), ("a_w_out", [2048, 1024]), ("b_w_in", [1024, 4096]),
                          ("b_wa", [8, 256, 256]), ("b_wx", [8, 256, 256]), ("b_w_out", [2048, 1024]),
                          ("c_w_in", [1024, 2880]), ("c_w_uq", [512, 3072]), ("c_w_uk", [256, 2048]),
                          ("c_w_uv", [256, 2048]), ("c_w_out", [2048, 1024]), ("d_w_in", [1024, 6144]),
                          ("d_w_out", [2048, 1024])]:
            if len(shp) == 3:
                shp = [shp[0] * shp[1], shp[2]]
            self.W32[name] = self.dram_in(name, shp)
            W[name] = self.nc.dram_tensor(name + "_bf", list(shp), BF16, kind="Internal").ap()
            self.Wbuf[name] = Buf("W" + name)
        O["yp"] = self.dram_out("yp", [NPS, NT, 128, 8, 512])
        O["ys"] = self.dram_out("ys", [128, 8, n_s])
        O["o_hgrn"] = self.dram_out("o_hgrn", [NPS + NS, 128, 16, 128])
        O["o_h"] = self.dram_out("o_h", [128, NPS + NS, 16])
        O["o_c1"] = self.dram_out("o_c1", [128, NPS + NS, 16, 3])
        O["o_c3"] = self.dram_out("o_c3", [128, NPS + NS, 16, 30])
        O["o_lat_p"] = self.dram_out("o_lat_p", [NPS, 128, 2, self.NKP])
        O["o_lat_s"] = self.dram_out("o_lat_s", [128, 2, n_s])
        O["o_rope_p"] = self.dram_out("o_rope_p", [NPS, 64, self.NKP])
        O["o_rope_s"] = self.dram_out("o_rope_s", [64, n_s])
        self.I, self.O, self.W = I, O, W

    def build(self):
        nc = self.nc
        NPS, NT, NS = self.NPS, self.NT, self.NS
        self.declare_io()
        with contextlib.ExitStack() as st:
            self.st = st
            self.A = st.enter_context(nc.sbuf_tensor("arena", [128, self.NU * 512], F32))
            self.abufs = [Buf("a%d" % i) for i in range(self.NU)]
            self.afree = [True] * self.NU
            self.PS = [st.enter_context(nc.psum_tensor("ps%d" % i, [128, 512], F32)) for i in range(8)]
            self.pbufs = [Buf("ps%d" % i, ps=True) for i in range(8)]
            self._mb = 0
            self.XT = self.sb("xt", [128, 4096])
            self.wslots = [self.sb("w%d" % i, [128, 4096], BF16) for i in range(self.NW)]
            self.params = self.sb("params", [128, NPAR])
            self.identbf = self.sb("identbf", [128, 128], BF16)
            self.swapbf = self.sb("swapbf", [64, 64], BF16)
            self.tri = self.sb("tri", [64, 512])
            self.scanm = self.sb("scanm", [128, 512])
            self.negm = self.sb("negm", [128, 2048], BF16)
            self.onesbf = self.sb("onesbf", [128, 128], BF16)
            self.epsb = self.sb("epsb", [128, 1])
            self.oml = self.sb("oml", [128, 16])
            self.noml = self.sb("noml", [128, 16])
            self.S32 = [self.sb("s32_%d" % i, [128, 512]) for i in range(4)]
            self.Sbf = [self.sb("sbf_%d" % i, [128, 512], BF16) for i in range(4)]
            self.S32m = self.nc.dram_tensor("s32m", [4, 128, 512], F32, kind="Internal").ap()
            self.alloc_states()
            xv = self.XT.t[:, :].rearrange("p (c t) -> p c t", c=8)
            self.dma("pool", xv[:, :, 0:16], self.I["xm"][:, :, :], [], [self.XT.buf], self.XT.buf)
            self.prologue()
            Tm = TileDesc("meta", 16, [(0, 16, None)], [(0, 16, None)])
            self.run_layers(Tm)
            self.save_meta()
            for s in range(NPS):
                self.restore_meta(s)
                for t in range(NT):
                    T = TileDesc("frame", 512, [(128 * i, 128, s) for i in range(4)], [(0, 512, s)], seq=s, tidx=t)
                    self.dma("pool", xv, self.I["xp"][s, t], [], [self.XT.buf], self.XT.buf)
                    self.run_layers(T)
                    self.dma("pool", self.O["yp"][s, t], xv, [self.XT.buf], [], self.XT.buf)
                self.store_seq(s)
            n_s = NS * 64
            Ts = TileDesc("sample", n_s, [(64 * i, 64, i) for i in range(NS)], [(64 * i, 64, i) for i in range(NS)])
            self.dma("pool", xv[:, :, 0:n_s], self.I["xs"][:, :, :], [], [self.XT.buf], self.XT.buf)
            self.load_sample_states()
            self.run_layers(Ts)
            self.dma("pool", self.O["ys"][:, :, :], xv[:, :, 0:n_s], [self.XT.buf], [], self.XT.buf)
            self.store_sample_states()
            with nc.allow_low_precision("bf16 matmul operands, fp32 accumulation"):
                self.S.emit()
        return nc

    def convert_weights(self):
        for name in ["a_w_in", "a_w_out", "b_w_in", "b_wa", "b_wx", "b_w_out", "c_w_in", "c_w_uq", "c_w_uk", "c_w_uv",
                     "c_w_out", "d_w_in", "d_w_out"]:
            src, dst, wb = self.W32[name], self.W[name], self.Wbuf[name]
            rows = src.shape[0]
            for r0 in range(0, rows, 128):
                self.dma("pool", dst[r0:r0 + 128, :], src[r0:r0 + 128, :], [], [wb], wb)

    def run_layers(self, T):
        for l in self.layers:
            getattr(self, "layer%d" % l)(T)

    def prologue(self):
        self.dma("sp", self.params.t[:, :], self.I["params"][:, :], [], [self.params.buf], self.params.buf)
        cs = self.alloc(7)
        self.dma("sp", cs.f()[:, 0:NCON], self.I["consts"][:, :], [], cs.bufs, cs.bufs[0])

        def c(name, w, rows=128):
            return cs.f()[0:rows, CON_OFF[name]:CON_OFF[name] + w]
        self.cp("dve", self.identbf.t[:, :], c("ident", 128), cs.bufs, [self.identbf.buf])
        self.cp("dve", self.swapbf.t[:, :], c("swap", 64, 64), cs.bufs, [self.swapbf.buf])
        self.cp("dve", self.tri.t[:, :], c("tri", 512, 64), cs.bufs, [self.tri.buf])
        self.cp("dve", self.scanm.t[:, :], c("scanm", 512), cs.bufs, [self.scanm.buf])
        self.ts("dve", self.negm.t[:, :], c("cmask", 2048), -1.0, 30000.0, ALU.add, ALU.mult, cs.bufs, [self.negm.buf])
        self.free(cs)
        self.memset("dve", self.onesbf.t[:, :], 1.0, [self.onesbf.buf])
        self.memset("dve", self.epsb.t[:, :], EPS, [self.epsb.buf])
        self.tt("dve", self.oml.t[:, :], self.par("lb1", 0, 16), self.par("lb0", 0, 16), ALU.subtract,
                [self.params.buf], [self.oml.buf])
        self.act(self.oml.t[:, :], self.oml.t[:, :], AF.Sigmoid, [self.oml.buf], [self.oml.buf])
        self.ts("dve", self.noml.t[:, :], self.oml.t[:, :], -1.0, None, ALU.mult, None, [self.oml.buf], [self.noml.buf])
        for q in range(4):
            self.memset("pool", self.S32[q].t[:, :], 0.0, [self.S32[q].buf])
            self.memset("pool", self.Sbf[q].t[:, :], 0.0, [self.Sbf[q].buf])
        self.prologue_rest()
        self.convert_weights()


def core_inputs(inp, cfg, core, shared):
    NPS, NT, NS, PAST = cfg["NPS"], cfg["NT"], cfg["NS"], cfg["PAST"]
    f = np.float32
    d = dict(shared)
    xp = np.asarray(inp["x_prompt"][core * NPS:(core + 1) * NPS], f)
    d["xp"] = np.ascontiguousarray(xp.reshape(NPS, NT, 512, 8, 128).transpose(0, 1, 4, 3, 2))
    xs = np.asarray(inp["x_sample"][core * NS:(core + 1) * NS], f)
    d["xs"] = np.ascontiguousarray(xs.reshape(NS * 64, 8, 128).transpose(2, 1, 0))
    sl = slice(core * NS, (core + 1) * NS)
    d["st_hgrn"] = np.ascontiguousarray(np.asarray(inp["state_hgrn"][0][sl], f).transpose(0, 2, 1, 3))
    d["st_h"] = np.ascontiguousarray(np.asarray(inp["state_rglru_h"][0][sl], f).reshape(NS, 16, 128).transpose(2, 0, 1))
    d["st_c1"] = np.ascontiguousarray(np.asarray(inp["state_rglru_conv"][0][sl], f).reshape(NS, 3, 16, 128).transpose(3, 0, 2, 1))
    d["st_c3"] = np.ascontiguousarray(np.asarray(inp["state_conformer_conv"][0][sl], f).reshape(NS, 30, 16, 128).transpose(3, 0, 2, 1))
    cl = np.asarray(inp["cache_mla_latent"][0][sl], f)
    d["c_lat"] = np.ascontiguousarray(cl.reshape(NS, -1, 2, 128).transpose(0, 3, 2, 1))
    d["c_rope"] = np.ascontiguousarray(np.asarray(inp["cache_mla_rope"][0][sl], f).transpose(0, 2, 1))
    return d


def shared_inputs(inp, cfg):
    NT, NS, PAST = cfg["NT"], cfg["NS"], cfg["PAST"]
    f = np.float32
    d = {}
    d["xm"] = np.ascontiguousarray(np.asarray(inp["meta_tokens"], f).reshape(16, 8, 128).transpose(2, 1, 0))
    d["params"] = pack_params(inp)
    d["consts"] = pack_consts()
    d["ropeP"] = rope_table(np.arange(16 + 512 * NT))
    rs = rope_table(16 + PAST + np.arange(64))
    d["ropeS"] = np.ascontiguousarray(np.tile(rs, (1, 1, NS)))
    for name in ["a_w_in", "a_w_out", "b_w_in", "b_wa", "b_wx", "b_w_out", "c_w_in", "c_w_uq", "c_w_uk", "c_w_uv",
                 "c_w_out", "d_w_in", "d_w_out"]:
        w = np.asarray(inp[name][0], f)
        d[name] = np.ascontiguousarray(w.reshape(-1, w.shape[-1]))
    return d


def assemble(results, cfg):
    NPS, NT, NS = cfg["NPS"], cfg["NT"], cfg["NS"]
    ncore = len(results)
    T = 512 * NT
    cat = lambda xs: np.concatenate(xs, axis=0)
    yp = cat([r["yp"].transpose(0, 1, 4, 3, 2).reshape(NPS, T, 1024) for r in results])
    ys = cat([r["ys"].transpose(2, 1, 0).reshape(NS, 64, 1024) for r in results])
    hg = [r["o_hgrn"].transpose(0, 2, 1, 3) for r in results]
    hgp = cat([h[:NPS] for h in hg])[None]
    hgs = cat([h[NPS:] for h in hg])[None]
    oh = [r["o_h"].transpose(1, 2, 0).reshape(NPS + NS, 2048) for r in results]
    ohp = cat([h[:NPS] for h in oh])[None]
    ohs = cat([h[NPS:] for h in oh])[None]
    c1 = [r["o_c1"].transpose(1, 3, 2, 0).reshape(NPS + NS, 3, 2048) for r in results]
    c1p = cat([h[:NPS] for h in c1])[None]
    c1s = cat([h[NPS:] for h in c1])[None]
    c3 = [r["o_c3"].transpose(1, 3, 2, 0).reshape(NPS + NS, 30, 2048) for r in results]
    c3p = cat([h[:NPS] for h in c3])[None]
    c3s = cat([h[NPS:] for h in c3])[None]
    latp = cat([r["o_lat_p"].transpose(0, 3, 2, 1).reshape(NPS, 16 + T, 256) for r in results])[None]
    lats = cat([r["o_lat_s"].transpose(2, 1, 0).reshape(NS, 64, 256) for r in results])[None]
    ropp = cat([r["o_rope_p"].transpose(0, 2, 1) for r in results])[None]
    rops = cat([r["o_rope_s"].T.reshape(NS, 64, 64) for r in results])[None]
    outs = (yp, ys, hgp, hgs, ohp, ohs, c1p, c1s, latp, lats, ropp, rops, c3p, c3s)
    return tuple(np.ascontiguousarray(o, dtype=np.float32) for o in outs)


def run(inp, cfg, ncore):
    b = Builder(cfg)
    nc = b.build()
    shared = shared_inputs(inp, cfg)
    in_maps = [core_inputs(inp, cfg, c, shared) for c in range(ncore)]
    res = run_bass_kernel_spmd(nc, in_maps, core_ids=list(range(ncore)))
    return assemble(res.results, cfg)


def kernel(**inputs):
    cfg = dict(NPS=4, NT=4, NS=4, PAST=4096)
    return run(inputs, cfg, 8)
```

```python
import contextlib
import numpy as np
import concourse.bass as bass
import concourse.mybir as mybir
from concourse.bass_utils import run_bass_kernel_spmd

F32 = mybir.dt.float32
BF16 = mybir.dt.bfloat16
ALU = mybir.AluOpType
AF = mybir.ActivationFunctionType

EPS = 1e-6
C_SCALE = 192.0 ** -0.5
ENGS = ("pe", "act", "dve", "pool", "sp")


class Buf:
    __slots__ = ("name", "last_w", "readers", "dsem", "dcount", "ps", "last_by")

    def __init__(self, name, ps=False):
        self.name = name
        self.last_w = None
        self.readers = []
        self.dsem = None
        self.dcount = 0
        self.ps = ps
        self.last_by = {}


class Op:
    __slots__ = ("eng", "fn", "waits", "signal", "sigval", "dma_buf", "dma_val")

    def __init__(self, eng, fn):
        self.eng = eng
        self.fn = fn
        self.waits = []
        self.signal = False
        self.sigval = None
        self.dma_buf = None
        self.dma_val = None


class Sched:
    def __init__(self, nc):
        self.nc = nc
        self.ops = {e: [] for e in ENGS}
        self.nops = 0

    def _dep(self, op, tok):
        if tok is None:
            return
        if tok.dma_buf is not None:
            op.waits.append(("dma", tok.dma_buf, tok.dma_val))
            return
        if tok.eng == op.eng and op.eng == "pe":
            return
        tok.signal = True
        op.waits.append(tok)

    def add(self, eng, fn, reads=(), writes=(), dma_buf=None):
        op = Op(eng, fn)
        for b in reads:
            if b.ps:
                continue
            self._dep(op, b.last_w)
        for b in writes:
            if b.ps:
                continue
            self._dep(op, b.last_w)
            for r in b.readers:
                if r.dma_buf is None and r.eng == eng:
                    continue
                self._dep(op, r)
        seen = set()
        for b in list(reads) + list(writes):
            if not b.ps or id(b) in seen:
                continue
            seen.add(id(b))
            for e2, o2 in b.last_by.items():
                if e2 != eng:
                    self._dep(op, o2)
            b.last_by[eng] = op
        if dma_buf is not None:
            dma_buf.dcount += 16
            op.dma_buf = dma_buf
            op.dma_val = dma_buf.dcount
        for b in reads:
            if not b.ps:
                b.readers.append(op)
        for b in writes:
            if not b.ps:
                b.last_w = op
                b.readers = []
        self.ops[eng].append(op)
        self.nops += 1
        return op

    def emit(self):
        nc = self.nc
        with contextlib.ExitStack() as stack:
            esem = {e: stack.enter_context(nc.semaphore("s_" + e)) for e in ENGS if e != "sp"}
            for e in ENGS:
                n = 0
                for op in self.ops[e]:
                    if op.dma_buf is None and op.signal:
                        n += 1
                        op.sigval = n
            dbufs, seen = [], set()
            for e in ENGS:
                for op in self.ops[e]:
                    if op.dma_buf is not None and id(op.dma_buf) not in seen:
                        seen.add(id(op.dma_buf))
                        dbufs.append(op.dma_buf)
            for i, b in enumerate(dbufs):
                b.dsem = stack.enter_context(nc.semaphore("d%d_%s" % (i, b.name)))
            self.n_sems = len(dbufs) + 4
            block = stack.enter_context(nc.Block())
            handles = {"pe": "tensor", "act": "scalar", "dve": "vector", "pool": "gpsimd", "sp": "sync"}

            def make(e):
                def body(eng):
                    known = {}
                    for op in self.ops[e]:
                        need = {}
                        for w in op.waits:
                            if isinstance(w, Op):
                                s, v = esem[w.eng], w.sigval
                            else:
                                s, v = w[1].dsem, w[2]
                            k = id(s)
                            if known.get(k, 0) >= v:
                                continue
                            if k not in need or need[k][1] < v:
                                need[k] = (s, v)
                        for k, (s, v) in need.items():
                            eng.wait_ge(s, v)
                            known[k] = v
                        ins = op.fn(eng)
                        if op.dma_buf is not None:
                            ins.then_inc(op.dma_buf.dsem, 16)
                        elif op.signal:
                            ins.then_inc(esem[e], 1)
                    if e == "sp":
                        for b in dbufs:
                            if b.dcount:
                                eng.wait_ge(b.dsem, b.dcount)
                return body

            for e in ENGS:
                getattr(block, handles[e])(make(e))


PAR_FIELDS = [("gpre", 32), ("gpost", 32), ("lb0", 16), ("lb1", 16), ("ang", 1), ("bcw", 64), ("bcb", 16),
              ("bba", 16), ("bbx", 16), ("blam", 16), ("cqn", 4), ("ckvn", 2), ("dcw", 496), ("dcb", 16),
              ("dlg", 16), ("dlb", 16)]
PAR_OFF = {}
_o = 0
for _n, _w in PAR_FIELDS:
    PAR_OFF[_n] = _o
    _o += _w
NPAR = _o

CON_FIELDS = [("ident", 128), ("swap", 64), ("tri", 512), ("cmask", 2048), ("scanm", 512)]
CON_OFF = {}
_o = 0
for _n, _w in CON_FIELDS:
    CON_OFF[_n] = _o
    _o += _w
NCON = _o


def _pc(v):
    v = np.asarray(v, np.float32)
    lead = v.shape[:-1]
    c = v.shape[-1] // 128
    v = v.reshape(lead + (c, 128))
    return np.moveaxis(v, -1, 0)


def pack_params(inp):
    P = np.zeros((128, NPAR), np.float32)

    def put(name, arr):
        arr = np.ascontiguousarray(arr, np.float32).reshape(128, -1)
        P[:, PAR_OFF[name]:PAR_OFF[name] + arr.shape[1]] = arr

    put("gpre", _pc(inp["norm_pre"]))
    put("gpost", _pc(inp["norm_post"]))
    put("lb0", _pc(inp["a_lb_logits"][0]))
    put("lb1", _pc(inp["a_lb_logits"][1]))
    put("ang", np.asarray(inp["a_norm_g"][0]).reshape(128, 1))
    put("bcw", np.transpose(_pc(inp["b_conv_w"][0]), (0, 2, 1)))
    put("bcb", _pc(inp["b_conv_b"][0]))
    put("bba", _pc(inp["b_ba"][0]))
    put("bbx", _pc(inp["b_bx"][0]))
    put("blam", _pc(inp["b_lambda"][0]))
    put("cqn", _pc(inp["c_q_norm"][0]))
    put("ckvn", _pc(inp["c_kv_norm"][0]))
    put("dcw", np.transpose(_pc(inp["d_conv_w"][0]), (0, 2, 1)))
    put("dcb", _pc(inp["d_conv_b"][0]))
    put("dlg", _pc(inp["d_ln_g"][0]))
    put("dlb", _pc(inp["d_ln_b"][0]))
    return P


def pack_consts():
    C = np.zeros((128, NCON), np.float32)
    C[:, CON_OFF["ident"]:CON_OFF["ident"] + 128] = np.eye(128, dtype=np.float32)
    sw = np.zeros((64, 64), np.float32)
    for i in range(64):
        sw[(i + 32) % 64, i] = 1.0
    C[0:64, CON_OFF["swap"]:CON_OFF["swap"] + 64] = sw
    m = np.arange(64)[:, None]
    l = np.arange(64)[None, :]
    tri = (m <= l).astype(np.float32)
    C[0:64, CON_OFF["tri"]:CON_OFF["tri"] + 512] = np.tile(tri, (1, 8))
    k = np.arange(128)[:, None]
    q = np.arange(512)[None, :]
    for r in range(4):
        mk = ((2 * r + k // 64) <= (q // 64)).astype(np.float32)
        C[:, CON_OFF["cmask"] + r * 512:CON_OFF["cmask"] + (r + 1) * 512] = mk
    sm = np.ones((128, 512), np.float32)
    sm[:, 0::64] = 0.0
    C[:, CON_OFF["scanm"]:CON_OFF["scanm"] + 512] = sm
    return C


def rope_table(pos):
    pos = np.asarray(pos, np.float32)
    inv = (1.0 / (np.float32(10000.0) ** (np.arange(0, 64, 2, dtype=np.float32) / np.float32(64)))).astype(np.float32)
    ang = (pos[:, None] * inv[None, :]).astype(np.float32)
    cos = np.cos(ang).astype(np.float32).T
    sin = np.sin(ang).astype(np.float32).T
    out = np.zeros((64, 2, pos.shape[0]), np.float32)
    out[0:32, 0] = cos
    out[32:64, 0] = cos
    out[0:32, 1] = -sin
    out[32:64, 1] = sin
    return out


class ATile:
    def __init__(self, K, u0, k):
        self.K, self.u0, self.k = K, u0, k
        self.bufs = K.abufs[u0:u0 + k]

    def f(self):
        return self.K.A[:, self.u0 * 512:(self.u0 + self.k) * 512]

    def b(self):
        return self.f().bitcast(BF16)


class DT:
    def __init__(self, t, buf):
        self.t, self.buf = t, buf


class TileDesc:
    def __init__(self, kind, n, segs, seqs, seq=None, tidx=0):
        self.kind, self.n, self.segs, self.seqs, self.seq, self.tidx = kind, n, segs, seqs, seq, tidx


class Builder:
    def __init__(self, cfg):
        self.cfg = cfg
        self.NPS, self.NT, self.NS, self.PAST = cfg["NPS"], cfg["NT"], cfg["NS"], cfg["PAST"]
        self.LS = 64
        self.layers = cfg.get("layers", [0, 1, 2, 3])
        self.NKP = 16 + 512 * self.NT
        self.NKC = 16 + self.PAST
        self.NKS = self.NKC + self.LS
        self.NKMAX = max(self.NKP, self.NKS)
        self.NU = cfg.get("NU", 44)
        self.NW = 4
        self.nc = bass.Bass("TRN2", target_bir_lowering=False)
        self.S = Sched(self.nc)
        self.wi = 0

    def dram_in(self, name, shape):
        return self.nc.dram_tensor(name, list(shape), F32, kind="ExternalInput").ap()

    def dram_out(self, name, shape):
        return self.nc.dram_tensor(name, list(shape), F32, kind="ExternalOutput").ap()

    def sb(self, name, shape, dt=F32):
        t = self.st.enter_context(self.nc.sbuf_tensor("sb_" + name, list(shape), dt))
        return DT(t, Buf(name))

    def alloc(self, k):
        free = self.afree
        for u0 in range(0, self.NU - k + 1):
            if all(free[u0:u0 + k]):
                for i in range(u0, u0 + k):
                    free[i] = False
                return ATile(self, u0, k)
        raise RuntimeError("arena full (need %d units, free %d)" % (k, sum(free)))

    def free(self, *tiles):
        for t in tiles:
            for i in range(t.u0, t.u0 + t.k):
                assert not self.afree[i]
                self.afree[i] = True

    def mm(self, out, lhsT, rhs, R, W, start=True, stop=True):
        self.S.add("pe", lambda e: e.matmul(out, lhsT=lhsT, rhs=rhs, start=start, stop=stop), R, W)

    def tr(self, out, in_, R, W):
        ident = self.identbf.t[:, :]
        self.S.add("pe", lambda e: e.transpose(out, in_, ident), list(R) + [self.identbf.buf], W)

    def act(self, out, in_, func, R, W, scale=1.0, bias=0.0):
        self.S.add("act", lambda e: e.activation(out=out, in_=in_, func=func, bias=bias, scale=scale), R, W)

    def tt(self, eng, out, a, b, op, R, W):
        self.S.add(eng, lambda e: e.tensor_tensor(out=out, in0=a, in1=b, op=op), R, W)

    def ts(self, eng, out, a, s1, s2, op0, op1, R, W):
        if s2 is None:
            self.S.add(eng, lambda e: e.tensor_scalar(out=out, in0=a, scalar1=s1, scalar2=None, op0=op0), R, W)
        else:
            self.S.add(eng, lambda e: e.tensor_scalar(out=out, in0=a, scalar1=s1, scalar2=s2, op0=op0, op1=op1), R, W)

    def stt(self, out, a, s, b, op0, op1, R, W):
        self.S.add("dve", lambda e: e.scalar_tensor_tensor(out=out, in0=a, scalar=s, in1=b, op0=op0, op1=op1), R, W)

    def cp(self, eng, out, in_, R, W):
        if eng == "act":
            self.S.add("act", lambda e: e.activation(out=out, in_=in_, func=AF.Copy), R, W)
        else:
            self.S.add(eng, lambda e: e.tensor_copy(out=out, in_=in_), R, W)

    def scan(self, out, d0, d1, init, R, W):
        self.S.add("dve", lambda e: e.tensor_tensor_scan(out=out, data0=d0, data1=d1, initial=init,
                                                        op0=ALU.mult, op1=ALU.add), R, W)

    def memset(self, eng, ap, val, W):
        self.S.add(eng, lambda e: e.memset(ap, val), [], W)

    def dma(self, q, out, in_, R, W, dbuf):
        self.S.add(q, lambda e: e.dma_start(out=out, in_=in_), R, W, dma_buf=dbuf)

    def par(self, name, i=0, w=1):
        o = PAR_OFF[name] + i
        return self.params.t[:, o:o + w]

    def wl(self, name, kc, c0, ncol, k0=0):
        view = self.W[name].rearrange("(c p) m -> p c m", p=128)[:, k0:k0 + kc, c0:c0 + ncol]
        slot = self.wslots[self.wi % self.NW]
        self.wi += 1
        ap = slot.t[:, 0:kc * ncol].rearrange("p (a b) -> p a b", a=kc)
        self.dma("sp", ap, view, [self.Wbuf[name]], [slot.buf], slot.buf)
        return ap, slot.buf

    def bank(self, i):
        return self.PS[i], self.pbufs[i]

    def rstd_from(self, ps_ap, psbuf, n, inv_d, rows=128):
        r = self.alloc(1)
        ra = r.f()[0:rows, 0:n]
        self.act(ra, ps_ap, AF.Ln, [psbuf, self.epsb.buf], r.bufs, scale=inv_d, bias=self.epsb.t[0:rows, 0:1])
        self.act(ra, ra, AF.Exp, r.bufs, r.bufs, scale=-0.5)
        return r

    def prenorm(self, l, n):
        X = self.XT
        sq = self.alloc(4)
        for c in range(8):
            self.act(sq.b()[:, c * 512:c * 512 + n], X.t[:, c * 512:c * 512 + n], AF.Square, [X.buf], sq.bufs)
        ps, pb = self.bank(self.nb_next())
        for c in range(8):
            self.mm(ps[:, 0:n], self.onesbf.t[:, :], sq.b()[:, c * 512:c * 512 + n], sq.bufs + [self.onesbf.buf], [pb],
                    start=(c == 0), stop=(c == 7))
        r = self.rstd_from(ps[:, 0:n], pb, n, 1.0 / 1024)
        u = sq
        for c in range(8):
            self.stt(u.b()[:, c * 512:c * 512 + n], X.t[:, c * 512:c * 512 + n], self.par("gpre", l * 8 + c),
                     r.f()[:, 0:n], ALU.mult, ALU.mult, [X.buf, self.params.buf] + r.bufs, u.bufs)
        self.free(r)
        return u

    def nb_next(self):
        self._mb = (self._mb + 1) % 2
        return self._mb

    def outproj(self, l, hs, Wd, n):
        X = self.XT
        y = self.alloc(8)
        ysq = self.alloc(4)
        for g in range(4):
            w, wb = self.wl(Wd, 16, g * 256, 256)
            for mm_ in range(2):
                m = g * 2 + mm_
                ps, pb = self.bank(self.nb_next())
                for k in range(16):
                    self.mm(ps[:, 0:n], w[:, k, mm_ * 128:(mm_ + 1) * 128], hs.b()[:, k * 512:k * 512 + n],
                            hs.bufs + [wb], [pb], start=(k == 0), stop=(k == 15))
                ya = y.f()[:, m * 512:m * 512 + n]
                self.act(ya, ps[:, 0:n], AF.Copy, [pb], [y.bufs[m]])
                self.tt("pool", ysq.b()[:, m * 512:m * 512 + n], ya, ya, ALU.mult, [y.bufs[m]], ysq.bufs)
        ps, pb = self.bank(self.nb_next())
        for m in range(8):
            self.mm(ps[:, 0:n], self.onesbf.t[:, :], ysq.b()[:, m * 512:m * 512 + n], ysq.bufs + [self.onesbf.buf],
                    [pb], start=(m == 0), stop=(m == 7))
        r = self.rstd_from(ps[:, 0:n], pb, n, 1.0 / 1024)
        for m in range(8):
            ya = y.f()[:, m * 512:m * 512 + n]
            self.stt(ya, ya, self.par("gpost", l * 8 + m), r.f()[:, 0:n], ALU.mult, ALU.mult,
                     [y.bufs[m], self.params.buf] + r.bufs, [y.bufs[m]])
            xa = X.t[:, m * 512:m * 512 + n]
            self.tt("pool", xa, xa, ya, ALU.add, [X.buf, y.bufs[m]], [X.buf])
        self.free(r, y, ysq)

    def layer0(self, T):
        n = T.n
        bl = min(64, n)
        nb = n // bl
        Wd = "a_w_in"
        u = self.prenorm(0, n)
        hs = self.alloc(8)
        pbuf = self.params.buf
        for half in range(2):
            QT, KT, KP, VV = (self.alloc(4) for _ in range(4))
            EL = self.alloc(1)
            SG = self.alloc(4)
            for pr in range(4):
                h0 = half * 8 + pr * 2
                hhs = [pr * 2, pr * 2 + 1]
                wq, wqb = self.wl(Wd, 8, h0 * 128, 256)
                wg, wgb = self.wl(Wd, 8, 6144 + h0 * 128, 256)
                wf, wfb = self.wl(Wd, 8, 2048 + h0 * 128, 256)
                wi, wib = self.wl(Wd, 8, 4096 + h0 * 128, 256)
                t1 = [self.alloc(1), self.alloc(1)]
                kk = [self.alloc(1), self.alloc(1)]
                fg = [self.alloc(1), self.alloc(1)]
                cum = [self.alloc(1), self.alloc(1)]
                for hp in range(2):
                    ps, pb = self.proj8(wq, wqb, hp * 128, u, n)
                    self.act(t1[hp].f()[:, 0:n], ps[:, 0:n], AF.Silu, [pb], t1[hp].bufs)
                for hp in range(2):
                    ps, pb = self.proj8(wg, wgb, hp * 128, u, n)
                    self.act(SG.b()[:, hhs[hp] * 512:hhs[hp] * 512 + n], ps[:, 0:n], AF.Silu, [pb], SG.bufs)
                for hp in range(2):
                    ps, pb = self.proj8(wf, wfb, hp * 128, u, n)
                    self.act(kk[hp].f()[:, 0:n], ps[:, 0:n], AF.Sigmoid, [pb], kk[hp].bufs, scale=-1.0)
                for hp in range(2):
                    ps, pb = self.proj8(wi, wib, hp * 128, u, n)
                    self.act(VV.b()[:, hhs[hp] * 512:hhs[hp] * 512 + n], ps[:, 0:n], AF.Copy, [pb], VV.bufs)
                for hp in range(2):
                    h = h0 + hp
                    self.ts("dve", fg[hp].f()[:, 0:n], kk[hp].f()[:, 0:n], self.noml.t[:, h:h + 1], 1.0, ALU.mult, ALU.add,
                            kk[hp].bufs + [self.noml.buf], fg[hp].bufs)
                for hp in range(2):
                    self.act(fg[hp].f()[:, 0:n], fg[hp].f()[:, 0:n], AF.Ln, fg[hp].bufs, fg[hp].bufs)
                for hp in range(2):
                    self.scan(cum[hp].f()[:, 0:n], self.scanm.t[:, 0:n], fg[hp].f()[:, 0:n], 0.0,
                              fg[hp].bufs + [self.scanm.buf], cum[hp].bufs)
                for hp in range(2):
                    ee = fg[hp]
                    self.act(ee.f()[:, 0:n], cum[hp].f()[:, 0:n], AF.Exp, cum[hp].bufs, ee.bufs)
                    self.act(cum[hp].f()[:, 0:n], cum[hp].f()[:, 0:n], AF.Exp, cum[hp].bufs, cum[hp].bufs, scale=-1.0)
                for hp in range(2):
                    h = h0 + hp
                    hh = hhs[hp]
                    cs = slice(hh * 512, hh * 512 + n)
                    ee = fg[hp]
                    self.tt("dve", QT.b()[:, cs], t1[hp].f()[:, 0:n], ee.f()[:, 0:n], ALU.mult, t1[hp].bufs + ee.bufs, QT.bufs)
                    self.stt(kk[hp].f()[:, 0:n], kk[hp].f()[:, 0:n], self.oml.t[:, h:h + 1], cum[hp].f()[:, 0:n], ALU.mult, ALU.mult,
                             kk[hp].bufs + cum[hp].bufs + [self.oml.buf], kk[hp].bufs)
                    self.cp("pool", KT.b()[:, cs], kk[hp].f()[:, 0:n], kk[hp].bufs, KT.bufs)
                    elast = ee.f()[:, bl - 1:n:bl]
                    self.tt("dve", KP.b()[:, cs].rearrange("p (b t) -> p b t", t=bl),
                            kk[hp].f()[:, 0:n].rearrange("p (b t) -> p b t", t=bl),
                            elast.unsqueeze(2).broadcast_to([128, nb, bl]), ALU.mult, kk[hp].bufs + ee.bufs, KP.bufs)
                    self.cp("pool", EL.f()[:, hh * 32:hh * 32 + nb], elast, ee.bufs, EL.bufs)
                self.free(*t1, *kk, *fg, *cum)
            for (c0, L, sq) in T.segs:
                nbk = L // bl
                if T.kind == "sample":
                    self.load_S(sq, half)
                att = self.alloc(1)
                for qd in range(2):
                    ps, pb = self.bank(qd)
                    for hq in range(4):
                        hh = qd * 4 + hq
                        for b in range(nbk):
                            cc = hh * 512 + c0 + b * bl
                            oa = (hq * nbk + b) * 64
                            self.mm(ps[0:bl, oa:oa + bl], KT.b()[:, cc:cc + bl], QT.b()[:, cc:cc + bl],
                                    KT.bufs + QT.bufs, [pb])
                    w = 4 * nbk * 64
                    self.tt("dve", att.b()[0:bl, qd * 512:qd * 512 + w], ps[0:bl, 0:w], self.tri.t[0:bl, 0:w], ALU.mult,
                            [pb, self.tri.buf], att.bufs)
                vks = [[self.alloc(1), self.alloc(1)], [self.alloc(1), self.alloc(1)]]
                ops_ = [self.bank(6), self.bank(7)]

                def t_stage(b, qd):
                    tp, tpb = self.bank(2 + qd)
                    tpv = tp[:, :].bitcast(BF16)
                    for hq in range(4):
                        hh = qd * 4 + hq
                        cc = hh * 512 + c0 + b * bl
                        self.tr(tpv[0:bl, (hq * 2) * 128:(hq * 2 + 1) * 128], VV.b()[:, cc:cc + bl], VV.bufs, [tpb])
                        self.tr(tpv[0:bl, (hq * 2 + 1) * 128:(hq * 2 + 2) * 128], KP.b()[:, cc:cc + bl], KP.bufs, [tpb])
                    vk = vks[qd][b % 2]
                    self.cp("act", vk.b()[0:bl, 0:1024], tpv[0:bl, 0:1024], [tpb], vk.bufs)

                def c_stage(b, qd):
                    gq = half * 2 + qd
                    vk = vks[qd][b % 2]
                    op_, opb = ops_[qd]
                    Sb = self.Sbf[gq]
                    for hq in range(4):
                        hh = qd * 4 + hq
                        cc = hh * 512 + c0 + b * bl
                        oc = (hq * nbk + b) * bl
                        oa = (hq * nbk + b) * 64
                        self.mm(op_[:, oc:oc + bl], Sb.t[:, hq * 128:(hq + 1) * 128], QT.b()[:, cc:cc + bl],
                                [Sb.buf] + QT.bufs, [opb], start=True, stop=False)
                        self.mm(op_[:, oc:oc + bl], vk.b()[0:bl, (hq * 2) * 128:(hq * 2 + 1) * 128],
                                att.b()[0:bl, qd * 512 + oa:qd * 512 + oa + bl], vk.bufs + att.bufs, [opb],
                                start=False, stop=True)
                    sp_, spb = self.bank(4 + qd)
                    for hq in range(4):
                        self.mm(sp_[:, hq * 128:(hq + 1) * 128], vk.b()[0:bl, (hq * 2 + 1) * 128:(hq * 2 + 2) * 128],
                                vk.b()[0:bl, (hq * 2) * 128:(hq * 2 + 1) * 128], vk.bufs, [spb])
                    S3 = self.S32[gq]
                    blk = (c0 // bl) + b
                    dec = EL.f()[:, qd * 128:(qd + 1) * 128].rearrange("p (h b) -> p h b", h=4)[:, :, blk:blk + 1]
                    s3v = S3.t[:, :].rearrange("p (h v) -> p h v", h=4)
                    self.tt("dve", s3v, s3v, dec.broadcast_to([128, 4, 128]), ALU.mult, [S3.buf] + EL.bufs, [S3.buf])
                    self.tt("dve", S3.t[:, :], S3.t[:, :], sp_[:, :], ALU.add, [S3.buf, spb], [S3.buf])
                    self.cp("pool", Sb.t[:, :], S3.t[:, :], [S3.buf], [Sb.buf])

                t_stage(0, 0)
                t_stage(0, 1)
                for b in range(nbk):
                    if b + 1 < nbk:
                        t_stage(b + 1, 0)
                        t_stage(b + 1, 1)
                    c_stage(b, 0)
                    c_stage(b, 1)
                for qd in range(2):
                    op_, opb = ops_[qd]
                    w = 4 * L
                    osq = self.alloc(1)
                    self.act(osq.b()[:, 0:w], op_[:, 0:w], AF.Square, [opb], osq.bufs)
                    ss, ssb = self.bank(qd)
                    self.mm(ss[:, 0:w], self.onesbf.t[:, :], osq.b()[:, 0:w], osq.bufs + [self.onesbf.buf], [ssb])
                    r = self.rstd_from(ss[:, 0:w], ssb, w, 1.0 / 128)
                    self.stt(r.f()[:, 0:w], op_[:, 0:w], self.par("ang"), r.f()[:, 0:w],
                             ALU.mult, ALU.mult, [opb] + r.bufs + [pbuf], r.bufs)
                    hq0 = half * 8 + qd * 4
                    hsv = hs.b()[:, hq0 * 512:(hq0 + 4) * 512].rearrange("p (h t) -> p h t", h=4)[:, :, c0:c0 + L]
                    sgv = SG.b()[:, qd * 2048:(qd + 1) * 2048].rearrange("p (h t) -> p h t", h=4)[:, :, c0:c0 + L]
                    self.tt("dve", hsv, r.f()[:, 0:w].rearrange("p (h t) -> p h t", h=4), sgv, ALU.mult, r.bufs + SG.bufs, hs.bufs)
                    self.free(osq, r)
                if T.kind == "sample":
                    self.store_S(self.NPS + sq, half)
                self.free(vks[0][0], vks[0][1], vks[1][0], vks[1][1], att)
            self.free(QT, KT, KP, VV, EL, SG)
        self.free(u)
        self.outproj(0, hs, "a_w_out", n)
        self.free(hs)

    def load_S(self, s, half):
        for qd in range(2):
            gq = half * 2 + qd
            S3 = self.S32[gq]
            self.dma("pool", S3.t[:, :].rearrange("p (h v) -> p h v", h=4), self.I["st_hgrn"][s, :, gq * 4:(gq + 1) * 4, :],
                     [], [S3.buf], S3.buf)
            self.cp("act", self.Sbf[gq].t[:, :], S3.t[:, :], [S3.buf], [self.Sbf[gq].buf])

    def store_S(self, oidx, half):
        for qd in range(2):
            gq = half * 2 + qd
            S3 = self.S32[gq]
            self.dma("pool", self.O["o_hgrn"][oidx, :, gq * 4:(gq + 1) * 4, :], S3.t[:, :].rearrange("p (h v) -> p h v", h=4),
                     [S3.buf], [], S3.buf)


    def proj8(self, w, wb, c0, u, n, M=128):
        ps, pb = self.bank(self.nb_next())
        for k in range(8):
            self.mm(ps[0:M, 0:n], w[:, k, c0:c0 + M], u.b()[:, k * 512:k * 512 + n], u.bufs + [wb], [pb],
                    start=(k == 0), stop=(k == 7))
        return ps, pb

    def layer1(self, T):
        n = T.n
        Wd = "b_w_in"
        pbuf = self.params.buf
        u = self.prenorm(1, n)
        hs = self.alloc(8)
        nseg = len(T.seqs)
        L = T.seqs[0][1]
        sample = T.kind == "sample"
        HW = 3 + L
        def stage_a(g):
            wx_, wxb = self.wl(Wd, 8, g * 256, 256)
            xbufs, xvs = [], []
            for m in range(2):
                xbuf = self.alloc(2)
                xv = xbuf.f()[:, 0:nseg * HW].rearrange("p (s t) -> p s t", s=nseg)
                ps, pb = self.proj8(wx_, wxb, m * 128, u, n)
                self.act(xv[:, :, 3:3 + L], ps[:, 0:n].rearrange("p (s t) -> p s t", s=nseg), AF.Copy, [pb], xbuf.bufs)
                xbufs.append(xbuf)
                xvs.append(xv)
            xcs, xcb = [], []
            for m in range(2):
                j = g * 2 + m
                xbuf, xv = xbufs[m], xvs[m]
                if sample:
                    hal = self.hal1S.t[:, :].rearrange("p (s j k) -> p s j k", s=nseg, j=16)[:, :, j, :]
                    halb = self.hal1S.buf
                else:
                    hal = self.hal1.t[:, j * 3:(j + 1) * 3].unsqueeze(1)
                    halb = self.hal1.buf
                self.cp("pool", xv[:, :, 0:3], hal, [halb], xbuf.bufs)
                xc = self.alloc(1)
                xcv = xc.f()[:, 0:n].rearrange("p (s t) -> p s t", s=nseg)
                self.ts("dve", xcv, xv[:, :, 3:3 + L], self.par("bcw", j * 4 + 3), self.par("bcb", j), ALU.mult, ALU.add,
                        xbuf.bufs + [pbuf], xc.bufs)
                for k in range(3):
                    self.stt(xcv, xv[:, :, k:k + L], self.par("bcw", j * 4 + k), xcv, ALU.mult, ALU.add,
                             xbuf.bufs + xc.bufs + [pbuf], xc.bufs)
                self.cp("pool", hal, xv[:, :, L:L + 3], xbuf.bufs, [halb])
                xb16 = self.alloc(1)
                self.cp("pool", xb16.b()[:, 0:n], xc.f()[:, 0:n], xc.bufs, xb16.bufs)
                xcs.append(xc)
                xcb.append(xb16)
            self.free(*xbufs)
            return xcs, xcb

        def stage_b(g, st):
            xcs, xcb = st
            J = [2 * g, 2 * g + 1]
            wa, wab = self.wl("b_wa", 2, 0, 256, k0=g * 2)
            wxg, wxgb = self.wl("b_wx", 2, 0, 256, k0=g * 2)
            rs, igs, as_ = [], [], []
            for m in range(2):
                j = J[m]
                r, ig = self.alloc(1), self.alloc(1)
                for (wm, wmb, dst, bias) in ((wa, wab, r, "bba"), (wxg, wxgb, ig, "bbx")):
                    ps, pb = self.bank(self.nb_next())
                    for jj in range(2):
                        self.mm(ps[:, 0:n], wm[:, jj, m * 128:(m + 1) * 128], xcb[jj].b()[:, 0:n],
                                xcb[jj].bufs + [wmb], [pb], start=(jj == 0), stop=(jj == 1))
                    self.act(dst.f()[:, 0:n], ps[:, 0:n], AF.Sigmoid, [pb, pbuf], dst.bufs, bias=self.par(bias, j))
                rs.append(r)
                igs.append(ig)
            for m in range(2):
                j, r = J[m], rs[m]
                a = self.alloc(1)
                self.act(a.f()[:, 0:n], r.f()[:, 0:n], AF.Exp, r.bufs + [self.cneg.buf], a.bufs, scale=self.cneg.t[:, j:j + 1])
                self.act(r.f()[:, 0:n], r.f()[:, 0:n], AF.Exp, r.bufs + [self.cneg.buf], r.bufs, scale=self.cneg.t[:, 16 + j:17 + j])
                self.act(r.f()[:, 0:n], r.f()[:, 0:n], AF.Ln, r.bufs + [self.oneb.buf], r.bufs, scale=-1.0, bias=self.oneb.t[:, 0:1])
                self.act(r.f()[:, 0:n], r.f()[:, 0:n], AF.Exp, r.bufs, r.bufs, scale=0.5)
                if T.kind == "meta":
                    self.memset("dve", r.f()[:, 0:1], 1.0, r.bufs)
                as_.append(a)
            for m in range(2):
                j, r, ig, a = J[m], rs[m], igs[m], as_[m]
                bt = self.alloc(1)
                self.tt("dve", bt.f()[:, 0:n], r.f()[:, 0:n], ig.f()[:, 0:n], ALU.mult, r.bufs + ig.bufs, bt.bufs)
                self.tt("dve", bt.f()[:, 0:n], bt.f()[:, 0:n], xcs[m].f()[:, 0:n], ALU.mult, bt.bufs + xcs[m].bufs, bt.bufs)
                hh = ig
                for si, (c0, Ls, sq) in enumerate(T.seqs):
                    if sample:
                        st_ = self.hstS.t[:, sq * 16 + j:sq * 16 + j + 1]
                        stb = self.hstS.buf
                    else:
                        st_ = self.hst.t[:, j:j + 1]
                        stb = self.hst.buf
                    self.scan(hh.f()[:, c0:c0 + Ls], a.f()[:, c0:c0 + Ls], bt.f()[:, c0:c0 + Ls], st_,
                              a.bufs + bt.bufs + [stb], hh.bufs)
                    self.cp("pool", st_, hh.f()[:, c0 + Ls - 1:c0 + Ls], hh.bufs, [stb])
                self.free(bt, r, xcs[m], xcb[m])
            wg_, wgb = self.wl(Wd, 8, 2048 + g * 256, 256)
            for m in range(2):
                j, hh, a = J[m], igs[m], as_[m]
                ps, pb = self.proj8(wg_, wgb, m * 128, u, n)
                self.act(a.f()[:, 0:n], ps[:, 0:n], AF.Silu, [pb], a.bufs)
                self.tt("dve", hs.b()[:, j * 512:j * 512 + n], hh.f()[:, 0:n], a.f()[:, 0:n], ALU.mult, hh.bufs + a.bufs, hs.bufs)
                self.free(hh, a)

        st_prev = stage_a(0)
        for g in range(8):
            st_next = stage_a(g + 1) if g + 1 < 8 else None
            stage_b(g, st_prev)
            st_prev = st_next
        self.free(u)
        self.outproj(1, hs, "b_w_out", n)
        self.free(hs)

    def layer3(self, T):
        n = T.n
        Wd = "d_w_in"
        pbuf = self.params.buf
        u = self.prenorm(3, n)
        nseg = len(T.seqs)
        L = T.seqs[0][1]
        sample = T.kind == "sample"
        HW = 30 + L
        c32 = self.alloc(16)
        sum_ps, sumb = self.bank(2)
        sq_ps, sqb = self.bank(3)
        wts = {}

        def stage_a(j):
            g, m = j // 2, j % 2
            if m == 0:
                wts[g] = (self.wl(Wd, 8, g * 256, 256), self.wl(Wd, 8, 2048 + g * 256, 256))
            (wa, wab), (wb_, wbb) = wts[g]
            sg = self.alloc(1)
            ps, pb = self.proj8(wb_, wbb, m * 128, u, n)
            self.act(sg.f()[:, 0:n], ps[:, 0:n], AF.Sigmoid, [pb], sg.bufs)
            vbuf = self.alloc(2)
            vv = vbuf.f()[:, 0:nseg * HW].rearrange("p (s t) -> p s t", s=nseg)
            ps, pb = self.proj8(wa, wab, m * 128, u, n)
            self.tt("dve", vv[:, :, 30:30 + L], ps[:, 0:n].rearrange("p (s t) -> p s t", s=nseg),
                    sg.f()[:, 0:n].rearrange("p (s t) -> p s t", s=nseg), ALU.mult, [pb] + sg.bufs, vbuf.bufs)
            if sample:
                hal = self.hal3S.t[:, :].rearrange("p (s j k) -> p s j k", s=nseg, j=16)[:, :, j, :]
                halb = self.hal3S.buf
            else:
                hal = self.hal3.t[:, j * 30:(j + 1) * 30].unsqueeze(1)
                halb = self.hal3.buf
            self.cp("pool", vv[:, :, 0:30], hal, [halb], vbuf.bufs)
            vbf = self.alloc(1)
            self.cp("act", vbf.b()[:, 0:nseg * HW], vbuf.f()[:, 0:nseg * HW], vbuf.bufs, vbf.bufs)
            self.cp("pool", hal, vv[:, :, L:L + 30], vbuf.bufs, [halb])
            D = self.alloc(4)
            dv = D.b()[:, 0:31 * 128].rearrange("p (k c) -> p k c", k=31)
            self.tt("dve", dv, self.identbf.t[:, :].unsqueeze(1).broadcast_to([128, 31, 128]),
                    self.par("dcw", j * 31, 31).unsqueeze(2).broadcast_to([128, 31, 128]), ALU.mult,
                    [self.identbf.buf, pbuf], D.bufs)
            self.free(sg, vbuf)
            return (vbf, D, dv)

        def stage_b(j, st):
            vbf, D, dv = st
            cps, cpb = self.bank(4 + (j % 2))
            vb3 = vbf.b()[:, 0:nseg * HW].rearrange("p (s t) -> p s t", s=nseg)
            for si in range(nseg):
                for k in range(31):
                    self.mm(cps[:, si * L:(si + 1) * L], dv[:, k, :], vb3[:, si, k:k + L], D.bufs + vbf.bufs, [cpb],
                            start=(k == 0), stop=(k == 30))
            cj = c32.f()[:, j * 512:j * 512 + n]
            self.ts("dve", cj, cps[:, 0:n], self.par("dcb", j), None, ALU.add, None, [cpb, pbuf], [c32.bufs[j]])
            cb = self.alloc(1)
            self.cp("act", cb.b()[:, 0:n], cj, [c32.bufs[j]], cb.bufs)
            self.act(cb.b()[:, 512:512 + n], cj, AF.Square, [c32.bufs[j]], cb.bufs)
            self.mm(sum_ps[:, 0:n], self.onesbf.t[:, :], cb.b()[:, 0:n], cb.bufs + [self.onesbf.buf], [sumb],
                    start=(j == 0), stop=(j == 15))
            self.mm(sq_ps[:, 0:n], self.onesbf.t[:, :], cb.b()[:, 512:512 + n], cb.bufs + [self.onesbf.buf], [sqb],
                    start=(j == 0), stop=(j == 15))
            self.free(vbf, D, cb)

        st_prev = stage_a(0)
        for j in range(16):
            st_next = stage_a(j + 1) if j + 1 < 16 else None
            stage_b(j, st_prev)
            st_prev = st_next
        mean, rstd, nmr = (self.alloc(1) for _ in range(3))
        self.act(mean.f()[:, 0:n], sum_ps[:, 0:n], AF.Copy, [sumb], mean.bufs, scale=1.0 / 2048)
        self.tt("dve", nmr.f()[:, 0:n], mean.f()[:, 0:n], mean.f()[:, 0:n], ALU.mult, mean.bufs, nmr.bufs)
        self.stt(rstd.f()[:, 0:n], sq_ps[:, 0:n], 1.0 / 2048, nmr.f()[:, 0:n], ALU.mult, ALU.subtract, [sqb] + nmr.bufs, rstd.bufs)
        self.act(rstd.f()[:, 0:n], rstd.f()[:, 0:n], AF.Ln, rstd.bufs + [self.epsb.buf], rstd.bufs, bias=self.epsb.t[:, 0:1])
        self.act(rstd.f()[:, 0:n], rstd.f()[:, 0:n], AF.Exp, rstd.bufs, rstd.bufs, scale=-0.5)
        self.stt(nmr.f()[:, 0:n], mean.f()[:, 0:n], -1.0, rstd.f()[:, 0:n], ALU.mult, ALU.mult, mean.bufs + rstd.bufs, nmr.bufs)
        hs = self.alloc(8)
        for g in range(8):
            wg, wgb = self.wl(Wd, 8, 4096 + g * 256, 256)
            for m in range(2):
                j = g * 2 + m
                cj = c32.f()[:, j * 512:j * 512 + n]
                t = self.alloc(1)
                sg = self.alloc(1)
                self.tt("dve", t.f()[:, 0:n], cj, rstd.f()[:, 0:n], ALU.mult, [c32.bufs[j]] + rstd.bufs, t.bufs)
                self.tt("dve", t.f()[:, 0:n], t.f()[:, 0:n], nmr.f()[:, 0:n], ALU.add, t.bufs + nmr.bufs, t.bufs)
                self.act(t.f()[:, 0:n], t.f()[:, 0:n], AF.Silu, t.bufs + [pbuf], t.bufs, scale=self.par("dlg", j), bias=self.par("dlb", j))
                ps, pb = self.proj8(wg, wgb, m * 128, u, n)
                self.act(sg.f()[:, 0:n], ps[:, 0:n], AF.Silu, [pb], sg.bufs)
                self.tt("dve", hs.b()[:, j * 512:j * 512 + n], t.f()[:, 0:n], sg.f()[:, 0:n], ALU.mult, t.bufs + sg.bufs, hs.bufs)
                self.free(t, sg)
        self.free(mean, rstd, nmr, c32, u)
        self.outproj(3, hs, "d_w_out", n)
        self.free(hs)


    def layer2(self, T):
        n = T.n
        Wd = "c_w_in"
        pbuf = self.params.buf
        NKM = self.NKMAX
        KC = self.KC
        sample = T.kind == "sample"
        u = self.prenorm(2, n)
        RP = self.ROPE
        if T.kind == "meta":
            src = self.I["ropeP"][:, :, 0:16]
        elif T.kind == "frame":
            src = self.I["ropeP"][:, :, 16 + 512 * T.tidx:16 + 512 * (T.tidx + 1)]
        else:
            src = self.I["ropeS"][:, :, :]
        rpv = RP.t[:, :].rearrange("p (a t) -> p a t", a=2)
        self.dma("pool", rpv[:, :, 0:n], src, [], [RP.buf], RP.buf)
        rq = self.alloc(2)
        rqv = rq.f()[0:64, :].rearrange("p (a t) -> p a t", a=2)
        self.ts("dve", rqv[:, :, 0:n], rpv[:, :, 0:n], C_SCALE, None, ALU.mult, None, [RP.buf], rq.bufs)

        def rms_chunks(w, wb, nch, gname, inv_d, dst_f32, dst_bufs, dst_bf):
            raw = self.alloc(nch)
            sq = self.alloc((nch + 1) // 2)
            for c in range(nch):
                ps, pb = self.proj8(w, wb, c * 128, u, n)
                ra = raw.f()[:, c * 512:c * 512 + n]
                self.act(ra, ps[:, 0:n], AF.Copy, [pb], [raw.bufs[c]])
                self.tt("pool", sq.b()[:, c * 512:c * 512 + n], ra, ra, ALU.mult, [raw.bufs[c]], sq.bufs)
            ps, pb = self.bank(self.nb_next())
            for c in range(nch):
                self.mm(ps[:, 0:n], self.onesbf.t[:, :], sq.b()[:, c * 512:c * 512 + n], sq.bufs + [self.onesbf.buf], [pb],
                        start=(c == 0), stop=(c == nch - 1))
            r = self.rstd_from(ps[:, 0:n], pb, n, inv_d)
            for c in range(nch):
                ra = raw.f()[:, c * 512:c * 512 + n]
                if dst_f32 is not None:
                    self.stt(dst_f32(c), ra, self.par(gname, c), r.f()[:, 0:n], ALU.mult, ALU.mult,
                             [raw.bufs[c], pbuf] + r.bufs, dst_bufs)
                    self.cp("pool", dst_bf(c), dst_f32(c), dst_bufs, dst_bf.bufs)
                else:
                    self.stt(dst_bf(c), ra, self.par(gname, c), r.f()[:, 0:n], ALU.mult, ALU.mult,
                             [raw.bufs[c], pbuf] + r.bufs, dst_bf.bufs)
            self.free(raw, sq, r)

        qn = self.alloc(2)
        w, wb = self.wl(Wd, 8, 0, 512)
        dq = lambda c: qn.b()[:, c * 512:c * 512 + n]
        dq.bufs = qn.bufs
        rms_chunks(w, wb, 4, "cqn", 1.0 / 512, None, None, dq)
        CKV = self.CKV
        ckb = self.alloc(1)
        w, wb = self.wl(Wd, 8, 512, 256)
        df = lambda c: CKV.t[:, c * 512:c * 512 + n]
        db = lambda c: ckb.b()[:, c * 512:c * 512 + n]
        db.bufs = ckb.bufs
        rms_chunks(w, wb, 2, "ckvn", 1.0 / 256, df, [CKV.buf], db)
        KPE = self.KPE
        w, wb = self.wl(Wd, 8, 768, 64)
        ps, pb = self.proj8(w, wb, 0, u, n, M=64)
        kp = self.alloc(1)
        kpb = self.alloc(1)
        self.act(kp.f()[0:64, 0:n], ps[0:64, 0:n], AF.Copy, [pb], kp.bufs)
        self.cp("pool", kpb.b()[0:64, 0:n], kp.f()[0:64, 0:n], kp.bufs, kpb.bufs)
        ps2, pb2 = self.bank(self.nb_next())
        self.mm(ps2[0:64, 0:n], self.swapbf.t[:, :], kpb.b()[0:64, 0:n], kpb.bufs + [self.swapbf.buf], [pb2])
        self.tt("dve", kp.f()[0:64, 0:n], kp.f()[0:64, 0:n], rpv[:, 0, 0:n], ALU.mult, kp.bufs + [RP.buf], kp.bufs)
        self.tt("dve", KPE.t[:, 0:n], ps2[0:64, 0:n], rpv[:, 1, 0:n], ALU.mult, [pb2, RP.buf], [KPE.buf])
        self.tt("dve", KPE.t[:, 0:n], KPE.t[:, 0:n], kp.f()[0:64, 0:n], ALU.add, [KPE.buf] + kp.bufs, [KPE.buf])
        self.cp("pool", kpb.b()[0:64, 0:n], KPE.t[:, 0:n], [KPE.buf], kpb.bufs)
        self.free(kp)
        ckv3 = CKV.t[:, :].rearrange("p (c t) -> p c t", c=2)
        if T.kind == "meta":
            for s in range(self.NPS):
                self.dma("pool", self.O["o_lat_p"][s, :, :, 0:16], ckv3[:, :, 0:16], [CKV.buf], [], CKV.buf)
                self.dma("pool", self.O["o_rope_p"][s, :, 0:16], KPE.t[:, 0:16], [KPE.buf], [], KPE.buf)
            k0 = 0
        elif T.kind == "frame":
            k0 = 16 + 512 * T.tidx
            self.dma("pool", self.O["o_lat_p"][T.seq, :, :, k0:k0 + 512], ckv3, [CKV.buf], [], CKV.buf)
            self.dma("pool", self.O["o_rope_p"][T.seq, :, k0:k0 + 512], KPE.t[:, :], [KPE.buf], [], KPE.buf)
        else:
            self.dma("pool", self.O["o_lat_s"][:, :, :], ckv3[:, :, 0:n], [CKV.buf], [], CKV.buf)
            self.dma("pool", self.O["o_rope_s"][:, :], KPE.t[:, 0:n], [KPE.buf], [], KPE.buf)
        lat = lambda c, a, b: KC.t[:, c * NKM + a:c * NKM + b]
        kpe = lambda a, b: self.KPT.t[:, a:b]
        if not sample:
            for c in range(2):
                self.cp("pool", lat(c, k0, k0 + n), ckb.b()[:, c * 512:c * 512 + n], ckb.bufs, [KC.buf])
            self.cp("pool", kpe(k0, k0 + n), kpb.b()[0:64, 0:n], kpb.bufs, [KC.buf])
        wuk = self.alloc(4)
        wuv = self.alloc(4)
        wukv = wuk.b()[:, 0:4096].rearrange("p (c m) -> p c m", c=2)
        wuvv = wuv.b()[:, 0:4096].rearrange("p (c m) -> p c m", c=2)
        self.dma("sp", wukv, self.W["c_w_uk"].rearrange("(c p) m -> p c m", p=128), [self.Wbuf["c_w_uk"]], wuk.bufs, wuk.bufs[0])
        self.dma("sp", wuvv, self.W["c_w_uv"].rearrange("(c p) m -> p c m", p=128), [self.Wbuf["c_w_uv"]], wuv.bufs, wuv.bufs[0])
        hs = self.alloc(8)
        o_ps, opb = self.bank(6)
        d_ps, dpb = self.bank(7)
        for (c0, L, sq) in (T.seqs if not sample else []):
            if T.kind == "meta":
                kts = [(0, 16, None)]
            elif T.kind == "frame":
                kts = [(0, 16, None)]
                for i in range(4 * T.tidx + 4):
                    kts.append((16 + 128 * i, 128, (i - 4 * T.tidx) if i >= 4 * T.tidx else None))
            else:
                NKC = self.NKC
                self.dma("pool", KC.t[:, 0:2 * NKM].rearrange("p (c k) -> p c k", c=2)[:, :, 0:NKC], self.I["c_lat"][sq],
                         [], [KC.buf], KC.buf)
                self.dma("pool", self.KPT.t[:, 0:NKC], self.I["c_rope"][sq], [], [KC.buf], KC.buf)
                for c in range(2):
                    self.cp("pool", lat(c, NKC, NKC + L), ckb.b()[:, c * 512 + c0:c * 512 + c0 + L], ckb.bufs, [KC.buf])
                self.cp("pool", kpe(NKC, NKC + L), kpb.b()[0:64, c0:c0 + L], kpb.bufs, [KC.buf])
                tot = NKC + L
                kts = [(a, min(128, tot - a), None) for a in range(0, tot, 128)]
            for hp in range(8):
                wq, wqb = self.wl("c_w_uq", 4, hp * 384, 384)
                if hp % 2 == 0:
                    wg, wgb = self.wl(Wd, 8, 832 + (hp // 2) * 512, 512)
                for hq in range(2):
                    h = hp * 2 + hq
                    qnb, qpb, qrb, sg = (self.alloc(1) for _ in range(4))
                    qp = self.alloc(2)
                    ps, pb = self.bank(self.nb_next())
                    for k in range(4):
                        self.mm(ps[:, 0:L], wq[:, k, hq * 192:hq * 192 + 128], qn.b()[:, k * 512 + c0:k * 512 + c0 + L],
                                qn.bufs + [wqb], [pb], start=(k == 0), stop=(k == 3))
                    self.act(qnb.b()[:, 0:L], ps[:, 0:L], AF.Copy, [pb], qnb.bufs, scale=C_SCALE)
                    ps, pb = self.bank(self.nb_next())
                    for k in range(4):
                        self.mm(ps[0:64, 0:L], wq[:, k, hq * 192 + 128:hq * 192 + 192], qn.b()[:, k * 512 + c0:k * 512 + c0 + L],
                                qn.bufs + [wqb], [pb], start=(k == 0), stop=(k == 3))
                    self.act(qpb.b()[0:64, 0:L], ps[0:64, 0:L], AF.Copy, [pb], qpb.bufs)
                    self.tt("dve", qp.f()[0:64, 0:L], ps[0:64, 0:L], rqv[:, 0, c0:c0 + L], ALU.mult, [pb] + rq.bufs, qp.bufs)
                    ps2, pb2 = self.bank(self.nb_next())
                    self.mm(ps2[0:64, 0:L], self.swapbf.t[:, :], qpb.b()[0:64, 0:L], qpb.bufs + [self.swapbf.buf], [pb2])
                    self.tt("dve", qp.f()[0:64, 512:512 + L], ps2[0:64, 0:L], rqv[:, 1, c0:c0 + L], ALU.mult, [pb2] + rq.bufs, qp.bufs)
                    self.tt("dve", qrb.b()[0:64, 0:L], qp.f()[0:64, 0:L], qp.f()[0:64, 512:512 + L], ALU.add, qp.bufs, qrb.bufs)
                    gl = (hp % 2) * 256 + hq * 128
                    ps, pb = self.bank(self.nb_next())
                    for k in range(8):
                        self.mm(ps[:, 0:L], wg[:, k, gl:gl + 128], u.b()[:, k * 512 + c0:k * 512 + c0 + L], u.bufs + [wgb], [pb],
                                start=(k == 0), stop=(k == 7))
                    self.act(sg.f()[:, 0:L], ps[:, 0:L], AF.Exp, [pb], sg.bufs, scale=-1.0)
                    self.act(sg.f()[:, 0:L], sg.f()[:, 0:L], AF.Ln, sg.bufs + [self.oneb.buf], sg.bufs, bias=self.oneb.t[:, 0:1])
                    self.act(sg.f()[:, 0:L], sg.f()[:, 0:L], AF.Exp, sg.bufs, sg.bufs, scale=-1.0)
                    self.tt("dve", sg.f()[:, 0:L], ps[:, 0:L], sg.f()[:, 0:L], ALU.mult, [pb] + sg.bufs, sg.bufs)
                    ntile = len(kts)
                    for sg0 in range(0, ntile, 16):
                        grp = kts[sg0:sg0 + 16]
                        KN = self.alloc(2)
                        VV = self.alloc(2)
                        gbase = grp[0][0]
                        gend = grp[-1][0] + grp[-1][1]
                        for qi, g0 in enumerate(range(gbase, gend, 512)):
                            gw = min(512, gend - g0)
                            kn_ps, knb_ = self.bank(2 + (qi % 2))
                            for c in range(2):
                                self.mm(kn_ps[:, 0:gw], wukv[:, c, h * 128:(h + 1) * 128], lat(c, g0, g0 + gw), wuk.bufs + [KC.buf], [knb_],
                                        start=(c == 0), stop=(c == 1))
                            self.cp("act" if qi % 2 == 0 else "dve", KN.b()[:, g0 - gbase:g0 - gbase + gw], kn_ps[:, 0:gw], [knb_], KN.bufs)
                        for qi in range(0, len(grp), 4):
                            sub = grp[qi:qi + 4]
                            v_ps, vpb = self.bank(2 + ((qi // 4) % 2))
                            for j, (a, kw, mi) in enumerate(sub):
                                for c in range(2):
                                    self.mm(v_ps[0:kw, j * 128:(j + 1) * 128], lat(c, a, a + kw), wuvv[:, c, h * 128:(h + 1) * 128],
                                            wuv.bufs + [KC.buf], [vpb], start=(c == 0), stop=(c == 1))
                            w_ = len(sub) * 128
                            self.cp("dve" if (qi // 4) % 2 == 0 else "act", VV.b()[:, qi * 128:qi * 128 + w_], v_ps[:, 0:w_], [vpb], VV.bufs)

                        def stage_a(li):
                            a, kw, mi = grp[li]
                            s_ps, spb = self.bank(4 + (li % 2))
                            self.mm(s_ps[0:kw, 0:L], KN.b()[:, a - gbase:a - gbase + kw], qnb.b()[:, 0:L], KN.bufs + qnb.bufs, [spb],
                                    start=True, stop=False)
                            self.mm(s_ps[0:kw, 0:L], kpe(a, a + kw), qrb.b()[0:64, 0:L], [KC.buf] + qrb.bufs, [spb],
                                    start=False, stop=(mi is None))
                            if mi is not None:
                                self.mm(s_ps[0:kw, 0:L], self.identbf.t[0:kw, 0:kw], self.negm.t[0:kw, mi * 512:mi * 512 + L],
                                        [self.identbf.buf, self.negm.buf], [spb], start=False, stop=True)
                            P = self.alloc(1)
                            self.act(P.b()[0:kw, 0:L], s_ps[0:kw, 0:L], AF.Exp, [spb], P.bufs)
                            return P

                        def stage_b(li, P):
                            a, kw, mi = grp[li]
                            gi = sg0 + li
                            first, last = gi == 0, gi == ntile - 1
                            self.mm(o_ps[:, 0:L], VV.b()[0:kw, li * 128:(li + 1) * 128], P.b()[0:kw, 0:L], VV.bufs + P.bufs, [opb],
                                    start=first, stop=last)
                            self.mm(d_ps[:, 0:L], self.onesbf.t[0:kw, :], P.b()[0:kw, 0:L], P.bufs + [self.onesbf.buf], [dpb],
                                    start=first, stop=last)
                            self.free(P)
                        Pprev = stage_a(0)
                        for li in range(len(grp)):
                            Pn = stage_a(li + 1) if li + 1 < len(grp) else None
                            stage_b(li, Pprev)
                            Pprev = Pn
                        self.free(KN, VV)
                    rd = self.alloc(1)
                    self.act(rd.f()[:, 0:L], d_ps[:, 0:L], AF.Ln, [dpb], rd.bufs)
                    self.act(rd.f()[:, 0:L], rd.f()[:, 0:L], AF.Exp, rd.bufs, rd.bufs, scale=-1.0)
                    self.tt("dve", rd.f()[:, 0:L], o_ps[:, 0:L], rd.f()[:, 0:L], ALU.mult, [opb] + rd.bufs, rd.bufs)
                    self.tt("dve", hs.b()[:, h * 512 + c0:h * 512 + c0 + L], rd.f()[:, 0:L], sg.f()[:, 0:L], ALU.mult, rd.bufs + sg.bufs, hs.bufs)
                    self.free(qnb, qp, qpb, qrb, sg, rd)
        if sample:
            self.l2_sample(T, u, qn, rq, rqv, ckb, kpb, wuk, wukv, wuv, wuvv, hs, Wd, lat, kpe)
            self.free(u, rq, qn, ckb, kpb, wuv)
        else:
            self.free(u, rq, qn, ckb, kpb, wuk, wuv)
        self.outproj(2, hs, "c_w_out", n)
        self.free(hs)


    def l2_sample(self, T, u, qn, rq, rqv, ckb, kpb, wuk, wukv, wuv, wuvv, hs, Wd, lat, kpe):
        n, NS, NKC, NKM, KC = T.n, self.NS, self.NKC, self.NKMAX, self.KC
        WT = self.alloc(4)
        for h in range(16):
            tp, tpb = self.bank(2 + (h % 2))
            tpv = tp[:, :].bitcast(BF16)
            for c in range(2):
                self.tr(tpv[:, c * 128:(c + 1) * 128], wukv[:, c, h * 128:(h + 1) * 128], wuk.bufs, [tpb])
            self.cp("act" if h % 2 == 0 else "dve", WT.b()[:, h * 256:(h + 1) * 256], tpv[:, 0:256], [tpb], WT.bufs)
        self.free(wuk)
        QA = [self.alloc(4), self.alloc(4)]
        QR = self.alloc(4)
        qav = [q.b()[:, 0:NS * 1024].rearrange("p (s h q) -> p s h q", s=NS, h=16) for q in QA]
        qrv = QR.b()[0:64, 0:NS * 1024].rearrange("p (s h q) -> p s h q", s=NS, h=16)
        for hp in range(8):
            wq, wqb = self.wl("c_w_uq", 4, hp * 384, 384)
            for hq in range(2):
                h = hp * 2 + hq
                qnb, qpb = self.alloc(1), self.alloc(1)
                qp = self.alloc(2)
                ps, pb = self.bank(self.nb_next())
                for k in range(4):
                    self.mm(ps[:, 0:n], wq[:, k, hq * 192:hq * 192 + 128], qn.b()[:, k * 512:k * 512 + n],
                            qn.bufs + [wqb], [pb], start=(k == 0), stop=(k == 3))
                self.act(qnb.b()[:, 0:n], ps[:, 0:n], AF.Copy, [pb], qnb.bufs, scale=C_SCALE)
                for c in range(2):
                    ps, pb = self.bank(self.nb_next())
                    self.mm(ps[:, 0:n], WT.b()[:, h * 256 + c * 128:h * 256 + (c + 1) * 128], qnb.b()[:, 0:n], WT.bufs + qnb.bufs, [pb])
                    self.cp("act" if c == 0 else "dve", qav[c][:, :, h, :], ps[:, 0:n].rearrange("p (s q) -> p s q", s=NS), [pb], QA[c].bufs)
                ps, pb = self.bank(self.nb_next())
                for k in range(4):
                    self.mm(ps[0:64, 0:n], wq[:, k, hq * 192 + 128:hq * 192 + 192], qn.b()[:, k * 512:k * 512 + n],
                            qn.bufs + [wqb], [pb], start=(k == 0), stop=(k == 3))
                self.act(qp.f()[0:64, 0:n], ps[0:64, 0:n], AF.Copy, [pb], qp.bufs)
                self.cp("pool", qpb.b()[0:64, 0:n], qp.f()[0:64, 0:n], qp.bufs, qpb.bufs)
                ps2, pb2 = self.bank(self.nb_next())
                self.mm(ps2[0:64, 0:n], self.swapbf.t[:, :], qpb.b()[0:64, 0:n], qpb.bufs + [self.swapbf.buf], [pb2])
                self.tt("dve", qp.f()[0:64, 0:n], qp.f()[0:64, 0:n], rqv[:, 0, 0:n], ALU.mult, qp.bufs + rq.bufs, qp.bufs)
                self.tt("dve", qp.f()[0:64, 512:512 + n], ps2[0:64, 0:n], rqv[:, 1, 0:n], ALU.mult, [pb2] + rq.bufs, qp.bufs)
                self.tt("dve", qrv[:, :, h, :], qp.f()[0:64, 0:n].rearrange("p (s q) -> p s q", s=NS),
                        qp.f()[0:64, 512:512 + n].rearrange("p (s q) -> p s q", s=NS), ALU.add, qp.bufs, QR.bufs)
                self.free(qnb, qpb, qp)
        self.free(WT)
        ob = [self.bank(0), self.bank(1)]
        d_ps, dpb = self.bank(2)
        for s_ in range(NS):
            c0 = 64 * s_
            self.dma("pool", KC.t[:, 0:2 * NKM].rearrange("p (c k) -> p c k", c=2)[:, :, 0:NKC], self.I["c_lat"][s_],
                     [], [KC.buf], KC.buf)
            self.dma("pool", self.KPT.t[:, 0:NKC], self.I["c_rope"][s_], [], [KC.buf], KC.buf)
            for c in range(2):
                self.cp("pool", lat(c, NKC, NKC + 64), ckb.b()[:, c * 512 + c0:c * 512 + c0 + 64], ckb.bufs, [KC.buf])
            self.cp("pool", kpe(NKC, NKC + 64), kpb.b()[0:64, c0:c0 + 64], kpb.bufs, [KC.buf])
            tot = NKC + 64
            kts = [(a, min(128, tot - a)) for a in range(0, tot, 128)]
            for half in range(2):
                hsl = slice(half * 8, half * 8 + 8)
                qa = [qav[c][:, s_, hsl, :] for c in range(2)]
                qr = qrv[:, s_, hsl, :]

                def stage_a(ki):
                    a, kw = kts[ki]
                    s_ps, spb = self.bank(4 + (ki % 2))
                    for c in range(2):
                        self.mm(s_ps[0:kw, :], lat(c, a, a + kw), qa[c], [KC.buf] + QA[c].bufs, [spb], start=(c == 0), stop=False)
                    self.mm(s_ps[0:kw, :], kpe(a, a + kw), qr, [KC.buf] + QR.bufs, [spb], start=False, stop=True)
                    P = self.alloc(1)
                    self.act(P.b()[0:kw, 0:512], s_ps[0:kw, :], AF.Exp, [spb], P.bufs)
                    tp, tpb = self.bank(3)
                    tpv = tp[:, :].bitcast(BF16)
                    for c in range(2):
                        self.tr(tpv[0:kw, c * 128:(c + 1) * 128], lat(c, a, a + kw), [KC.buf], [tpb])
                    LT = self.alloc(1)
                    self.cp("dve", LT.b()[0:kw, 0:256], tpv[0:kw, 0:256], [tpb], LT.bufs)
                    return P, LT

                def stage_b(ki, st):
                    P, LT = st
                    a, kw = kts[ki]
                    first, last = ki == 0, ki == len(kts) - 1
                    for c in range(2):
                        self.mm(ob[c][0][:, :], LT.b()[0:kw, c * 128:(c + 1) * 128], P.b()[0:kw, 0:512], LT.bufs + P.bufs, [ob[c][1]],
                                start=first, stop=last)
                    self.mm(d_ps[:, :], self.onesbf.t[0:kw, :], P.b()[0:kw, 0:512], P.bufs + [self.onesbf.buf], [dpb], start=first, stop=last)
                    self.free(P, LT)
                st = stage_a(0)
                for ki in range(len(kts)):
                    nx = stage_a(ki + 1) if ki + 1 < len(kts) else None
                    stage_b(ki, st)
                    st = nx
                rd = self.alloc(1)
                self.act(rd.f()[:, :], d_ps[:, :], AF.Ln, [dpb], rd.bufs)
                self.act(rd.f()[:, :], rd.f()[:, :], AF.Exp, rd.bufs, rd.bufs, scale=-1.0)
                OL = self.alloc(1)
                for c in range(2):
                    self.tt("dve", OL.b()[:, c * 512:(c + 1) * 512], ob[c][0][:, :], rd.f()[:, :], ALU.mult, [ob[c][1]] + rd.bufs, OL.bufs)
                po, pob = self.bank(6)
                for hq in range(8):
                    h = half * 8 + hq
                    for c in range(2):
                        self.mm(po[:, hq * 64:(hq + 1) * 64], wuvv[:, c, h * 128:(h + 1) * 128], OL.b()[:, c * 512 + hq * 64:c * 512 + (hq + 1) * 64],
                                wuv.bufs + OL.bufs, [pob], start=(c == 0), stop=(c == 1))
                hsv = hs.b()[:, half * 8 * 512:(half * 8 + 8) * 512].rearrange("p (h t) -> p h t", h=8)[:, :, c0:c0 + 64]
                self.cp("act", hsv, po[:, :].rearrange("p (h t) -> p h t", h=8), [pob], hs.bufs)
                self.free(rd, OL)
        self.free(QA[0], QA[1], QR)
        for g4 in range(4):
            wg, wgb = self.wl(Wd, 8, 832 + g4 * 512, 512)
            for i in range(4):
                h = g4 * 4 + i
                ps, pb = self.proj8(wg, wgb, i * 128, u, n)
                sg = self.alloc(1)
                self.act(sg.f()[:, 0:n], ps[:, 0:n], AF.Silu, [pb], sg.bufs)
                hv = hs.b()[:, h * 512:h * 512 + n]
                self.tt("dve", hv, hv, sg.f()[:, 0:n], ALU.mult, hs.bufs + sg.bufs, hs.bufs)
                self.free(sg)

    def alloc_states(self):
        NS = self.NS
        self.hst = self.sb("hst", [128, 16])
        self.hal1 = self.sb("hal1", [128, 48])
        self.hal3 = self.sb("hal3", [128, 480])
        self.hstm = self.sb("hstm", [128, 16])
        self.hal1m = self.sb("hal1m", [128, 48])
        self.hal3m = self.sb("hal3m", [128, 480])
        self.hstS = self.sb("hstS", [128, NS * 16])
        self.hal1S = self.sb("hal1S", [128, NS * 48])
        self.hal3S = self.sb("hal3S", [128, NS * 480])
        self.cneg = self.sb("cneg", [128, 32])
        self.oneb = self.sb("oneb", [128, 1])
        self.KC = self.sb("kc", [128, 2 * self.NKMAX], BF16)
        self.KPT = DT(self.st.enter_context(self.nc.sbuf_tensor("sb_kpt", [64, self.NKMAX], BF16)), self.KC.buf)
        self.CKV = self.sb("ckv", [128, 1024])
        self.KPE = self.sb("kpe", [64, 512])
        self.ROPE = self.sb("rope", [64, 1024])

    def prologue_rest(self):
        self.memset("pool", self.oneb.t[:, :], 1.0, [self.oneb.buf])
        for t in (self.hst, self.hal1, self.hal3):
            self.memset("pool", t.t[:, :], 0.0, [t.buf])
        c = self.cneg
        self.act(c.t[:, 0:16], self.par("blam", 0, 16), AF.Exp, [self.params.buf], [c.buf], scale=-1.0)
        self.act(c.t[:, 0:16], c.t[:, 0:16], AF.Ln, [c.buf, self.oneb.buf], [c.buf], bias=self.oneb.t[:, 0:1])
        self.ts("dve", c.t[:, 16:32], c.t[:, 0:16], -16.0, None, ALU.mult, None, [c.buf], [c.buf])
        self.ts("dve", c.t[:, 0:16], c.t[:, 0:16], -8.0, None, ALU.mult, None, [c.buf], [c.buf])

    def save_meta(self):
        for q in range(4):
            self.dma("pool", self.S32m[q], self.S32[q].t[:, :], [self.S32[q].buf], [], self.S32[q].buf)
        for a, b in ((self.hstm, self.hst), (self.hal1m, self.hal1), (self.hal3m, self.hal3)):
            self.cp("pool", a.t[:, :], b.t[:, :], [b.buf], [a.buf])

    def restore_meta(self, s):
        for q in range(4):
            self.dma("pool", self.S32[q].t[:, :], self.S32m[q], [], [self.S32[q].buf], self.S32[q].buf)
            self.cp("act", self.Sbf[q].t[:, :], self.S32[q].t[:, :], [self.S32[q].buf], [self.Sbf[q].buf])
        for a, b in ((self.hstm, self.hst), (self.hal1m, self.hal1), (self.hal3m, self.hal3)):
            self.cp("pool", b.t[:, :], a.t[:, :], [a.buf], [b.buf])

    def store_seq(self, s):
        if 0 in self.layers:
            self.store_S(s, 0)
            self.store_S(s, 1)
        if 1 in self.layers:
            self.dma("pool", self.O["o_h"][:, s, :], self.hst.t[:, :], [self.hst.buf], [], self.hst.buf)
            self.dma("pool", self.O["o_c1"][:, s, :, :], self.hal1.t[:, :].rearrange("p (j k) -> p j k", j=16), [self.hal1.buf], [], self.hal1.buf)
        if 3 in self.layers:
            self.dma("pool", self.O["o_c3"][:, s, :, :], self.hal3.t[:, :].rearrange("p (j k) -> p j k", j=16), [self.hal3.buf], [], self.hal3.buf)

    def load_sample_states(self):
        NS = self.NS
        self.dma("pool", self.hstS.t[:, :].rearrange("p (s j) -> p s j", s=NS), self.I["st_h"][:, :, :], [], [self.hstS.buf], self.hstS.buf)
        self.dma("pool", self.hal1S.t[:, :].rearrange("p (s j k) -> p s j k", s=NS, j=16), self.I["st_c1"][:, :, :, :], [], [self.hal1S.buf], self.hal1S.buf)
        self.dma("pool", self.hal3S.t[:, :].rearrange("p (s j k) -> p s j k", s=NS, j=16), self.I["st_c3"][:, :, :, :], [], [self.hal3S.buf], self.hal3S.buf)

    def store_sample_states(self):
        NS, NPS = self.NS, self.NPS
        if 1 in self.layers:
            self.dma("pool", self.O["o_h"][:, NPS:NPS + NS, :], self.hstS.t[:, :].rearrange("p (s j) -> p s j", s=NS), [self.hstS.buf], [], self.hstS.buf)
            self.dma("pool", self.O["o_c1"][:, NPS:NPS + NS, :, :], self.hal1S.t[:, :].rearrange("p (s j k) -> p s j k", s=NS, j=16), [self.hal1S.buf], [], self.hal1S.buf)
        if 3 in self.layers:
            self.dma("pool", self.O["o_c3"][:, NPS:NPS + NS, :, :], self.hal3S.t[:, :].rearrange("p (s j k) -> p s j k", s=NS, j=16), [self.hal3S.buf], [], self.hal3S.buf)

    def declare_io(self):
        NPS, NT, NS = self.NPS, self.NT, self.NS
        n_s = NS * 64
        I, O, W = {}, {}, {}
        self.W32, self.Wbuf = {}, {}
        I["xp"] = self.dram_in("xp", [NPS, NT, 128, 8, 512])
        I["xm"] = self.dram_in("xm", [128, 8, 16])
        I["xs"] = self.dram_in("xs", [128, 8, n_s])
        I["st_hgrn"] = self.dram_in("st_hgrn", [NS, 128, 16, 128])
        I["st_h"] = self.dram_in("st_h", [128, NS, 16])
        I["st_c1"] = self.dram_in("st_c1", [128, NS, 16, 3])
        I["st_c3"] = self.dram_in("st_c3", [128, NS, 16, 30])
        I["c_lat"] = self.dram_in("c_lat", [NS, 128, 2, self.NKC])
        I["c_rope"] = self.dram_in("c_rope", [NS, 64, self.NKC])
        I["params"] = self.dram_in("params", [128, NPAR])
        I["consts"] = self.dram_in("consts", [128, NCON])
        I["ropeP"] = self.dram_in("ropeP", [64, 2, self.NKP])
        I["ropeS"] = self.dram_in("ropeS", [64, 2, n_s])
        for name, shp in [("a_w_in", [1024, 8192]), ("a_w_out", [2048, 1024]), ("b_w_in", [1024, 4096]),
                          ("b_wa", [8, 256, 256]), ("b_wx", [8, 256, 256]), ("b_w_out", [2048, 1024]),
                          ("c_w_in", [1024, 2880]), ("c_w_uq", [512, 3072]), ("c_w_uk", [256, 2048]),
                          ("c_w_uv", [256, 2048]), ("c_w_out", [2048, 1024]), ("d_w_in", [1024, 6144]),
                          ("d_w_out", [2048, 1024])]:
            if len(shp) == 3:
                shp = [shp[0] * shp[1], shp[2]]
            self.W32[name] = self.dram_in(name, shp)
            W[name] = self.nc.dram_tensor(name + "_bf", list(shp), BF16, kind="Internal").ap()
            self.Wbuf[name] = Buf("W" + name)
        O["yp"] = self.dram_out("yp", [NPS, NT, 128, 8, 512])
        O["ys"] = self.dram_out("ys", [128, 8, n_s])
        O["o_hgrn"] = self.dram_out("o_hgrn", [NPS + NS, 128, 16, 128])
        O["o_h"] = self.dram_out("o_h", [128, NPS + NS, 16])
        O["o_c1"] = self.dram_out("o_c1", [128, NPS + NS, 16, 3])
        O["o_c3"] = self.dram_out("o_c3", [128, NPS + NS, 16, 30])
        O["o_lat_p"] = self.dram_out("o_lat_p", [NPS, 128, 2, self.NKP])
        O["o_lat_s"] = self.dram_out("o_lat_s", [128, 2, n_s])
        O["o_rope_p"] = self.dram_out("o_rope_p", [NPS, 64, self.NKP])
        O["o_rope_s"] = self.dram_out("o_rope_s", [64, n_s])
        self.I, self.O, self.W = I, O, W

    def build(self):
        nc = self.nc
        NPS, NT, NS = self.NPS, self.NT, self.NS
        self.declare_io()
        with contextlib.ExitStack() as st:
            self.st = st
            self.A = st.enter_context(nc.sbuf_tensor("arena", [128, self.NU * 512], F32))
            self.abufs = [Buf("a%d" % i) for i in range(self.NU)]
            self.afree = [True] * self.NU
            self.PS = [st.enter_context(nc.psum_tensor("ps%d" % i, [128, 512], F32)) for i in range(8)]
            self.pbufs = [Buf("ps%d" % i, ps=True) for i in range(8)]
            self._mb = 0
            self.XT = self.sb("xt", [128, 4096])
            self.wslots = [self.sb("w%d" % i, [128, 4096], BF16) for i in range(self.NW)]
            self.params = self.sb("params", [128, NPAR])
            self.identbf = self.sb("identbf", [128, 128], BF16)
            self.swapbf = self.sb("swapbf", [64, 64], BF16)
            self.tri = self.sb("tri", [64, 512])
            self.scanm = self.sb("scanm", [128, 512])
            self.negm = self.sb("negm", [128, 2048], BF16)
            self.onesbf = self.sb("onesbf", [128, 128], BF16)
            self.epsb = self.sb("epsb", [128, 1])
            self.oml = self.sb("oml", [128, 16])
            self.noml = self.sb("noml", [128, 16])
            self.S32 = [self.sb("s32_%d" % i, [128, 512]) for i in range(4)]
            self.Sbf = [self.sb("sbf_%d" % i, [128, 512], BF16) for i in range(4)]
            self.S32m = self.nc.dram_tensor("s32m", [4, 128, 512], F32, kind="Internal").ap()
            self.alloc_states()
            xv = self.XT.t[:, :].rearrange("p (c t) -> p c t", c=8)
            self.dma("pool", xv[:, :, 0:16], self.I["xm"][:, :, :], [], [self.XT.buf], self.XT.buf)
            self.prologue()
            Tm = TileDesc("meta", 16, [(0, 16, None)], [(0, 16, None)])
            self.run_layers(Tm)
            self.save_meta()
            for s in range(NPS):
                self.restore_meta(s)
                for t in range(NT):
                    T = TileDesc("frame", 512, [(128 * i, 128, s) for i in range(4)], [(0, 512, s)], seq=s, tidx=t)
                    self.dma("pool", xv, self.I["xp"][s, t], [], [self.XT.buf], self.XT.buf)
                    self.run_layers(T)
                    self.dma("pool", self.O["yp"][s, t], xv, [self.XT.buf], [], self.XT.buf)
                self.store_seq(s)
            n_s = NS * 64
            Ts = TileDesc("sample", n_s, [(64 * i, 64, i) for i in range(NS)], [(64 * i, 64, i) for i in range(NS)])
            self.dma("pool", xv[:, :, 0:n_s], self.I["xs"][:, :, :], [], [self.XT.buf], self.XT.buf)
            self.load_sample_states()
            self.run_layers(Ts)
            self.dma("pool", self.O["ys"][:, :, :], xv[:, :, 0:n_s], [self.XT.buf], [], self.XT.buf)
            self.store_sample_states()
            with nc.allow_low_precision("bf16 matmul operands, fp32 accumulation"):
                self.S.emit()
        return nc

    def convert_weights(self):
        for name in ["a_w_in", "a_w_out", "b_w_in", "b_wa", "b_wx", "b_w_out", "c_w_in", "c_w_uq", "c_w_uk", "c_w_uv",
                     "c_w_out", "d_w_in", "d_w_out"]:
            src, dst, wb = self.W32[name], self.W[name], self.Wbuf[name]
            rows = src.shape[0]
            for r0 in range(0, rows, 128):
                self.dma("pool", dst[r0:r0 + 128, :], src[r0:r0 + 128, :], [], [wb], wb)

    def run_layers(self, T):
        for l in self.layers:
            getattr(self, "layer%d" % l)(T)

    def prologue(self):
        self.dma("sp", self.params.t[:, :], self.I["params"][:, :], [], [self.params.buf], self.params.buf)
        cs = self.alloc(7)
        self.dma("sp", cs.f()[:, 0:NCON], self.I["consts"][:, :], [], cs.bufs, cs.bufs[0])

        def c(name, w, rows=128):
            return cs.f()[0:rows, CON_OFF[name]:CON_OFF[name] + w]
        self.cp("dve", self.identbf.t[:, :], c("ident", 128), cs.bufs, [self.identbf.buf])
        self.cp("dve", self.swapbf.t[:, :], c("swap", 64, 64), cs.bufs, [self.swapbf.buf])
        self.cp("dve", self.tri.t[:, :], c("tri", 512, 64), cs.bufs, [self.tri.buf])
        self.cp("dve", self.scanm.t[:, :], c("scanm", 512), cs.bufs, [self.scanm.buf])
        self.ts("dve", self.negm.t[:, :], c("cmask", 2048), -1.0, 30000.0, ALU.add, ALU.mult, cs.bufs, [self.negm.buf])
        self.free(cs)
        self.memset("dve", self.onesbf.t[:, :], 1.0, [self.onesbf.buf])
        self.memset("dve", self.epsb.t[:, :], EPS, [self.epsb.buf])
        self.tt("dve", self.oml.t[:, :], self.par("lb1", 0, 16), self.par("lb0", 0, 16), ALU.subtract,
                [self.params.buf], [self.oml.buf])
        self.act(self.oml.t[:, :], self.oml.t[:, :], AF.Sigmoid, [self.oml.buf], [self.oml.buf])
        self.ts("dve", self.noml.t[:, :], self.oml.t[:, :], -1.0, None, ALU.mult, None, [self.oml.buf], [self.noml.buf])
        for q in range(4):
            self.memset("pool", self.S32[q].t[:, :], 0.0, [self.S32[q].buf])
            self.memset("pool", self.Sbf[q].t[:, :], 0.0, [self.Sbf[q].buf])
        self.prologue_rest()
        self.convert_weights()


def core_inputs(inp, cfg, core, shared):
    NPS, NT, NS, PAST = cfg["NPS"], cfg["NT"], cfg["NS"], cfg["PAST"]
    f = np.float32
    d = dict(shared)
    xp = np.asarray(inp["x_prompt"][core * NPS:(core + 1) * NPS], f)
    d["xp"] = np.ascontiguousarray(xp.reshape(NPS, NT, 512, 8, 128).transpose(0, 1, 4, 3, 2))
    xs = np.asarray(inp["x_sample"][core * NS:(core + 1) * NS], f)
    d["xs"] = np.ascontiguousarray(xs.reshape(NS * 64, 8, 128).transpose(2, 1, 0))
    sl = slice(core * NS, (core + 1) * NS)
    d["st_hgrn"] = np.ascontiguousarray(np.asarray(inp["state_hgrn"][0][sl], f).transpose(0, 2, 1, 3))
    d["st_h"] = np.ascontiguousarray(np.asarray(inp["state_rglru_h"][0][sl], f).reshape(NS, 16, 128).transpose(2, 0, 1))
    d["st_c1"] = np.ascontiguousarray(np.asarray(inp["state_rglru_conv"][0][sl], f).reshape(NS, 3, 16, 128).transpose(3, 0, 2, 1))
    d["st_c3"] = np.ascontiguousarray(np.asarray(inp["state_conformer_conv"][0][sl], f).reshape(NS, 30, 16, 128).transpose(3, 0, 2, 1))
    cl = np.asarray(inp["cache_mla_latent"][0][sl], f)
    d["c_lat"] = np.ascontiguousarray(cl.reshape(NS, -1, 2, 128).transpose(0, 3, 2, 1))
    d["c_rope"] = np.ascontiguousarray(np.asarray(inp["cache_mla_rope"][0][sl], f).transpose(0, 2, 1))
    return d


def shared_inputs(inp, cfg):
    NT, NS, PAST = cfg["NT"], cfg["NS"], cfg["PAST"]
    f = np.float32
    d = {}
    d["xm"] = np.ascontiguousarray(np.asarray(inp["meta_tokens"], f).reshape(16, 8, 128).transpose(2, 1, 0))
    d["params"] = pack_params(inp)
    d["consts"] = pack_consts()
    d["ropeP"] = rope_table(np.arange(16 + 512 * NT))
    rs = rope_table(16 + PAST + np.arange(64))
    d["ropeS"] = np.ascontiguousarray(np.tile(rs, (1, 1, NS)))
    for name in ["a_w_in", "a_w_out", "b_w_in", "b_wa", "b_wx", "b_w_out", "c_w_in", "c_w_uq", "c_w_uk", "c_w_uv",
                 "c_w_out", "d_w_in", "d_w_out"]:
        w = np.asarray(inp[name][0], f)
        d[name] = np.ascontiguousarray(w.reshape(-1, w.shape[-1]))
    return d


def assemble(results, cfg):
    NPS, NT, NS = cfg["NPS"], cfg["NT"], cfg["NS"]
    ncore = len(results)
    T = 512 * NT
    cat = lambda xs: np.concatenate(xs, axis=0)
    yp = cat([r["yp"].transpose(0, 1, 4, 3, 2).reshape(NPS, T, 1024) for r in results])
    ys = cat([r["ys"].transpose(2, 1, 0).reshape(NS, 64, 1024) for r in results])
    hg = [r["o_hgrn"].transpose(0, 2, 1, 3) for r in results]
    hgp = cat([h[:NPS] for h in hg])[None]
    hgs = cat([h[NPS:] for h in hg])[None]
    oh = [r["o_h"].transpose(1, 2, 0).reshape(NPS + NS, 2048) for r in results]
    ohp = cat([h[:NPS] for h in oh])[None]
    ohs = cat([h[NPS:] for h in oh])[None]
    c1 = [r["o_c1"].transpose(1, 3, 2, 0).reshape(NPS + NS, 3, 2048) for r in results]
    c1p = cat([h[:NPS] for h in c1])[None]
    c1s = cat([h[NPS:] for h in c1])[None]
    c3 = [r["o_c3"].transpose(1, 3, 2, 0).reshape(NPS + NS, 30, 2048) for r in results]
    c3p = cat([h[:NPS] for h in c3])[None]
    c3s = cat([h[NPS:] for h in c3])[None]
    latp = cat([r["o_lat_p"].transpose(0, 3, 2, 1).reshape(NPS, 16 + T, 256) for r in results])[None]
    lats = cat([r["o_lat_s"].transpose(2, 1, 0).reshape(NS, 64, 256) for r in results])[None]
    ropp = cat([r["o_rope_p"].transpose(0, 2, 1) for r in results])[None]
    rops = cat([r["o_rope_s"].T.reshape(NS, 64, 64) for r in results])[None]
    outs = (yp, ys, hgp, hgs, ohp, ohs, c1p, c1s, latp, lats, ropp, rops, c3p, c3s)
    return tuple(np.ascontiguousarray(o, dtype=np.float32) for o in outs)


def run(inp, cfg, ncore):
    b = Builder(cfg)
    nc = b.build()
    shared = shared_inputs(inp, cfg)
    in_maps = [core_inputs(inp, cfg, c, shared) for c in range(ncore)]
    res = run_bass_kernel_spmd(nc, in_maps, core_ids=list(range(ncore)))
    return assemble(res.results, cfg)


def kernel(**inputs):
    cfg = dict(NPS=4, NT=4, NS=4, PAST=4096)
    return run(inputs, cfg, 8)
```
